# Optimizing a Trainium2 kernel written in Bass

```python
import jax, jax.numpy as jnp
from jax import lax
import numpy as np

D_MODEL = 1024
BATCH = 4
SEQ = 4096
DEPTH = 1
DEC_BATCH = 128
DEC_SEQ = 8
PAST_LEN = 2048
PAGE_SIZE = 128

SWA_GROUPS = ((128, 1), (512, 4), (2048, 16))
N_GROUPS = 3
A_HEADS = 4
A_HEAD_DIM = 64
A_WIDTH = A_HEADS * A_HEAD_DIM
BAND = 128
G_HEADS = 4
G_DK = D_MODEL // 2 // G_HEADS
G_DV = D_MODEL // G_HEADS
G_KW = G_HEADS * G_DK
G_VW = G_HEADS * G_DV
G_LOWRANK = 16
G_TAU = 16.0
G_CHUNK = 32
D_FF = 2816
EPS = 1e-6
SPLITS = (N_GROUPS * A_WIDTH, N_GROUPS * A_WIDTH, N_GROUPS * A_WIDTH,
          G_KW, G_KW, G_VW, G_VW, G_LOWRANK, D_MODEL, D_MODEL)
D_IN = sum(SPLITS)

kernel_name = "macaron_dilated_swa_gla_gated_hybrid_step"


def rmsnorm(x, g):
    xf = x.astype(jnp.float32)
    y = xf * lax.rsqrt(jnp.mean(xf * xf, axis=-1, keepdims=True) + EPS)
    return (y * g.astype(jnp.float32)).astype(x.dtype)


def head_rmsnorm(x, g):
    xf = x.astype(jnp.float32)
    return xf * lax.rsqrt(jnp.mean(xf * xf, axis=-1, keepdims=True) + EPS) * g.astype(jnp.float32)


def swiglu(x, w_gate, w_up, w_down):
    return (jax.nn.silu(x @ w_gate) * (x @ w_up)) @ w_down


def dilated_band_attn(q, k, v, d):
    B, L, H, Dh = q.shape
    n = L // d
    nb = -(-n // BAND)
    n_pad = nb * BAND

    def to_sub(t):
        t = t.reshape(B, n, d, H, Dh).transpose(0, 2, 1, 3, 4).reshape(B * d, n, H, Dh)
        return jnp.pad(t, ((0, 0), (0, n_pad - n), (0, 0), (0, 0)))

    def with_prev(t):
        tb = t.reshape(B * d, nb, BAND, H, Dh)
        prev = jnp.pad(tb, ((0, 0), (1, 0), (0, 0), (0, 0), (0, 0)))[:, :-1]
        return jnp.concatenate([prev, tb], axis=2)

    qb = to_sub(q).reshape(B * d, nb, BAND, H, Dh)
    kb = with_prev(to_sub(k))
    vb = with_prev(to_sub(v))
    s = jnp.einsum('znqhd,znkhd->znhqk', qb, kb) * (Dh ** -0.5)
    dist = (jnp.arange(BAND)[:, None] + BAND) - jnp.arange(2 * BAND)[None, :]
    band = (dist >= 0) & (dist <= BAND)
    has_prev = (jnp.arange(nb)[:, None, None] > 0) | (jnp.arange(2 * BAND)[None, None, :] >= BAND)
    valid = band[None] & has_prev
    s = jnp.where(valid[None, :, None], s, -jnp.inf)
    m = jnp.max(s, axis=-1, keepdims=True)
    p = jnp.exp(s - m)
    den = jnp.sum(p, axis=-1, keepdims=True)
    o = jnp.einsum('znhqk,znkhd->znqhd', p / den, vb)
    lse = jnp.swapaxes((m + jnp.log(den))[..., 0], 2, 3)

    def from_sub(t):
        rest = t.shape[3:]
        t = t.reshape(B, d, n_pad, *rest)[:, :, :n]
        return jnp.moveaxis(t, 1, 2).reshape(B, L, *rest)

    return from_sub(o), from_sub(lse)


def dilated_cached_attn(q, k_all, v_all, d, n_past):
    T = q.shape[1]
    Dh = q.shape[-1]
    idx = n_past + jnp.arange(T)[:, None] - d * jnp.arange(BAND + 1)[None, :]
    valid = idx >= 0
    idx = jnp.maximum(idx, 0)
    kg = k_all[:, idx]
    vg = v_all[:, idx]
    s = jnp.einsum('bthd,btjhd->bthj', q, kg) * (Dh ** -0.5)
    s = jnp.where(valid[None, :, None, :], s, -jnp.inf)
    m = jnp.max(s, axis=-1, keepdims=True)
    p = jnp.exp(s - m)
    den = jnp.sum(p, axis=-1, keepdims=True)
    o = jnp.einsum('bthj,btjhd->bthd', p / den, vg)
    return o, (m + jnp.log(den))[..., 0]


def merge_groups(outs, lses):
    w = jax.nn.softmax(jnp.stack(lses, 0), axis=0)
    return jnp.einsum('gblh,gblhd->blhd', w, jnp.stack(outs, 0))


def swa_prompt(qa, ka, va):
    L = qa.shape[1]
    outs, lses, rows = [], [], []
    for g, (w, d) in enumerate(SWA_GROUPS):
        o, lse = dilated_band_attn(qa[:, :, g], ka[:, :, g], va[:, :, g], d)
        outs.append(o)
        lses.append(lse)
        keep = min(w, L)
        rows.append(jnp.stack([ka[:, L - keep:, g], va[:, L - keep:, g]], axis=2))
    return merge_groups(outs, lses), rows


def make_swa_sample(caches):
    def fn(qa, ka, va):
        outs, lses, rows = [], [], []
        for g, ((w, d), buf) in enumerate(zip(SWA_GROUPS, caches)):
            buf = buf.astype(jnp.float32)
            k_all = jnp.concatenate([buf[:, :, 0], ka[:, :, g]], axis=1)
            v_all = jnp.concatenate([buf[:, :, 1], va[:, :, g]], axis=1)
            o, lse = dilated_cached_attn(qa[:, :, g], k_all, v_all, d, buf.shape[1])
            outs.append(o)
            lses.append(lse)
            rows.append(jnp.stack([ka[:, :, g], va[:, :, g]], axis=2))
        return merge_groups(outs, lses), rows
    return fn


def gla_chunked(q, k, v, log_a, S0):
    B, L, H, Dk = q.shape
    Dv = v.shape[-1]
    C = min(G_CHUNK, L)
    n = -(-L // C)
    pad = n * C - L

    def chunks(t):
        t = jnp.pad(t, ((0, 0), (0, pad), (0, 0), (0, 0)))
        return jnp.moveaxis(t.reshape(B, n, C, H, t.shape[-1]), 1, 0)

    causal = jnp.tril(jnp.ones((C, C), dtype=bool))

    def step(S, inp):
        qc, kc, vc, ac = inp
        b = jnp.cumsum(ac, axis=1)
        qe = qc * jnp.exp(b)
        ke = kc * jnp.exp(-b)
        att = jnp.where(causal, jnp.einsum('bthk,bshk->bhts', qe, ke), 0.0)
        o = jnp.einsum('bhts,bshv->bthv', att, vc) + jnp.einsum('bthk,bhkv->bthv', qe, S)
        b_last = b[:, -1]
        S = S * jnp.exp(b_last)[..., None] + jnp.einsum(
            'bshk,bshv->bhkv', kc * jnp.exp(b_last[:, None] - b), vc)
        return S, o

    S, o = lax.scan(step, S0, (chunks(q), chunks(k), chunks(v), chunks(log_a)))
    o = jnp.moveaxis(o, 0, 1).reshape(B, n * C, H, Dv)[:, :L]
    return o, S


def hybrid_layer(x, mixer_a, S0, p):
    B, L, _ = x.shape
    dt = x.dtype
    x = x + 0.5 * swiglu(rmsnorm(x, p['norm_ffn1']), p['ffn1_gate'], p['ffn1_up'], p['ffn1_down'])
    h = rmsnorm(x, p['norm_mix'])
    proj = h @ p['w_in']
    split_at = [int(i) for i in np.cumsum(SPLITS)[:-1]]
    qa, ka, va, qg, kg, vg, rg, lr, ga, gb = jnp.split(proj, split_at, axis=-1)
    shp = (B, L, N_GROUPS, A_HEADS, A_HEAD_DIM)
    qa = head_rmsnorm(qa.reshape(shp), p['a_q_norm'][:, None, :])
    ka = head_rmsnorm(ka.reshape(shp), p['a_k_norm'][:, None, :])
    va = va.reshape(shp).astype(jnp.float32)
    a_out, a_rows = mixer_a(qa, ka, va)
    a_out = a_out.reshape(B, L, A_WIDTH).astype(dt) @ p['w_a_out']
    q = qg.reshape(B, L, G_HEADS, G_DK).astype(jnp.float32) * (G_DK ** -0.5)
    k = kg.reshape(B, L, G_HEADS, G_DK).astype(jnp.float32)
    v = vg.reshape(B, L, G_HEADS, G_DV).astype(jnp.float32)
    log_a = jax.nn.log_sigmoid((lr @ p['g_alpha_up'] + p['g_alpha_bias']).astype(jnp.float32)) / G_TAU
    o, S = gla_chunked(q, k, v, log_a.reshape(B, L, G_HEADS, G_DK), S0.astype(jnp.float32))
    o = head_rmsnorm(o, p['g_out_norm']).reshape(B, L, G_VW) * jax.nn.silu(rg.astype(jnp.float32))
    b_out = o.astype(dt) @ p['w_b_out']
    m = jax.nn.sigmoid(ga) * a_out + jax.nn.sigmoid(gb) * b_out
    x = x + m @ p['w_out']
    x = x + 0.5 * swiglu(rmsnorm(x, p['norm_ffn2']), p['ffn2_gate'], p['ffn2_up'], p['ffn2_down'])
    return x, [r.astype(dt) for r in a_rows], S.astype(dt)


def setup_inputs(seed: int = 0) -> dict:
    key = jax.random.key(seed)
    ks = jax.random.split(key, 32)
    f32 = jnp.float32

    def nrm(k, shape, scale):
        return jax.random.normal(k, shape, f32) * scale

    def gain(k, shape):
        return 1.0 + 0.01 * jax.random.normal(k, shape, f32)

    cache_shape = lambda w: (DEC_BATCH, min(w, PAST_LEN), 2, A_HEADS, A_HEAD_DIM)
    return {
        'x_prompt': nrm(ks[0], (BATCH, SEQ, D_MODEL), 1.0),
        'x_sample': nrm(ks[1], (DEC_BATCH, DEC_SEQ, D_MODEL), 1.0),
        'cache_swa128_kv': nrm(ks[2], cache_shape(SWA_GROUPS[0][0]), 1.0),
        'cache_swa512_kv': nrm(ks[3], cache_shape(SWA_GROUPS[1][0]), 1.0),
        'cache_swa2048_kv': nrm(ks[4], cache_shape(SWA_GROUPS[2][0]), 1.0),
        'state_gla': nrm(ks[5], (DEC_BATCH, G_HEADS, G_DK, G_DV), 0.3),
        'norm_ffn1': gain(ks[6], (D_MODEL,)),
        'ffn1_gate': nrm(ks[7], (D_MODEL, D_FF), D_MODEL ** -0.5),
        'ffn1_up': nrm(ks[8], (D_MODEL, D_FF), D_MODEL ** -0.5),
        'ffn1_down': nrm(ks[9], (D_FF, D_MODEL), D_FF ** -0.5),
        'norm_mix': gain(ks[10], (D_MODEL,)),
        'w_in': nrm(ks[11], (D_MODEL, D_IN), D_MODEL ** -0.5),
        'a_q_norm': gain(ks[12], (N_GROUPS, A_HEAD_DIM)),
        'a_k_norm': gain(ks[13], (N_GROUPS, A_HEAD_DIM)),
        'g_alpha_up': nrm(ks[14], (G_LOWRANK, G_KW), G_LOWRANK ** -0.5),
        'g_alpha_bias': nrm(ks[15], (G_KW,), 0.1),
        'g_out_norm': gain(ks[16], (G_DV,)),
        'w_a_out': nrm(ks[17], (A_WIDTH, D_MODEL), A_WIDTH ** -0.5),
        'w_b_out': nrm(ks[18], (G_VW, D_MODEL), G_VW ** -0.5),
        'w_out': nrm(ks[19], (D_MODEL, D_MODEL), D_MODEL ** -0.5),
        'norm_ffn2': gain(ks[20], (D_MODEL,)),
        'ffn2_gate': nrm(ks[21], (D_MODEL, D_FF), D_MODEL ** -0.5),
        'ffn2_up': nrm(ks[22], (D_MODEL, D_FF), D_MODEL ** -0.5),
        'ffn2_down': nrm(ks[23], (D_FF, D_MODEL), D_FF ** -0.5),
    }


def reference(x_prompt, x_sample, cache_swa128_kv, cache_swa512_kv, cache_swa2048_kv, state_gla,
              norm_ffn1, ffn1_gate, ffn1_up, ffn1_down, norm_mix, w_in, a_q_norm, a_k_norm,
              g_alpha_up, g_alpha_bias, g_out_norm, w_a_out, w_b_out, w_out,
              norm_ffn2, ffn2_gate, ffn2_up, ffn2_down):
    p = {'norm_ffn1': norm_ffn1, 'ffn1_gate': ffn1_gate, 'ffn1_up': ffn1_up, 'ffn1_down': ffn1_down,
         'norm_mix': norm_mix, 'w_in': w_in, 'a_q_norm': a_q_norm, 'a_k_norm': a_k_norm,
         'g_alpha_up': g_alpha_up, 'g_alpha_bias': g_alpha_bias, 'g_out_norm': g_out_norm,
         'w_a_out': w_a_out, 'w_b_out': w_b_out, 'w_out': w_out, 'norm_ffn2': norm_ffn2,
         'ffn2_gate': ffn2_gate, 'ffn2_up': ffn2_up, 'ffn2_down': ffn2_down}
    yp, yp_rows, yp_gla = x_prompt, None, None
    ys, ys_rows, ys_gla = x_sample, None, None
    for _ in range(DEPTH):
        S0_prompt = jnp.zeros((x_prompt.shape[0], G_HEADS, G_DK, G_DV), jnp.float32)
        yp, yp_rows, yp_gla = hybrid_layer(yp, swa_prompt, S0_prompt, p)
        ys, ys_rows, ys_gla = hybrid_layer(
            ys, make_swa_sample((cache_swa128_kv, cache_swa512_kv, cache_swa2048_kv)), state_gla, p)
    return (yp, ys, yp_rows[0], yp_rows[1], yp_rows[2], yp_gla,
            ys_rows[0], ys_rows[1], ys_rows[2], ys_gla)
```

```python
import math
from contextlib import ExitStack
import numpy as np
import concourse.bass as bass
import concourse.mybir as mybir
from concourse.bass_utils import run_bass_kernel_spmd

F32 = mybir.dt.float32
BF16 = mybir.dt.bfloat16
AF = mybir.ActivationFunctionType
ALU = mybir.AluOpType
AX = mybir.AxisListType

N_DMA_SEMS = 12
D = 1024
DFF = 2816
EPS = 1e-6
C_QA, C_KA, C_VA, C_QG, C_KG, C_VG, C_RG, C_LR, C_GA, C_GB = 0, 768, 1536, 2304, 2816, 3328, 4352, 5376, 5392, 6416
DIL = (1, 4, 16)


class Buf:
    __slots__ = ("name", "w", "rs", "excl")

    def __init__(self, name="", excl=False):
        self.name = name
        self.w = None
        self.rs = []
        self.excl = excl


class Op:
    __slots__ = ("eng", "fn", "deps", "sig", "tick", "dma", "semi", "semv", "gi", "cc")


class Sched:
    ENGS = ("pe", "act", "dve", "pool", "sp")

    def __init__(self):
        self.ops = []
        self.last = {}
        self.dmas_since = []
        self.cur_bar = None
        self.bar_seen = set()
        self.ncc = 0

    def add(self, eng, fn, reads=(), writes=(), dma=False, extra=(), cc=False):
        op = Op()
        op.eng = eng
        op.fn = fn
        op.dma = dma
        op.cc = -1
        if cc:
            op.cc = self.ncc
            self.ncc += 1
        op.sig = False
        op.tick = 0
        op.semi = -1
        op.semv = 0
        op.gi = len(self.ops)
        deps = {}
        for b in reads:
            if b.w is not None:
                deps[b.w.gi] = b.w
            if b.excl:
                for r in b.rs:
                    if r.eng != eng:
                        deps[r.gi] = r
        for b in writes:
            if b.w is not None:
                deps[b.w.gi] = b.w
            for r in b.rs:
                deps[r.gi] = r
        for d in extra:
            deps[d.gi] = d
        if self.cur_bar is not None and eng not in self.bar_seen:
            deps[self.cur_bar.gi] = self.cur_bar
            self.bar_seen.add(eng)
        for b in reads:
            b.rs.append(op)
        for b in writes:
            b.w = op
            b.rs = []
        op.deps = [d for d in deps.values() if not (eng == "pe" and d.eng == "pe" and not d.dma)]
        self.ops.append(op)
        if dma or cc:
            self.dmas_since.append(op)
        else:
            self.last[eng] = op
        return op

    def coll(self, fn, reads=(), writes=()):
        return self.add("pool", fn, reads, writes, cc=True)

    def pe(self, fn, reads=(), writes=()):
        return self.add("pe", fn, reads, writes)

    def act(self, fn, reads=(), writes=()):
        return self.add("act", fn, reads, writes)

    def dve(self, fn, reads=(), writes=()):
        return self.add("dve", fn, reads, writes)

    def pool(self, fn, reads=(), writes=()):
        return self.add("pool", fn, reads, writes)

    def dma(self, q, fn, reads=(), writes=()):
        return self.add(q, fn, reads, writes, dma=True)

    def barrier(self):
        deps = [o for o in self.last.values()] + list(self.dmas_since)
        self.cur_bar = None
        op = self.add("sp", lambda e: e.nop(), extra=deps)
        self.cur_bar = op
        self.bar_seen = {"sp"}
        self.dmas_since = []
        self.last = {"sp": op}
        return op

    def emit(self, nc):
        ops = self.ops
        qcount = {e: 0 for e in self.ENGS}
        qhist = {e: [] for e in self.ENGS}
        for op in ops:
            if op.dma:
                i = qcount[op.eng]
                qcount[op.eng] += 1
                op.semi = i % N_DMA_SEMS
                op.semv = 16 * (i // N_DMA_SEMS + 1)
                if i >= N_DMA_SEMS:
                    op.deps.append(qhist[op.eng][i - N_DMA_SEMS])
                qhist[op.eng].append(op)
                op.sig = True
        for op in ops:
            for d in op.deps:
                d.sig = True
        ticks = {e: 0 for e in self.ENGS}
        for op in ops:
            if op.cc >= 0:
                op.sig = True
            elif op.sig and not op.dma:
                ticks[op.eng] += 1
                op.tick = ticks[op.eng]
        per_eng = {e: [o for o in ops if o.eng == e] for e in self.ENGS}
        with ExitStack() as st:
            csem = {e: st.enter_context(nc.semaphore("c_" + e)) for e in self.ENGS}
            dsem = {
                e: [st.enter_context(nc.semaphore("d_%s_%d" % (e, j))) for j in range(N_DMA_SEMS)]
                for e in self.ENGS
                if qcount[e] > 0
            }
            ccsem = [st.enter_context(nc.semaphore("cc_%d" % j)) for j in range(self.ncc)]
            block = st.enter_context(nc.Block())

            def run(engname, eng):
                known_c = {e: 0 for e in self.ENGS}
                known_d = {}
                for op in per_eng[engname]:
                    need_c = {}
                    need_d = {}
                    for d in op.deps:
                        if d.cc >= 0:
                            need_d[("cc", d.cc)] = 1
                        elif d.dma:
                            key = (d.eng, d.semi)
                            if d.semv > need_d.get(key, 0):
                                need_d[key] = d.semv
                        else:
                            if d.tick > need_c.get(d.eng, 0):
                                need_c[d.eng] = d.tick
                    for e2, tk in need_c.items():
                        if tk > known_c[e2]:
                            eng.wait_ge(csem[e2], tk)
                            known_c[e2] = tk
                    for key, sv in need_d.items():
                        if sv > known_d.get(key, 0):
                            if key[0] == "cc":
                                eng.wait_ge(ccsem[key[1]], 1)
                            else:
                                eng.wait_ge(dsem[key[0]][key[1]], sv)
                            known_d[key] = sv
                    ins = op.fn(eng)
                    if op.sig:
                        if op.cc >= 0:
                            ins.then_inc(ccsem[op.cc])
                        elif op.dma:
                            ins.then_inc(dsem[op.eng][op.semi], 16)
                        else:
                            ins.then_inc(csem[op.eng], 1)

            if per_eng["pe"]:
                @block.tensor
                def _(e):
                    run("pe", e)
            if per_eng["act"]:
                @block.scalar
                def _(e):
                    run("act", e)
            if per_eng["dve"]:
                @block.vector
                def _(e):
                    run("dve", e)
            if per_eng["pool"]:
                @block.gpsimd
                def _(e):
                    run("pool", e)
            if per_eng["sp"]:
                @block.sync
                def _(e):
                    run("sp", e)


class Arena:
    def __init__(self, nc, nbytes):
        self.t = nc.alloc_sbuf_tensor("arena", [128, nbytes // 4], F32)
        self.cap = nbytes
        self.top = 0

    def alloc(self, free_shape, dt):
        n = 1
        for s in free_shape:
            n *= s
        nb = n * (2 if dt == BF16 else 4)
        nb = (nb + 31) // 32 * 32
        off = self.top
        assert off + nb <= self.cap, ("arena overflow", off, nb, self.cap)
        self.top = off + nb
        v = self.t[:, off // 4:(off + nb) // 4]
        if dt == BF16:
            v = v.bitcast(BF16)
        v = v[:, 0:n]
        if len(free_shape) == 2:
            v = v.rearrange("p (a b) -> p a b", a=free_shape[0])
        elif len(free_shape) == 3:
            v = v.rearrange("p (a b c) -> p a b c", a=free_shape[0], b=free_shape[1])
        return v


def build_nc(nsteps=None, dbg=False, mstop=99, fstop=0):
    nc = bass.Bass("TRN2", target_bir_lowering=False)
    S = Sched()

    def din(name, shape):
        return nc.dram_tensor(name, list(shape), F32, kind="ExternalInput").ap()

    def dout(name, shape):
        return nc.dram_tensor(name, list(shape), F32, kind="ExternalOutput").ap()

    def dscr(name, shape):
        return nc.dram_tensor(name, list(shape), F32).ap()

    xp = din("xp", (2048, D))
    xs = din("xs", (128, D))
    caches = [din("c128", (16, 128, 2, 256)), din("c512", (16, 512, 2, 256)), din("c2048", (16, 2048, 2, 256))]
    sgla = din("sgla", (16, 4, 128, 256))
    w_ffn = [(din("f1g", (D, DFF)), din("f1u", (D, DFF)), din("f1d", (DFF, D))),
             (din("f2g", (D, DFF)), din("f2u", (D, DFF)), din("f2d", (DFF, D)))]
    w_in = din("w_in", (D, 7440))
    w_a_out = din("w_a_out", (256, D))
    w_b_out = din("w_b_out", (D, D))
    w_out = din("w_out", (D, D))
    g_up = din("g_up", (16, 512))
    c_ident = din("c_ident", (128, 128))
    c_tri2 = din("c_tri2", (128, 256))
    c_causal = din("c_causal", (128, 128))
    c_bd8 = din("c_bd8", (128, 128))
    c_mnew = din("c_mnew", (128, 3, 128))
    c_mc = din("c_mc", (128, 13, 32))
    c_blk = din("c_blk", (128, 16))
    c_onespad = din("c_onespad", (128, 2, 128))
    c_gnorm = din("c_gnorm", (128, 3, D))
    c_gqk = din("c_gqk", (128, 3, 512))
    c_gout = din("c_gout", (128, 2))
    c_abias = din("c_abias", (128, 4))

    yp = dout("yp", (2048, D))
    ys = dout("ys", (128, D))
    kvp = [dout("kv128p", (128, 2, 256)), dout("kv512p", (512, 2, 256)), dout("kv2048p", (2048, 2, 256))]
    kvs = [dout("kv128s", (128, 2, 256)), dout("kv512s", (128, 2, 256)), dout("kv2048s", (128, 2, 256))]
    glap = dout("glap", (4, 128, 256))
    glas = dout("glas", (16, 4, 128, 256))
    x1d = (dout if dbg else dscr)("x1d", (2176, D))
    x2d = (dout if dbg else dscr)("x2d", (2176, D))
    cinK = [dscr("cinK%d" % g, (128, 128 * DIL[g])) for g in range(3)]
    coutK = [dscr("coutK%d" % g, (256, 128 * DIL[g])) for g in range(3)]
    cinV = [dscr("cinV%d" % g, (128, 256 * DIL[g])) for g in range(3)]
    coutV = [dscr("coutV%d" % g, (256, 256 * DIL[g])) for g in range(3)]
    sin_d = dscr("sin_d", (128, 1024))
    sout_d = dscr("sout_d", (256, 1024))
    c_flag = din("c_flag", (128, 1))
    if dbg:
        dbgA = dout("dbgA", (128, 2, 2176))
        dbgO = dout("dbgO", (128, 8, 2176))
        dbgQ = dout("dbgQ", (128, 4, 2176))
        dbgV = dout("dbgV", (128, 17, 512))

    def sbt(name, shape, dt=F32):
        return nc.alloc_sbuf_tensor(name, list(shape), dt)

    ident = sbt("ident", (128, 128), BF16)
    tri2 = sbt("tri2", (128, 256), BF16)
    causal = sbt("causal", (128, 128), BF16)
    bd8 = sbt("bd8", (128, 128), BF16)
    mnew = sbt("mnew", (128, 3, 128), BF16)
    mc = sbt("mc", (128, 13, 32), BF16)
    blk = sbt("blk", (128, 16), BF16)
    onespad = sbt("onespad", (128, 2, 128), BF16)
    gqk = sbt("gqk", (128, 3, 512))
    gout = sbt("gout", (128, 2))
    nabias = sbt("nabias", (128, 4))
    ones256 = sbt("ones256", (128, 128))
    one1 = sbt("one1", (128, 1))
    ones512 = sbt("ones512", (128, 512))
    onesfull = sbt("onesfull", (128, 128), BF16)
    epsb = sbt("epsb", (128, 1))
    lnsc = sbt("lnsc", (128, 1))
    gup = sbt("gup", (16, 512))
    Sst = sbt("Sst", (128, 4, 256))
    Sbf = sbt("Sbf", (128, 4, 256), BF16)
    xnT = sbt("xnT", (128, 8, 2176), BF16)
    flag = sbt("flag", (128, 1))
    ctxm = sbt("ctxm", (128, 128), BF16)
    SAf = sbt("SAf", (128, 4, 256))
    SAb = sbt("SAb", (128, 4, 256), BF16)
    B_const = Buf("const")
    B_S = [Buf("S%d" % h) for h in range(4)]
    B_Sb = [Buf("Sb%d" % h) for h in range(4)]
    B_xnT = [Buf("xnT%d" % t) for t in range(17)]
    B_KTc = [Buf() for _ in range(3)]
    B_Vcx = [Buf() for _ in range(3)]

    arena = Arena(nc, int(nc.sbuf_bytes_remaining) - 2048)

    PB = [nc.alloc_psum_tensor("pb%d" % i, [128, 512], F32) for i in range(6)]
    PT = [nc.alloc_psum_tensor("pt%d" % i, [128, 1024], BF16) for i in range(2)]
    B_PB = [Buf("pb%d" % i, excl=True) for i in range(6)]
    B_PT = [Buf("pt%d" % i, excl=True) for i in range(2)]
    rr = {"pb": 0, "pt": 0}

    def bank():
        i = rr["pb"]
        rr["pb"] = (i + 1) % 4
        return PB[i], B_PB[i]

    def rbank(i):
        return PB[4 + i], B_PB[4 + i]

    PT32 = [PT[i][:, :].bitcast(F32) for i in range(2)]
    ALL8 = [(PB[i][:, :], B_PB[i]) for i in range(6)] + [(PT32[i], B_PT[i]) for i in range(2)]
    rr["b8"] = 0

    def bank8():
        i = rr["b8"]
        rr["b8"] = (i + 1) % 8
        return ALL8[i]

    def tbank():
        i = rr["pt"]
        rr["pt"] = (i + 1) % 2
        return PT[i], B_PT[i]

    def MM(out, lhsT, rhs, start, stop, reads, writes):
        S.pe(lambda e: e.matmul(out, lhsT=lhsT, rhs=rhs, start=start, stop=stop, skip_group_check=True), reads, writes)

    def TR(out, in_, reads, writes):
        S.pe(lambda e: e.transpose(out, in_, ident[:]), list(reads) + [B_const], writes)

    def ACT(out, in_, func, reads, writes, bias=0.0, scale=1.0, accum_out=None):
        if accum_out is None:
            S.act(lambda e: e.activation(out=out, in_=in_, func=func, bias=bias, scale=scale), reads, writes)
        else:
            S.act(lambda e: e.activation(out=out, in_=in_, func=func, bias=bias, scale=scale, accum_out=accum_out), reads, writes)

    def TT(eng, out, in0, in1, op, reads, writes):
        S.add(eng, lambda e: e.tensor_tensor(out=out, in0=in0, in1=in1, op=op), reads, writes)

    def STT(eng, out, in0, scalar, in1, op0, op1, reads, writes):
        S.add(eng, lambda e: e.scalar_tensor_tensor(out=out, in0=in0, scalar=scalar, in1=in1, op0=op0, op1=op1), reads, writes)

    def TS(eng, out, in0, s1, op0, reads, writes):
        S.add(eng, lambda e: e.tensor_scalar(out=out, in0=in0, scalar1=s1, scalar2=None, op0=op0), reads, writes)

    def CP(eng, out, in_, reads, writes):
        if eng == "act":
            S.act(lambda e: e.copy(out=out, in_=in_), reads, writes)
        else:
            S.add(eng, lambda e: e.tensor_copy(out=out, in_=in_), reads, writes)

    def RCP(out, in_, reads, writes):
        S.dve(lambda e: e.reciprocal(out=out, in_=in_), reads, writes)

    def MSET(eng, ap, val, writes):
        S.add(eng, lambda e: e.memset(ap, val), (), writes)

    def DMA(q, out, in_, reads, writes):
        S.dma(q, lambda e: e.dma_start(out=out, in_=in_), reads, writes)

    for dst, src in ((ident, c_ident), (causal, c_causal), (bd8, c_bd8), (mnew, c_mnew),
                     (mc, c_mc), (blk, c_blk), (onespad, c_onespad)):
        DMA("pool", dst[:], src, (), [Buf()])
    B_nab, B_flag, B_tri = Buf(), Buf(), Buf()
    DMA("sp", gqk[:], c_gqk, (), [Buf()])
    DMA("sp", gout[:], c_gout, (), [Buf()])
    DMA("sp", nabias[:], c_abias, (), [B_nab])
    DMA("sp", gup[:], g_up, (), [Buf()])
    DMA("sp", flag[:], c_flag, (), [B_flag])
    DMA("pool", tri2[:], c_tri2, (), [B_tri])
    S.dve(lambda e: e.tensor_scalar(out=nabias[:], in0=nabias[:], scalar1=-1.0, scalar2=None, op0=ALU.mult), [B_nab], [B_nab])
    MSET("dve", ones256[:], 1.0 / 256.0, [B_const])
    MSET("dve", one1[:], 1.0, [B_const])
    MSET("dve", ones512[:], 1.0, [B_const])
    MSET("dve", onesfull[:], 1.0, [B_const])
    MSET("dve", epsb[:], EPS, [B_const])
    MSET("dve", lnsc[:], math.log(128.0 ** -0.5), [B_const])
    S.dve(lambda e: e.tensor_scalar(out=ctxm[:], in0=tri2[:, 128:256], scalar1=flag[:, 0:1], scalar2=None, op0=ALU.mult), [B_flag, B_tri], [Buf()])
    MSET("dve", Sst[:], 0.0, B_S)
    MSET("dve", Sbf[:], 0.0, B_Sb)
    S.barrier()

    def wload(dst, w, c0, ncol, kc_n, writes, q="pool"):
        DMA(q, dst, w.rearrange("(kc p) n -> p kc n", p=128)[:, 0:kc_n, c0:c0 + ncol], (), writes)

    def row0(ps, t):
        return 2048 if t == 16 else 128 * t

    def groups(ntl):
        g = [(0, 4), (4, 4), (8, 4), (12, 4)]
        if ntl == 17:
            g.append((16, 1))
        return g

    def mpipe(n, stages, lags):
        for step in range(n + lags[-1]):
            for k in reversed(range(len(stages))):
                t = step - lags[k]
                if 0 <= t < n:
                    stages[k](t)

    def phase_norm(ps, ntl, src, gi):
        m = arena.top
        gain = arena.alloc((D,), F32)
        B_g = Buf()
        DMA("sp", gain, c_gnorm[:, gi, :], (), [B_g])
        NB = 6
        xt = [arena.alloc((D,), F32) for _ in range(NB)]
        B_xt = [Buf() for _ in range(NB)]
        junk = arena.alloc((D,), BF16)
        B_junk = Buf()
        xnb = [arena.alloc((D,), BF16) for _ in range(NB)]
        B_xnb = [Buf() for _ in range(NB)]
        st = [arena.alloc((4,), F32) for _ in range(NB)]
        B_st = [Buf() for _ in range(NB)]
        pts = {}

        def n0(t):
            i = t % NB
            DMA("sp", xt[i], src(ps, t), (), [B_xt[i]])
            MSET("pool", st[i][:, 0:1], 0.0, [B_st[i]])
            ACT(junk, xt[i], AF.Square, [B_xt[i]], [B_junk, B_st[i]], accum_out=st[i][:, 0:1])

        def n1(t):
            i = t % NB
            ACT(st[i][:, 1:2], st[i][:, 0:1], AF.Sqrt, [B_st[i], B_const], [B_st[i]], bias=epsb[:], scale=1.0 / D)

        def n2(t):
            i = t % NB
            RCP(st[i][:, 2:3], st[i][:, 1:2], [B_st[i]], [B_st[i]])
            STT("dve", xnb[i], xt[i], st[i][:, 2:3], gain, ALU.mult, ALU.mult, [B_xt[i], B_st[i], B_g], [B_xnb[i]])

        def n3(t):
            i = t % NB
            pt, bpt = tbank()
            for kc in range(8):
                TR(pt[:, kc * 128:(kc + 1) * 128], xnb[i][:, kc * 128:(kc + 1) * 128], [B_xnb[i]], [bpt])
            pts[t] = (pt, bpt)

        def n4(t):
            pt, bpt = pts.pop(t)
            CP("act" if t % 2 == 0 else "dve", xnT[:, :, t * 128:(t + 1) * 128],
               pt[:, 0:1024].rearrange("p (k n) -> p k n", k=8), [bpt], [B_xnT[t]])

        mpipe(ntl, [n0, n1, n2, n3, n4], [0, 1, 2, 3, 4])
        arena.top = m

    def phase_ffn(ps, ntl, wi, src, dst):
        Wg, Wu, Wd = w_ffn[wi]
        m = arena.top
        NT = ntl * 128
        hT = arena.alloc((11, 2176), BF16)
        B_h = [[Buf() for _ in range(5)] for _ in range(11)]
        wg = [arena.alloc((8, 128), BF16) for _ in range(2)]
        wu = [arena.alloc((8, 128), BF16) for _ in range(2)]
        B_wg = [Buf() for _ in range(2)]
        B_wu = [Buf() for _ in range(2)]
        wd = [arena.alloc((11, 512), BF16) for _ in range(2)]
        B_wd = [Buf() for _ in range(2)]
        sg = [arena.alloc((512,), F32) for _ in range(2)]
        B_sg = [Buf() for _ in range(2)]
        xio = [arena.alloc((512,), F32) for _ in range(6)]
        B_xio = [Buf() for _ in range(6)]
        B_dst = [[Buf() for _ in range(2)] for _ in range(17)]
        cnt = 0
        for fg in range(2):
            for f in range(11):
                fc = fg * 11 + f
                wi_ = fc % 2
                wload(wg[wi_], Wg, fc * 128, 128, 8, [B_wg[wi_]])
                wload(wu[wi_], Wu, fc * 128, 128, 8, [B_wu[wi_]])
                for gidx, (t0, n) in enumerate(groups(ntl)):
                    N = n * 128
                    rd = [B_xnT[t0 + j] for j in range(n)]
                    pg, bpg = bank8()
                    for kc in range(8):
                        MM(pg[:, 0:N], wg[wi_][:, kc, :], xnT[:, kc, t0 * 128:t0 * 128 + N], kc == 0, kc == 7, rd + [B_wg[wi_]], [bpg])
                    pu, bpu = bank8()
                    for kc in range(8):
                        MM(pu[:, 0:N], wu[wi_][:, kc, :], xnT[:, kc, t0 * 128:t0 * 128 + N], kc == 0, kc == 7, rd + [B_wu[wi_]], [bpu])
                    si = cnt % 2
                    cnt += 1
                    ACT(sg[si][:, 0:N], pg[:, 0:N], AF.Silu, [bpg], [B_sg[si]])
                    TT("dve", hT[:, f, t0 * 128:t0 * 128 + N], sg[si][:, 0:N], pu[:, 0:N], ALU.mult, [B_sg[si], bpu], [B_h[f][gidx]])
            for ch in range(2):
                wdi = ch
                DMA("pool", wd[wdi], Wd.rearrange("(fc p) n -> p fc n", p=128)[:, fg * 11:(fg + 1) * 11, ch * 512:(ch + 1) * 512], (), [B_wd[wdi]])
                for t in range(ntl):
                    po, bpo = bank8()
                    gidx = t // 4
                    for f in range(11):
                        MM(po[:, :], hT[:, f, t * 128:(t + 1) * 128], wd[wdi][:, f, :], f == 0, f == 10, [B_h[f][gidx], B_wd[wdi]], [bpo])
                    xi = cnt % 6
                    cnt += 1
                    s_ap = (src if fg == 0 else dst)(ps, t)[:, ch * 512:(ch + 1) * 512]
                    DMA("sp", xio[xi], s_ap, [B_dst[t][ch]] if fg == 1 else (), [B_xio[xi]])
                    STT("dve", xio[xi], po[:, :], 0.5, xio[xi], ALU.mult, ALU.add, [bpo, B_xio[xi]], [B_xio[xi]])
                    DMA("act", dst(ps, t)[:, ch * 512:(ch + 1) * 512], xio[xi], [B_xio[xi]], [B_dst[t][ch]])
        arena.top = m

    def phase_mixer(ps, ntl, src, dst):
        m0 = arena.top
        NT = ntl * 128
        samp = ntl == 17
        AT = arena.alloc((2, 2176), BF16)
        B_AT = Buf()
        mA = arena.top
        AN = arena.alloc((2, 2176), F32)
        AD = arena.alloc((2, 2176), F32)
        B_ANall = Buf()
        m1 = arena.top
        def pipeline(items, lag):
            n = len(items)
            for i in range(n + lag):
                if i < n:
                    items[i][0]()
                if i >= lag:
                    items[i - lag][1]()

        RG = [[0, 1], [2, 3], [4, 5], [6, 7]]
        for g in range(3):
            arena.top = m1
            d = DIL[g]
            nb = 16 // d
            QKT = arena.alloc((4, 2176), BF16)
            B_QK = [Buf() for _ in range(17)]
            Vpad = arena.alloc((17, 4, 128), BF16)
            B_V = [Buf() for _ in range(17)]
            MSET("pool", Vpad, 0.0, B_V)
            mW = arena.top
            wq = arena.alloc((8, 256), BF16)
            wk = arena.alloc((8, 256), BF16)
            wv = arena.alloc((8, 256), BF16)
            B_wq, B_wk, B_wv = Buf(), Buf(), Buf()
            wload(wq, w_in, C_QA + g * 256, 256, 8, [B_wq])
            wload(wk, w_in, C_KA + g * 256, 256, 8, [B_wk])
            wload(wv, w_in, C_VA + g * 256, 256, 8, [B_wv])
            NBUF = 5
            sq = [arena.alloc((512,), F32) for _ in range(NBUF)]
            B_sq = [Buf() for _ in range(NBUF)]
            qkn = [arena.alloc((512,), F32) for _ in range(NBUF)]
            B_qkn = [Buf() for _ in range(NBUF)]
            qkb = [arena.alloc((512,), BF16) for _ in range(NBUF)]
            B_qkb = [Buf() for _ in range(NBUF)]
            st8 = [arena.alloc((24,), F32) for _ in range(NBUF)]
            B_st8 = [Buf() for _ in range(NBUF)]
            vf = [arena.alloc((256,), F32) for _ in range(2)]
            B_vf = [Buf() for _ in range(2)]
            pqs = {}
            pts = {}

            def q0(t):
                i = t % NBUF
                pqk, bp = bank()
                for kc in range(8):
                    MM(pqk[:, 0:256], xnT[:, kc, t * 128:(t + 1) * 128], wq[:, kc, :], kc == 0, kc == 7, [B_xnT[t], B_wq], [bp])
                for kc in range(8):
                    MM(pqk[:, 256:512], xnT[:, kc, t * 128:(t + 1) * 128], wk[:, kc, :], kc == 0, kc == 7, [B_xnT[t], B_wk], [bp])
                ACT(sq[i], pqk[:, :], AF.Square, [bp], [B_sq[i]])
                pqs[t] = (pqk, bp)

            def q1(t):
                i = t % NBUF
                S.dve(lambda e, o=st8[i][:, 0:8], a=sq[i].rearrange("p (h d) -> p h d", h=8): e.tensor_reduce(out=o, in_=a, axis=AX.X, op=ALU.add), [B_sq[i]], [B_st8[i]])

            def q2(t):
                i = t % NBUF
                ACT(st8[i][:, 8:16], st8[i][:, 0:8], AF.Sqrt, [B_st8[i], B_const], [B_st8[i]], bias=epsb[:], scale=1.0 / 64)

            def q3(t):
                i = t % NBUF
                pqk, bp = pqs.pop(t)
                RCP(st8[i][:, 16:24], st8[i][:, 8:16], [B_st8[i]], [B_st8[i]])
                TT("dve", qkn[i].rearrange("p (h d) -> p h d", h=8), pqk[:, :].rearrange("p (h d) -> p h d", h=8),
                   st8[i][:, 16:24].unsqueeze(2).broadcast_to([128, 8, 64]), ALU.mult, [bp, B_st8[i]], [B_qkn[i]])

            def q4(t):
                i = t % NBUF
                TT("pool", qkn[i], qkn[i], gqk[:, g, :], ALU.mult, [B_qkn[i], B_const], [B_qkn[i]])

            def q5(t):
                i = t % NBUF
                CP("act", qkb[i], qkn[i], [B_qkn[i]], [B_qkb[i]])
                if t == 16:
                    DMA("sp", kvs[g][:, 0, :], qkn[i][:, 256:512], [B_qkn[i]], ())
                elif t >= 16 - d:
                    r0 = 128 * t - (2048 - 128 * d)
                    DMA("sp", kvp[g][r0:r0 + 128, 0, :], qkn[i][:, 256:512], [B_qkn[i]], ())

            def q6(t):
                i = t % NBUF
                pt, bpt = tbank()
                for j in range(4):
                    TR(pt[:, j * 128:(j + 1) * 128], qkb[i][:, j * 128:(j + 1) * 128], [B_qkb[i]], [bpt])
                pts[t] = (pt, bpt)

            def q7(t):
                pt, bpt = pts.pop(t)
                CP("dve", QKT[:, :, t * 128:(t + 1) * 128], pt[:, 0:512].rearrange("p (k n) -> p k n", k=4), [bpt], [B_QK[t]])

            mpipe(ntl, [q0, q1, q2, q3, q4, q5, q6, q7], [0, 1, 2, 3, 4, 5, 6, 7])

            def vtile(ti, lhs_of_kc, rd, out_ap):
                i = ti % 2
                pv, bp = bank()
                for kc in range(8):
                    MM(pv[:, 0:256], lhs_of_kc(kc), wv[:, kc, :], kc == 0, kc == 7, rd + [B_wv], [bp])
                if out_ap is not None:
                    CP("act", vf[i], pv[:, 0:256], [bp], [B_vf[i]])
                    DMA("sp", out_ap, vf[i], [B_vf[i]], ())
                    pv3 = vf[i].rearrange("p (h d) -> p h d", h=4)
                    rdv = [B_vf[i]]
                else:
                    pv3 = pv[:, 0:256].rearrange("p (h d) -> p h d", h=4)
                    rdv = [bp]
                CP("dve", Vpad[:, ti, 0:4:2, 0:64], pv3[:, 0:4:2, :], rdv, [B_V[ti]])
                CP("dve", Vpad[:, ti, 1:4:2, 64:128], pv3[:, 1:4:2, :], rdv, [B_V[ti]])

            for r in range(d):
                for ib in range(nb):
                    ti = r * nb + ib
                    lo = r + d * 128 * ib
                    hi = r + d * 128 * (ib + 1)
                    out_ap = None
                    if ib == nb - 1:
                        out_ap = kvp[g].rearrange("(j dd) two c -> dd j two c", dd=d)[r, :, 1, :]
                    vtile(ti, lambda kc, lo=lo, hi=hi: xnT[:, kc, lo:hi:d], [B_xnT[t] for t in range(16)], out_ap)
            if samp:
                vtile(16, lambda kc: xnT[:, kc, 2048:2176], [B_xnT[16]], kvs[g][:, 1, :])
            S.barrier()
            arena.top = mW
            KTcg = arena.alloc((2, 128 * d), BF16)
            Vcxg = arena.alloc((d, 4, 128), BF16)
            B_cinK, B_cinV, B_coutK, B_coutV = Buf(), Buf(), Buf(), Buf()
            DMA("sp", cinK[g].rearrange("p (a n) -> p a n", a=2),
                QKT[:, 2:4, 2048 - 128 * d:2048].bitcast(F32), B_QK[0:16], [B_cinK])
            DMA("sp", cinV[g].rearrange("p (r n) -> p r n", r=d),
                Vpad[:, 0:16, :, :].rearrange("p (r i) h e -> p r i (h e)", i=nb)[:, :, nb - 1, :].bitcast(F32), B_V[0:16], [B_cinV])
            S.coll(lambda e, a=cinK[g], b=coutK[g]: e.collective_compute("AllGather", ALU.bypass, replica_groups=RG,
                                                                         ins=[a.opt()], outs=[b.opt()]), [B_cinK], [B_coutK])
            S.coll(lambda e, a=cinV[g], b=coutV[g]: e.collective_compute("AllGather", ALU.bypass, replica_groups=RG,
                                                                         ins=[a.opt()], outs=[b.opt()]), [B_cinV], [B_coutV])
            DMA("sp", KTcg.bitcast(F32), coutK[g][0:128, :].rearrange("p (a n) -> p a n", a=2), [B_coutK], [B_KTc[g]])
            DMA("sp", Vcxg.rearrange("p r h e -> p r (h e)").bitcast(F32), coutV[g][0:128, :].rearrange("p (r n) -> p r n", r=d), [B_coutV], [B_Vcx[g]])
            LAG = 3
            NPB = LAG + 2
            ptb = [arena.alloc((256,), BF16) for _ in range(NPB)]
            B_ptb = [Buf() for _ in range(NPB)]
            ptm = [arena.alloc((256,), BF16) for _ in range(NPB)]
            B_ptm = [Buf() for _ in range(NPB)]
            allQK = B_QK[0:16]
            items = []
            state = {"cnt": 0, "b3": 0}

            def bank3():
                i = state["b3"]
                state["b3"] = (i + 1) % 3
                return PB[i], B_PB[i]
            for hp in range(2):
                for r in range(d):
                    accs = {}
                    started = {}
                    kbs = [-1] + list(range(nb))
                    for kb in kbs:
                        for hh in range(2):
                            def mkA(hp=hp, r=r, kb=kb, hh=hh, slot=len(items) % NPB):
                                nq = 1 if (kb == -1 or kb == nb - 1) else 2
                                N = 128 * nq
                                qlo = r + d * 128 * max(kb, 0)
                                p0 = hh * 64
                                pss, bps = bank3()
                                if kb == -1:
                                    kap = KTcg[p0:p0 + 64, hp, r:128 * d:d]
                                    krd = [B_KTc[g]]
                                    msk = ctxm[:]
                                else:
                                    kap = QKT[p0:p0 + 64, 2 + hp, qlo:qlo + d * 128:d]
                                    krd = allQK
                                    msk = tri2[:, 0:N]
                                MM(pss[:, 0:N], kap, QKT[p0:p0 + 64, hp, qlo:qlo + d * N:d], True, True, krd + allQK, [bps])
                                ACT(ptb[slot][:, 0:N], pss[:, 0:N], AF.Exp, [bps], [B_ptb[slot]], scale=0.125)
                                state["cnt"] += 1
                                TT("pool" if state["cnt"] % 2 else "dve", ptm[slot][:, 0:N], ptb[slot][:, 0:N], msk, ALU.mult,
                                   [B_ptb[slot], B_const], [B_ptm[slot]])

                            def mkB(hp=hp, r=r, kb=kb, hh=hh, slot=len(items) % NPB, accs=accs, started=started):
                                nq = 1 if (kb == -1 or kb == nb - 1) else 2
                                h = 2 * hp + hh
                                if kb == -1:
                                    vap = Vcxg[:, r, h, :]
                                    vrd = [B_Vcx[g]]
                                else:
                                    vap = Vpad[:, r * nb + kb, h, :]
                                    vrd = [B_V[r * nb + kb]]
                                targets = [(max(kb, 0), 0)]
                                if nq == 2:
                                    targets.append((kb + 1, 1))
                                for (ib, half) in targets:
                                    if ib not in accs:
                                        accs[ib] = rbank(ib % 2)
                                        started[ib] = False
                                    pa, bpa = accs[ib]
                                    last = (kb == ib) and hh == 1
                                    rhs = ptm[slot][:, half * 128:(half + 1) * 128]
                                    MM(pa[:, 0:128], vap, rhs, not started[ib], last, vrd + [B_ptm[slot]], [bpa])
                                    started[ib] = True
                                    MM(pa[:, 128:256], onespad[:, hh, :], rhs, False, last, [B_const, B_ptm[slot]], [bpa])
                                if kb >= 0 and hh == 1:
                                    pa, bpa = accs.pop(kb)
                                    lo = r + d * 128 * kb
                                    hi = lo + d * 128
                                    if g == 0:
                                        CP("act", AN[:, hp, lo:hi:d], pa[:, 0:128], [bpa], [B_ANall])
                                        CP("act", AD[:, hp, lo:hi:d], pa[:, 128:256], [bpa], [B_ANall])
                                    else:
                                        TT("dve", AN[:, hp, lo:hi:d], AN[:, hp, lo:hi:d], pa[:, 0:128], ALU.add, [bpa, B_ANall], [B_ANall])
                                        TT("dve", AD[:, hp, lo:hi:d], AD[:, hp, lo:hi:d], pa[:, 128:256], ALU.add, [bpa, B_ANall], [B_ANall])
                            items.append((mkA, mkB))
            band_thunks = []
            for i in range(len(items) + LAG):
                if i < len(items):
                    band_thunks.append(items[i][0])
                if i >= LAG:
                    band_thunks.append(items[i - LAG][1])
            samp_thunks = []
            if samp:
                nres = (1, 4, 8)[g]
                mbase = (0, 1, 5)[g]
                W = nres * 32
                Kc = [arena.alloc((nres, 256), BF16) for _ in range(2)]
                B_Kc = [Buf() for _ in range(2)]
                Vc = [arena.alloc((nres, 256), BF16) for _ in range(2)]
                B_Vc = [Buf() for _ in range(2)]
                KcT = [arena.alloc((nres, 2, 128), BF16) for _ in range(2)]
                B_KcT = [Buf() for _ in range(2)]
                pbS = [arena.alloc((256,), BF16) for _ in range(2)]
                B_pbS = [Buf() for _ in range(2)]
                pmS = [arena.alloc((256,), BF16) for _ in range(2)]
                B_pmS = [Buf() for _ in range(2)]
                pnS = [arena.alloc((128,), BF16) for _ in range(2)]
                B_pnS = [Buf() for _ in range(2)]
                bsn, bsd = B_PT[1], B_PB[3]
                snum = PT32[1][:, 0:256].rearrange("p (a n) -> p a n", a=2)
                sden = PB[3][:, :].rearrange("p (h n) -> p h n", h=4)
                Qblk = arena.alloc((2, 16, 16), BF16)
                B_Qb = Buf()
                MSET("dve", Qblk, 0.0, [B_Qb])
                for hp in range(2):
                    CP("dve", Qblk[0:64, hp, :, 0:8], QKT[0:64, hp, 2048:2176].rearrange("p (b t) -> p b t", b=16), [B_QK[16]], [B_Qb])
                    CP("dve", Qblk[64:128, hp, :, 8:16], QKT[64:128, hp, 2048:2176].rearrange("p (b t) -> p b t", b=16), [B_QK[16]], [B_Qb])
                for h in range(4):
                    hp, hh = h // 2, h % 2
                    p0 = hh * 64
                    pss, bps = bank3()
                    MM(pss[:, 0:128], QKT[p0:p0 + 64, 2 + hp, 2048:2176], QKT[p0:p0 + 64, hp, 2048:2176], True, True, [B_QK[16]], [bps])
                    bi = h % 2
                    ACT(pbS[bi][:, 0:128], pss[:, 0:128], AF.Exp, [bps], [B_pbS[bi]], scale=0.125)
                    TT("dve", pnS[bi], pbS[bi][:, 0:128], mnew[:, g, :], ALU.mult, [B_pbS[bi], B_const], [B_pnS[bi]])
                    MM(snum[:, hp, :], Vpad[:, 16, h, :], pnS[bi], h == 0, False, [B_V[16], B_pnS[bi]], [bsn])
                    MM(sden[:, h, :], onesfull[:], pnS[bi], h == 0, False, [B_const, B_pnS[bi]], [bsd])
                cch = caches[g]

                def sL(b):
                    ci = b % 2
                    cv = cch[b].rearrange("(k dd) two c -> k dd two c", dd=d)
                    DMA("pool", Kc[ci], cv[:, 0:nres, 0, :], (), [B_Kc[ci]])
                    DMA("pool", Vc[ci], cv[:, 0:nres, 1, :], (), [B_Vc[ci]])

                def sA(b):
                    ci = b % 2
                    for r0 in range(0, nres, 4):
                        nr = min(4, nres - r0)
                        pt, bpt = PT[0], B_PT[0]
                        for rr_ in range(nr):
                            for j in range(2):
                                TR(pt[:, (2 * rr_ + j) * 128:(2 * rr_ + j + 1) * 128], Kc[ci][:, r0 + rr_, j * 128:(j + 1) * 128], [B_Kc[ci]], [bpt])
                        CP("act", KcT[ci][:, r0:r0 + nr, :, :], pt[:, 0:nr * 256].rearrange("p (r k n) -> p r k n", r=nr, k=2), [bpt], [B_KcT[ci]])
                    pss, bps = bank3()
                    for rr_ in range(nres):
                        for hp in range(2):
                            MM(pss[:, rr_ * 32 + hp * 16:rr_ * 32 + hp * 16 + 16], KcT[ci][:, rr_, hp, :], Qblk[:, hp, b, :],
                               True, True, [B_KcT[ci], B_Qb], [bps])
                    ACT(pbS[ci][:, 0:W], pss[:, 0:W], AF.Exp, [bps], [B_pbS[ci]], scale=0.125)
                    TT("dve", pmS[ci][:, 0:W], pbS[ci][:, 0:W], mc[:, mbase:mbase + nres, :].rearrange("p r n -> p (r n)"), ALU.mult,
                       [B_pbS[ci], B_const], [B_pmS[ci]])

                def sB(b):
                    ci = b % 2
                    for rr_ in range(nres):
                        for h in range(4):
                            rhs = pmS[ci][:, rr_ * 32 + h * 8:rr_ * 32 + h * 8 + 8]
                            r64 = (h % 2) * 64
                            MM(snum[r64:r64 + 64, h // 2, 8 * b:8 * b + 8], Vc[ci][:, rr_, h * 64:(h + 1) * 64], rhs, False, False,
                               [B_Vc[ci], B_pmS[ci]], [bsn])
                        MM(sden[:, :, 8 * b:8 * b + 8], onesfull[:], pmS[ci][:, rr_ * 32:(rr_ + 1) * 32], False, False, [B_const, B_pmS[ci]], [bsd])

                sL(0)
                for st_ in range(17):
                    def thunk(st_=st_):
                        if st_ >= 1:
                            sB(st_ - 1)
                        if st_ + 1 < 16:
                            sL(st_ + 1)
                        if st_ < 16:
                            sA(st_)
                    samp_thunks.append(thunk)
            nbt, nst = len(band_thunks), len(samp_thunks)
            si_ = 0
            for bi_, th in enumerate(band_thunks):
                th()
                while si_ < nst and (si_ + 1) * nbt <= (bi_ + 1) * nst:
                    samp_thunks[si_]()
                    si_ += 1
            while si_ < nst:
                samp_thunks[si_]()
                si_ += 1
            if samp:
                for hp in range(2):
                    if g == 0:
                        CP("act", AN[:, hp, 2048:2176], snum[:, hp, :], [bsn], [B_ANall])
                    else:
                        TT("dve", AN[:, hp, 2048:2176], AN[:, hp, 2048:2176], snum[:, hp, :], ALU.add, [bsn, B_ANall], [B_ANall])
                for h in range(4):
                    hp, hh = h // 2, h % 2
                    if g == 0:
                        CP("act", AD[hh * 64:(hh + 1) * 64, hp, 2048:2176], sden[hh * 64:(hh + 1) * 64, h, :], [bsd], [B_ANall])
                    else:
                        TT("dve", AD[hh * 64:(hh + 1) * 64, hp, 2048:2176], AD[hh * 64:(hh + 1) * 64, hp, 2048:2176], sden[hh * 64:(hh + 1) * 64, h, :],
                           ALU.add, [bsd, B_ANall], [B_ANall])
            S.barrier()
        for hp in range(2):
            RCP(AD[:, hp, 0:NT], AD[:, hp, 0:NT], [B_ANall], [B_ANall])
            TT("dve", AT[:, hp, 0:NT], AN[:, hp, 0:NT], AD[:, hp, 0:NT], ALU.mult, [B_ANall], [B_AT])
        if dbg:
            DMA("pool", dbgA[:, :, 0:NT], AT[:, :, 0:NT], [B_AT], ())
        S.barrier()
        arena.top = mA
        if mstop == 4:
            arena.top = m0; return
        onT = arena.alloc((8, 2176), BF16)
        B_on = [Buf() for _ in range(8)]
        mO = arena.top

        lrT = arena.alloc((2176,), F32)
        B_lr = Buf()
        wlr = arena.alloc((8, 16), BF16)
        B_wlr = Buf()
        wload(wlr, w_in, C_LR, 16, 8, [B_wlr])
        for (t0, n) in groups(ntl):
            N = n * 128
            pl, bp = bank()
            for kc in range(8):
                MM(pl[0:16, 0:N], wlr[:, kc, :], xnT[:, kc, t0 * 128:t0 * 128 + N], kc == 0, kc == 7, [B_xnT[t0 + j] for j in range(n)] + [B_wlr], [bp])
            CP("act", lrT[0:16, t0 * 128:t0 * 128 + N], pl[0:16, 0:N], [bp], [B_lr])
        qgT = arena.alloc((4, 2048), BF16)
        B_qg = [Buf() for _ in range(4)]
        dgall = arena.alloc((8,), F32)
        B_dg = Buf()
        osq = [arena.alloc((128,), F32) for _ in range(2)]
        B_osq = [Buf() for _ in range(2)]
        rs = [arena.alloc((128,), F32) for _ in range(2)]
        B_rs = [Buf() for _ in range(2)]

        def out_norm(h, po, bpo, c0, ci):
            pm, bpm = bank()
            for j in range(2):
                ACT(osq[j], po[j][:, 0:128], AF.Square, [bpo[j]], [B_osq[j]])
                MM(pm[:, 0:128], ones256[:], osq[j], j == 0, j == 1, [B_const, B_osq[j]], [bpm])
            ri = ci % 2
            ACT(rs[ri], pm[:, 0:128], AF.Sqrt, [bpm, B_const], [B_rs[ri]], bias=epsb[:], scale=1.0)
            RCP(rs[ri], rs[ri], [B_rs[ri]], [B_rs[ri]])
            for j in range(2):
                STT("dve", onT[:, 2 * h + j, c0:c0 + 128], po[j][:, 0:128], gout[:, j:j + 1], rs[ri], ALU.mult, ALU.mult,
                    [bpo[j], B_const, B_rs[ri]], [B_on[2 * h + j]])

        m2 = arena.top
        if mstop == 5:
            S.barrier(); arena.top = m0; return
        _hc = {}

        def ha(name, shape, dt):
            if name not in _hc:
                _hc[name] = arena.alloc(shape, dt)
            return _hc[name]

        def hb(name):
            if name not in _hc:
                _hc[name] = Buf()
            return _hc[name]

        for h in range(4):
            pass
            wqg = ha("wq%d" % (h % 2), (8, 128), BF16)
            wkg = ha("wk%d" % (h % 2), (8, 128), BF16)
            wvg = ha("wv", (8, 256), BF16)
            B_wq, B_wk, B_wv = hb("bwq%d" % (h % 2)), hb("bwk%d" % (h % 2)), hb("bwv")
            wload(wqg, w_in, C_QG + h * 128, 128, 8, [B_wq])
            wload(wkg, w_in, C_KG + h * 128, 128, 8, [B_wk])
            wload(wvg, w_in, C_VG + h * 256, 256, 8, [B_wv])
            spb = ha("a4", (2176,), F32)
            B_sp = hb("b2")
            Bc = ha("a5", (2176,), F32)
            B_Bc = hb("b3")
            e1 = [ha("a6_%d" % _i, (512,), F32) for _i in range(2)]
            B_e1 = [hb("b4_%d" % _i) for _i in range(2)]
            qeT = ha("a7", (2176,), BF16)
            keT = ha("a8", (2176,), BF16)
            B_qe = hb("b5")
            B_ke = hb("b6")
            vh = ha("a9", (17, 256), BF16)
            B_vh = [hb("b7_%d" % _i) for _i in range(17)]
            offs = ha("a10", (64,), F32)
            B_off = hb("b8")
            cnt = 0
            for (t0, n) in groups(ntl):
                N = n * 128
                c0 = t0 * 128
                pl, bp = bank()
                MM(pl[:, 0:N], gup[0:16, h * 128:(h + 1) * 128], lrT[0:16, c0:c0 + N], True, True, [B_const, B_lr], [bp])
                i = cnt % 2
                cnt += 1
                ACT(e1[i][:, 0:N], pl[:, 0:N], AF.Exp, [bp, B_const], [B_e1[i]], bias=nabias[:, h:h + 1], scale=-1.0)
                ACT(spb[:, c0:c0 + N], e1[i][:, 0:N], AF.Ln, [B_e1[i], B_const], [B_sp], bias=one1[:], scale=1.0)
            for t in range(ntl):
                pv, bp = bank()
                for kc in range(8):
                    MM(pv[:, 0:256], xnT[:, kc, t * 128:(t + 1) * 128], wvg[:, kc, :], kc == 0, kc == 7, [B_xnT[t], B_wv], [bp])
                CP("act", vh[:, t, :], pv[:, 0:256], [bp], [B_vh[t]])
            for pc in range(4):
                S.dve(lambda e, o=Bc[:, 512 * pc:512 * pc + 512], a=ones512[:], b=spb[:, 512 * pc:512 * pc + 512],
                      ini=(0.0 if pc == 0 else Bc[:, 512 * pc - 1:512 * pc]):
                      e.tensor_tensor_scan(out=o, data0=a, data1=b, initial=ini, op0=ALU.mult, op1=ALU.add), [B_sp, B_const, B_Bc], [B_Bc])
            MSET("dve", offs[:, 0:32], 0.0, [B_off])
            CP("dve", offs[:, 1:16], Bc[:, 127:1920:128], [B_Bc], [B_off])
            TT("dve", Bc[:, 0:2048].rearrange("p (c s) -> p c s", c=16), Bc[:, 0:2048].rearrange("p (c s) -> p c s", c=16),
               offs[:, 0:16].unsqueeze(2).broadcast_to([128, 16, 128]), ALU.subtract, [B_Bc, B_off], [B_Bc])
            ACT(offs[:, 32:48], Bc[:, 127:2048:128], AF.Exp, [B_Bc], [B_off], scale=-1.0 / 16)
            if samp:
                S.dve(lambda e, o=Bc[:, 2048:2176], a=ones512[:, 0:128], b=spb[:, 2048:2176]:
                      e.tensor_tensor_scan(out=o, data0=a, data1=b, initial=0.0, op0=ALU.mult, op1=ALU.add), [B_sp, B_const], [B_Bc])
                CP("dve", offs[:, 17:32], Bc[:, 2048 + 7:2048 + 120:8], [B_Bc], [B_off])
                TT("dve", Bc[:, 2048:2176].rearrange("p (c s) -> p c s", c=16), Bc[:, 2048:2176].rearrange("p (c s) -> p c s", c=16),
                   offs[:, 16:32].unsqueeze(2).broadcast_to([128, 16, 8]), ALU.subtract, [B_Bc, B_off], [B_Bc])
                ACT(offs[:, 48:64], Bc[:, 2048 + 7:2176:8], AF.Exp, [B_Bc], [B_off], scale=-1.0 / 16)
            for (t0, n) in groups(ntl):
                N = n * 128
                c0 = t0 * 128
                rd = [B_xnT[t0 + j] for j in range(n)]
                pq, bp = bank()
                for kc in range(8):
                    MM(pq[:, 0:N], wqg[:, kc, :], xnT[:, kc, c0:c0 + N], kc == 0, kc == 7, rd + [B_wq], [bp])
                i = cnt % 2
                cnt += 1
                ACT(e1[i][:, 0:N], Bc[:, c0:c0 + N], AF.Exp, [B_Bc, B_const], [B_e1[i]], bias=lnsc[:], scale=-1.0 / 16)
                TT("dve", qeT[:, c0:c0 + N], pq[:, 0:N], e1[i][:, 0:N], ALU.mult, [bp, B_e1[i]], [B_qe])
                pk, bp = bank()
                for kc in range(8):
                    MM(pk[:, 0:N], wkg[:, kc, :], xnT[:, kc, c0:c0 + N], kc == 0, kc == 7, rd + [B_wk], [bp])
                i = cnt % 2
                cnt += 1
                ACT(e1[i][:, 0:N], Bc[:, c0:c0 + N], AF.Exp, [B_Bc], [B_e1[i]], scale=1.0 / 16)
                TT("dve", keT[:, c0:c0 + N], pk[:, 0:N], e1[i][:, 0:N], ALU.mult, [bp, B_e1[i]], [B_ke])
            eoff = ha("a11", (16,), F32)
            B_eo = hb("b9")
            ACT(eoff, offs[:, 0:16], AF.Exp, [B_off], [B_eo], scale=-1.0 / 16)
            TT("dve", qgT[:, h, :].rearrange("p (c s) -> p c s", c=16), qeT[:, 0:2048].rearrange("p (c s) -> p c s", c=16),
               eoff.unsqueeze(2).broadcast_to([128, 16, 128]), ALU.mult, [B_qe, B_eo], [B_qg[h]])
            TT("dve", dgall[:, h:h + 1], eoff[:, 15:16], offs[:, 47:48], ALU.mult, [B_eo, B_off], [B_dg])
            if mstop == 6:
                S.barrier(); arena.top = m0; return
            attm = [ha("a12_%d" % _i, (128,), BF16) for _i in range(3)]
            B_att = [hb("b10_%d" % _i) for _i in range(3)]
            kdT = [ha("a13_%d" % _i, (128,), BF16) for _i in range(2)]
            B_kdT = [hb("b11_%d" % _i) for _i in range(2)]
            kd = [ha("a14_%d" % _i, (128,), BF16) for _i in range(2)]
            B_kd = [hb("b12_%d" % _i) for _i in range(2)]
            kdTall = ha("a15", (2048,), BF16)
            B_kdTall = hb("b13")
            kdall = ha("a16", (16, 128), BF16)
            B_kdall = [hb("b14"), hb("b15")]
            TT("pool", kdTall.rearrange("p (c s) -> p c s", c=16), keT[:, 0:2048].rearrange("p (c s) -> p c s", c=16),
               offs[:, 32:48].unsqueeze(2).broadcast_to([128, 16, 128]), ALU.mult, [B_ke, B_off], [B_kdTall])
            for hf in range(2):
                pt, bpt = tbank()
                for cc in range(8):
                    TR(pt[:, cc * 128:(cc + 1) * 128], kdTall[:, (hf * 8 + cc) * 128:(hf * 8 + cc + 1) * 128], [B_kdTall], [bpt])
                CP("act", kdall[:, hf * 8:(hf + 1) * 8, :], pt[:, 0:1024].rearrange("p (c n) -> p c n", c=8), [bpt], [B_kdall[hf]])
            pus = {}

            def gA(c):
                c0 = c * 128
                ai = c % 3
                pa, bpa = bank()
                MM(pa[:, 0:128], keT[:, c0:c0 + 128], qeT[:, c0:c0 + 128], True, True, [B_ke, B_qe], [bpa])
                TT("dve", attm[ai], pa[:, 0:128], causal[:], ALU.mult, [bpa, B_const], [B_att[ai]])
                pu, bpu = bank()
                MM(pu[:, 0:256], kdall[:, c, :], vh[:, c, :], True, True, [B_kdall[c // 8], B_vh[c]], [bpu])
                pus[c] = (pu, bpu)

            def gB(c):
                c0 = c * 128
                ai = c % 3
                for j in range(2):
                    po, bpo = rbank(j)
                    MM(po[:, 0:128], vh[:, c, j * 128:(j + 1) * 128], attm[ai], True, False, [B_vh[c], B_att[ai]], [bpo])
                    MM(po[:, 0:128], Sbf[:, h, j * 128:(j + 1) * 128], qeT[:, c0:c0 + 128], False, True, [B_Sb[h], B_qe], [bpo])
                    CP("act", onT[:, 2 * h + j, c0:c0 + 128], po[:, 0:128], [bpo], [B_on[2 * h + j]])
                pu, bpu = pus.pop(c)
                STT("dve", Sst[:, h, :], Sst[:, h, :], offs[:, 32 + c:33 + c], pu[:, 0:256], ALU.mult, ALU.add, [B_S[h], B_off, bpu], [B_S[h]])
                CP("dve", Sbf[:, h, :], Sst[:, h, :], [B_S[h]], [B_Sb[h]])

            pipeline([(lambda c=c: gA(c), lambda c=c: gB(c)) for c in range(16)], 1)
            if samp:
                c0 = 2048
                s0 = [ha("a17_%d" % _i, (256,), F32) for _i in range(4)]
                B_s0 = [hb("b16_%d" % _i) for _i in range(4)]
                s0b = [ha("a18_%d" % _i, (256,), BF16) for _i in range(4)]
                B_s0b = [hb("b17_%d" % _i) for _i in range(4)]
                vblk = ha("a19", (16, 256), BF16)
                B_vblk = hb("b18")
                pa, bpa = bank()
                MM(pa[:, 0:128], keT[:, c0:c0 + 128], qeT[:, c0:c0 + 128], True, True, [B_ke, B_qe], [bpa])
                TT("dve", attm[0], pa[:, 0:128], bd8[:], ALU.mult, [bpa, B_const], [B_att[0]])
                po, bpo = [None, None], [None, None]
                for j in range(2):
                    po[j], bpo[j] = rbank(j)
                    MM(po[j][:, 0:128], vh[:, 16, j * 128:(j + 1) * 128], attm[0], True, False, [B_vh[16], B_att[0]], [bpo[j]])
                TT("pool", kdT[0].rearrange("p (b s) -> p b s", b=16), keT[:, c0:c0 + 128].rearrange("p (b s) -> p b s", b=16),
                   offs[:, 48:64].unsqueeze(2).broadcast_to([128, 16, 8]), ALU.mult, [B_ke, B_off], [B_kdT[0]])
                pt, bpt = tbank()
                TR(pt[:, 0:128], kdT[0], [B_kdT[0]], [bpt])
                CP("act", kd[0], pt[:, 0:128], [bpt], [B_kd[0]])
                TT("dve", vblk, vh[:, 16, :].unsqueeze(1).broadcast_to([128, 16, 256]), blk[:].unsqueeze(2).broadcast_to([128, 16, 256]),
                   ALU.mult, [B_vh[16], B_const], [B_vblk])
                for b2 in range(8):
                    pu, bpu = bank()
                    MM(pu[:, 0:512], kd[0], vblk[:, 2 * b2:2 * b2 + 2, :], True, True, [B_kd[0], B_vblk], [bpu])
                    for bb in range(2):
                        b = 2 * b2 + bb
                        si = b % 4
                        DMA("pool", s0[si], sgla[b, h], (), [B_s0[si]])
                        CP("act", s0b[si], s0[si], [B_s0[si]], [B_s0b[si]])
                        for j in range(2):
                            MM(po[j][:, 8 * b:8 * b + 8], s0b[si][:, j * 128:(j + 1) * 128], qeT[:, c0 + 8 * b:c0 + 8 * b + 8], False, False,
                               [B_s0b[si], B_qe], [bpo[j]])
                        STT("dve", s0[si], s0[si], offs[:, 48 + b:49 + b], pu[:, bb * 256:(bb + 1) * 256], ALU.mult, ALU.add,
                            [B_s0[si], B_off, bpu, B_s0b[si]], [B_s0[si]])
                        DMA("sp", glas[b, h], s0[si], [B_s0[si]], ())
                out_norm(h, po, bpo, c0, 0)
        arena.top = m2
        B_sin = Buf()
        B_sout = Buf()
        B_SA = Buf()
        DMA("sp", sin_d.rearrange("p (h n) -> p h n", h=4), Sst[:, :, :], B_S, [B_sin])
        S.coll(lambda e: e.collective_compute("AllGather", ALU.bypass, replica_groups=[[0, 1], [2, 3], [4, 5], [6, 7]],
                                              ins=[sin_d.opt()], outs=[sout_d.opt()]), [B_sin], [B_sout])
        DMA("sp", SAf[:, :, :], sout_d[0:128, :].rearrange("p (h n) -> p h n", h=4), [B_sout], [B_SA])
        TS("dve", SAf[:, :, :], SAf[:, :, :], flag[:, 0:1], ALU.mult, [B_SA, B_const], [B_SA])
        CP("act", SAb[:, :, :], SAf[:, :, :], [B_SA], [B_SA])
        fsq = [[arena.alloc((512,), F32) for _ in range(2)] for _ in range(3)]
        B_fsq = [[Buf() for _ in range(2)] for _ in range(3)]
        frs = [arena.alloc((512,), F32) for _ in range(3)]
        B_frs = [Buf() for _ in range(3)]
        fpo = {}
        fpm = {}

        def f0(i):
            h, q4 = i // 4, i % 4
            c0 = q4 * 512
            si = i % 3
            lst = []
            for j in range(2):
                po, bpo = bank8()
                MM(po[:, 0:512], ident[:], onT[:, 2 * h + j, c0:c0 + 512], True, False, [B_const, B_on[2 * h + j]], [bpo])
                MM(po[:, 0:512], SAb[:, h, j * 128:(j + 1) * 128], qgT[:, h, c0:c0 + 512], False, True, [B_SA, B_qg[h]], [bpo])
                ACT(fsq[si][j], po[:, 0:512], AF.Square, [bpo], [B_fsq[si][j]])
                lst.append((po, bpo))
            fpo[i] = lst

        def f1(i):
            si = i % 3
            pm, bpm = bank8()
            for j in range(2):
                MM(pm[:, 0:512], ones256[:], fsq[si][j], j == 0, j == 1, [B_const, B_fsq[si][j]], [bpm])
            ACT(frs[si], pm[:, 0:512], AF.Sqrt, [bpm, B_const], [B_frs[si]], bias=epsb[:], scale=1.0)

        def f2(i):
            h, q4 = i // 4, i % 4
            c0 = q4 * 512
            si = i % 3
            lst = fpo.pop(i)
            RCP(frs[si], frs[si], [B_frs[si]], [B_frs[si]])
            for j in range(2):
                po, bpo = lst[j]
                STT("dve", onT[:, 2 * h + j, c0:c0 + 512], po[:, 0:512], gout[:, j:j + 1], frs[si], ALU.mult, ALU.mult,
                    [bpo, B_const, B_frs[si]], [B_on[2 * h + j]])

        mpipe(16, [f0, f1, f2], [0, 1, 2])
        for h in range(4):
            STT("dve", Sst[:, h, :], SAf[:, h, :], dgall[:, h:h + 1], Sst[:, h, :], ALU.mult, ALU.add, [B_SA, B_dg, B_S[h]], [B_S[h]])
            DMA("sp", glap[h], Sst[:, h, :], [B_S[h]], ())
        if dbg:
            DMA("pool", dbgO[:, :, 0:NT], onT[:, :, 0:NT], B_on, ())
        if mstop == 8:
            S.barrier(); arena.top = m0; return

        S.barrier()
        arena.top = mO
        allg = groups(ntl)
        wr = [arena.alloc((8, 128), BF16) for _ in range(3)]
        B_wr = [Buf() for _ in range(3)]
        sr = [arena.alloc((512,), BF16) for _ in range(3)]
        B_sr = [Buf() for _ in range(3)]
        cnt = 0
        for bk in range(8):
            wi_ = bk % 3
            wload(wr[wi_], w_in, C_RG + bk * 128, 128, 8, [B_wr[wi_]])
            for (t0, n) in allg:
                N = n * 128
                c0 = t0 * 128
                pr, bp = bank8()
                for kc in range(8):
                    MM(pr[:, 0:N], wr[wi_][:, kc, :], xnT[:, kc, c0:c0 + N], kc == 0, kc == 7, [B_xnT[t0 + j] for j in range(n)] + [B_wr[wi_]], [bp])
                i = cnt % 3
                cnt += 1
                ACT(sr[i][:, 0:N], pr[:, 0:N], AF.Silu, [bp], [B_sr[i]])
                TT("dve", onT[:, bk, c0:c0 + N], onT[:, bk, c0:c0 + N], sr[i][:, 0:N], ALU.mult, [B_sr[i], B_on[bk]], [B_on[bk]])
        mT = arena.alloc((8, 2176), BF16)
        B_m = [Buf() for _ in range(8)]
        NW = 3
        wga = [arena.alloc((8, 128), BF16) for _ in range(NW)]
        wgb = [arena.alloc((8, 128), BF16) for _ in range(NW)]
        wao = [arena.alloc((2, 128), BF16) for _ in range(NW)]
        wbo = [arena.alloc((8, 128), BF16) for _ in range(NW)]
        B_wga = [Buf() for _ in range(NW)]
        B_wgb = [Buf() for _ in range(NW)]
        B_wao = [Buf() for _ in range(NW)]
        B_wbo = [Buf() for _ in range(NW)]
        sa = [arena.alloc((512,), F32) for _ in range(4)]
        B_sa = [Buf() for _ in range(4)]
        t1 = [arena.alloc((512,), F32) for _ in range(3)]
        B_t1 = [Buf() for _ in range(3)]
        wo = arena.alloc((8, 1024), BF16)
        B_wo = Buf()
        xio = [arena.alloc((512,), F32) for _ in range(4)]
        B_xio = [Buf() for _ in range(4)]
        cnt = 0
        for fc in range(8):
            wi_ = fc % NW
            wload(wga[wi_], w_in, C_GA + fc * 128, 128, 8, [B_wga[wi_]])
            wload(wgb[wi_], w_in, C_GB + fc * 128, 128, 8, [B_wgb[wi_]])
            wload(wao[wi_], w_a_out, fc * 128, 128, 2, [B_wao[wi_]])
            wload(wbo[wi_], w_b_out, fc * 128, 128, 8, [B_wbo[wi_]])
            if fc == 2:
                wload(wo, w_out, 0, 1024, 8, [B_wo])
            for (t0, n) in allg:
                N = n * 128
                c0 = t0 * 128
                rdx = [B_xnT[t0 + j] for j in range(n)]
                pA, bA = bank8()
                for kc in range(8):
                    MM(pA[:, 0:N], wga[wi_][:, kc, :], xnT[:, kc, c0:c0 + N], kc == 0, kc == 7, rdx + [B_wga[wi_]], [bA])
                pC, bC = bank8()
                for hp in range(2):
                    MM(pC[:, 0:N], wao[wi_][:, hp, :], AT[:, hp, c0:c0 + N], hp == 0, hp == 1, [B_AT, B_wao[wi_]], [bC])
                ia = cnt % 4
                it = (cnt // 2) % 3
                cnt += 1
                ACT(sa[ia][:, 0:N], pA[:, 0:N], AF.Sigmoid, [bA], [B_sa[ia]])
                TT("dve", t1[it][:, 0:N], sa[ia][:, 0:N], pC[:, 0:N], ALU.mult, [B_sa[ia], bC], [B_t1[it]])
                pB, bB = bank8()
                for kc in range(8):
                    MM(pB[:, 0:N], wgb[wi_][:, kc, :], xnT[:, kc, c0:c0 + N], kc == 0, kc == 7, rdx + [B_wgb[wi_]], [bB])
                pD, bD = bank8()
                for bk in range(8):
                    MM(pD[:, 0:N], wbo[wi_][:, bk, :], onT[:, bk, c0:c0 + N], bk == 0, bk == 7, [B_on[bk], B_wbo[wi_]], [bD])
                ib = cnt % 4
                cnt += 1
                ACT(sa[ib][:, 0:N], pB[:, 0:N], AF.Sigmoid, [bB], [B_sa[ib]])
                TT("dve", sa[ib][:, 0:N], sa[ib][:, 0:N], pD[:, 0:N], ALU.mult, [B_sa[ib], bD], [B_sa[ib]])
                TT("pool", mT[:, fc, c0:c0 + N], t1[it][:, 0:N], sa[ib][:, 0:N], ALU.add, [B_t1[it], B_sa[ib]], [B_m[fc]])
        cnt = 0
        for t in range(ntl):
            for ch in range(2):
                po, bpo = bank8()
                for fc in range(8):
                    MM(po[:, :], mT[:, fc, t * 128:(t + 1) * 128], wo[:, fc, ch * 512:(ch + 1) * 512], fc == 0, fc == 7, [B_m[fc], B_wo], [bpo])
                xi = cnt % 4
                cnt += 1
                DMA("sp", xio[xi], src(ps, t)[:, ch * 512:(ch + 1) * 512], (), [B_xio[xi]])
                TT("dve", xio[xi], xio[xi], po[:, :], ALU.add, [bpo, B_xio[xi]], [B_xio[xi]])
                DMA("act", dst(ps, t)[:, ch * 512:(ch + 1) * 512], xio[xi], [B_xio[xi]], ())
        S.barrier()
        arena.top = m0

    def src_x(ps, t):
        return xs[0:128, :] if t == 16 else xp[128 * t:128 * (t + 1), :]

    def src_x1(ps, t):
        r = row0(ps, t)
        return x1d[r:r + 128, :]

    def src_x2(ps, t):
        r = row0(ps, t)
        return x2d[r:r + 128, :]

    def dst_y(ps, t):
        return ys[0:128, :] if t == 16 else yp[128 * t:128 * (t + 1), :]

    steps = []
    for ps in range(1):
        ntl = 17
        steps.append(lambda ps=ps, ntl=ntl: phase_norm(ps, ntl, src_x, 0))
        steps.append(lambda ps=ps, ntl=ntl: phase_ffn(ps, ntl, 0, src_x, src_x1))
        steps.append(lambda ps=ps, ntl=ntl: phase_norm(ps, ntl, src_x1, 1))
        steps.append(lambda ps=ps, ntl=ntl: phase_mixer(ps, ntl, src_x1, src_x2))
        steps.append(lambda ps=ps, ntl=ntl: phase_norm(ps, ntl, src_x2, 2))
        steps.append(lambda ps=ps, ntl=ntl: phase_ffn(ps, ntl, 1, src_x2, dst_y))
    for st_ in steps[:nsteps]:
        st_()
        S.barrier()
    S.emit(nc)
    return nc


def _consts():
    k = np.arange(128)[:, None]
    j = np.arange(256)[None, :]
    tri2 = np.where(j < 128, k <= j, k >= (j - 128)).astype(np.float32)
    q = np.arange(128)[None, :]
    causal = (k <= q).astype(np.float32)
    same = (k // 8) == (q // 8)
    bd8 = (same & (k <= q)).astype(np.float32)
    mnew = np.zeros((128, 3, 128), np.float32)
    for g, d in enumerate(DIL):
        mnew[:, g, :] = (same & (k <= q) & (((q - k) % d) == 0)).astype(np.float32)
    mc = np.zeros((128, 13, 32), np.float32)
    t = np.arange(8)[None, :]
    kk = np.arange(128)[:, None]
    m0 = (kk >= t).astype(np.float32)
    mc[:, 0, :] = np.tile(m0, (1, 4))
    for r in range(4):
        mm = (((t % 4) == r) & ((4 * kk + r) >= t)).astype(np.float32)
        mc[:, 1 + r, :] = np.tile(mm, (1, 4))
    for r in range(8):
        mm = ((t == r) & (kk >= 0)).astype(np.float32)
        mc[:, 5 + r, :] = np.tile(mm, (1, 4))
    blk = ((np.arange(128)[:, None] // 8) == np.arange(16)[None, :]).astype(np.float32)
    onespad = np.zeros((128, 2, 128), np.float32)
    onespad[:, 0, 0:64] = 1.0
    onespad[:, 1, 64:128] = 1.0
    return dict(c_ident=np.eye(128, dtype=np.float32), c_tri2=tri2, c_causal=causal, c_bd8=bd8, c_mnew=mnew,
                c_mc=mc, c_blk=blk, c_onespad=onespad)


_NC = None


def kernel(x_prompt, x_sample, cache_swa128_kv, cache_swa512_kv, cache_swa2048_kv, state_gla,
           norm_ffn1, ffn1_gate, ffn1_up, ffn1_down, norm_mix, w_in, a_q_norm, a_k_norm,
           g_alpha_up, g_alpha_bias, g_out_norm, w_a_out, w_b_out, w_out,
           norm_ffn2, ffn2_gate, ffn2_up, ffn2_down):
    global _NC
    f = lambda a: np.ascontiguousarray(np.asarray(a, dtype=np.float32))
    if _NC is None:
        _NC = build_nc()
    nc = _NC
    shared = dict(_consts())
    shared.update(f1g=f(ffn1_gate), f1u=f(ffn1_up), f1d=f(ffn1_down), f2g=f(ffn2_gate), f2u=f(ffn2_up), f2d=f(ffn2_down),
                  w_in=f(w_in), w_a_out=f(w_a_out), w_b_out=f(w_b_out), w_out=f(w_out), g_up=f(g_alpha_up))
    gn = np.stack([f(norm_ffn1), f(norm_mix), f(norm_ffn2)], 0)
    shared["c_gnorm"] = np.ascontiguousarray(np.broadcast_to(gn[None], (128, 3, D)))
    aq, ak = f(a_q_norm), f(a_k_norm)
    gqk = np.concatenate([np.tile(aq[:, None, :], (1, 4, 1)).reshape(3, 256), np.tile(ak[:, None, :], (1, 4, 1)).reshape(3, 256)], axis=1)
    shared["c_gqk"] = np.ascontiguousarray(np.broadcast_to(gqk[None], (128, 3, 512)))
    shared["c_gout"] = np.ascontiguousarray(f(g_out_norm).reshape(2, 128).T)
    shared["c_abias"] = np.ascontiguousarray(f(g_alpha_bias).reshape(4, 128).T)
    xp_, xs_ = f(x_prompt), f(x_sample)
    c1, c2, c3, sg = f(cache_swa128_kv), f(cache_swa512_kv), f(cache_swa2048_kv), f(state_gla)
    in_maps = []
    for c in range(8):
        m = dict(shared)
        m["xp"] = np.ascontiguousarray(xp_[c // 2, (c % 2) * 2048:(c % 2 + 1) * 2048])
        m["c_flag"] = np.full((128, 1), float(c % 2), np.float32)
        m["xs"] = xs_[16 * c:16 * c + 16].reshape(128, D)
        m["c128"] = c1[16 * c:16 * c + 16].reshape(16, 128, 2, 256)
        m["c512"] = c2[16 * c:16 * c + 16].reshape(16, 512, 2, 256)
        m["c2048"] = c3[16 * c:16 * c + 16].reshape(16, 2048, 2, 256)
        m["sgla"] = sg[16 * c:16 * c + 16]
        in_maps.append(m)
    res = run_bass_kernel_spmd(nc, in_maps, core_ids=list(range(8)))
    R = res.results
    y_prompt = np.stack([np.concatenate([R[2 * b]["yp"], R[2 * b + 1]["yp"]], 0) for b in range(4)], 0)
    y_sample = np.concatenate([R[c]["ys"].reshape(16, 8, D) for c in range(8)], 0)
    outs = [y_prompt, y_sample]
    for nm, w in (("kv128p", 128), ("kv512p", 512), ("kv2048p", 2048)):
        outs.append(np.stack([R[2 * b + 1][nm].reshape(w, 2, 4, 64) for b in range(4)], 0))
    outs.append(np.stack([R[2 * b + 1]["glap"] for b in range(4)], 0))
    for nm in ("kv128s", "kv512s", "kv2048s"):
        outs.append(np.concatenate([R[c][nm].reshape(16, 8, 2, 4, 64) for c in range(8)], 0))
    outs.append(np.concatenate([R[c]["glas"] for c in range(8)], 0))
    return tuple(np.ascontiguousarray(o, dtype=np.float32) for o in outs)
```

```python
import math
from contextlib import ExitStack
import numpy as np
import concourse.bass as bass
import concourse.mybir as mybir
from concourse.bass_utils import run_bass_kernel_spmd

F32 = mybir.dt.float32
BF16 = mybir.dt.bfloat16
AF = mybir.ActivationFunctionType
ALU = mybir.AluOpType
AX = mybir.AxisListType

N_DMA_SEMS = 12
D = 1024
DFF = 2816
EPS = 1e-6
C_QA, C_KA, C_VA, C_QG, C_KG, C_VG, C_RG, C_LR, C_GA, C_GB = 0, 768, 1536, 2304, 2816, 3328, 4352, 5376, 5392, 6416
DIL = (1, 4, 16)


class Buf:
    __slots__ = ("name", "w", "rs", "excl")

    def __init__(self, name="", excl=False):
        self.name = name
        self.w = None
        self.rs = []
        self.excl = excl


class Op:
    __slots__ = ("eng", "fn", "deps", "sig", "tick", "dma", "semi", "semv", "gi", "cc")


class Sched:
    ENGS = ("pe", "act", "dve", "pool", "sp")

    def __init__(self):
        self.ops = []
        self.last = {}
        self.dmas_since = []
        self.cur_bar = None
        self.bar_seen = set()
        self.ncc = 0

    def add(self, eng, fn, reads=(), writes=(), dma=False, extra=(), cc=False):
        op = Op()
        op.eng = eng
        op.fn = fn
        op.dma = dma
        op.cc = -1
        if cc:
            op.cc = self.ncc
            self.ncc += 1
        op.sig = False
        op.tick = 0
        op.semi = -1
        op.semv = 0
        op.gi = len(self.ops)
        deps = {}
        for b in reads:
            if b.w is not None:
                deps[b.w.gi] = b.w
            if b.excl:
                for r in b.rs:
                    if r.eng != eng:
                        deps[r.gi] = r
        for b in writes:
            if b.w is not None:
                deps[b.w.gi] = b.w
            for r in b.rs:
                deps[r.gi] = r
        for d in extra:
            deps[d.gi] = d
        if self.cur_bar is not None and eng not in self.bar_seen:
            deps[self.cur_bar.gi] = self.cur_bar
            self.bar_seen.add(eng)
        for b in reads:
            b.rs.append(op)
        for b in writes:
            b.w = op
            b.rs = []
        op.deps = [d for d in deps.values() if not (eng == "pe" and d.eng == "pe" and not d.dma)]
        self.ops.append(op)
        if dma or cc:
            self.dmas_since.append(op)
        else:
            self.last[eng] = op
        return op

    def coll(self, fn, reads=(), writes=()):
        return self.add("pool", fn, reads, writes, cc=True)

    def pe(self, fn, reads=(), writes=()):
        return self.add("pe", fn, reads, writes)

    def act(self, fn, reads=(), writes=()):
        return self.add("act", fn, reads, writes)

    def dve(self, fn, reads=(), writes=()):
        return self.add("dve", fn, reads, writes)

    def pool(self, fn, reads=(), writes=()):
        return self.add("pool", fn, reads, writes)

    def dma(self, q, fn, reads=(), writes=()):
        return self.add(q, fn, reads, writes, dma=True)

    def barrier(self):
        deps = [o for o in self.last.values()] + list(self.dmas_since)
        self.cur_bar = None
        op = self.add("sp", lambda e: e.nop(), extra=deps)
        self.cur_bar = op
        self.bar_seen = {"sp"}
        self.dmas_since = []
        self.last = {"sp": op}
        return op

    def emit(self, nc):
        ops = self.ops
        qcount = {e: 0 for e in self.ENGS}
        qhist = {e: [] for e in self.ENGS}
        for op in ops:
            if op.dma:
                i = qcount[op.eng]
                qcount[op.eng] += 1
                op.semi = i % N_DMA_SEMS
                op.semv = 16 * (i // N_DMA_SEMS + 1)
                if i >= N_DMA_SEMS:
                    op.deps.append(qhist[op.eng][i - N_DMA_SEMS])
                qhist[op.eng].append(op)
                op.sig = True
        for op in ops:
            for d in op.deps:
                d.sig = True
        ticks = {e: 0 for e in self.ENGS}
        for op in ops:
            if op.cc >= 0:
                op.sig = True
            elif op.sig and not op.dma:
                ticks[op.eng] += 1
                op.tick = ticks[op.eng]
        per_eng = {e: [o for o in ops if o.eng == e] for e in self.ENGS}
        with ExitStack() as st:
            csem = {e: st.enter_context(nc.semaphore("c_" + e)) for e in self.ENGS}
            dsem = {
                e: [st.enter_context(nc.semaphore("d_%s_%d" % (e, j))) for j in range(N_DMA_SEMS)]
                for e in self.ENGS
                if qcount[e] > 0
            }
            ccsem = [st.enter_context(nc.semaphore("cc_%d" % j)) for j in range(self.ncc)]
            block = st.enter_context(nc.Block())

            def run(engname, eng):
                known_c = {e: 0 for e in self.ENGS}
                known_d = {}
                for op in per_eng[engname]:
                    need_c = {}
                    need_d = {}
                    for d in op.deps:
                        if d.cc >= 0:
                            need_d[("cc", d.cc)] = 1
                        elif d.dma:
                            key = (d.eng, d.semi)
                            if d.semv > need_d.get(key, 0):
                                need_d[key] = d.semv
                        else:
                            if d.tick > need_c.get(d.eng, 0):
                                need_c[d.eng] = d.tick
                    for e2, tk in need_c.items():
                        if tk > known_c[e2]:
                            eng.wait_ge(csem[e2], tk)
                            known_c[e2] = tk
                    for key, sv in need_d.items():
                        if sv > known_d.get(key, 0):
                            if key[0] == "cc":
                                eng.wait_ge(ccsem[key[1]], 1)
                            else:
                                eng.wait_ge(dsem[key[0]][key[1]], sv)
                            known_d[key] = sv
                    ins = op.fn(eng)
                    if op.sig:
                        if op.cc >= 0:
                            ins.then_inc(ccsem[op.cc])
                        elif op.dma:
                            ins.then_inc(dsem[op.eng][op.semi], 16)
                        else:
                            ins.then_inc(csem[op.eng], 1)

            if per_eng["pe"]:
                @block.tensor
                def _(e):
                    run("pe", e)
            if per_eng["act"]:
                @block.scalar
                def _(e):
                    run("act", e)
            if per_eng["dve"]:
                @block.vector
                def _(e):
                    run("dve", e)
            if per_eng["pool"]:
                @block.gpsimd
                def _(e):
                    run("pool", e)
            if per_eng["sp"]:
                @block.sync
                def _(e):
                    run("sp", e)


class Arena:
    def __init__(self, nc, nbytes):
        self.t = nc.alloc_sbuf_tensor("arena", [128, nbytes // 4], F32)
        self.cap = nbytes
        self.top = 0

    def alloc(self, free_shape, dt):
        n = 1
        for s in free_shape:
            n *= s
        nb = n * (2 if dt == BF16 else 4)
        nb = (nb + 31) // 32 * 32
        off = self.top
        assert off + nb <= self.cap, ("arena overflow", off, nb, self.cap)
        self.top = off + nb
        v = self.t[:, off // 4:(off + nb) // 4]
        if dt == BF16:
            v = v.bitcast(BF16)
        v = v[:, 0:n]
        if len(free_shape) == 2:
            v = v.rearrange("p (a b) -> p a b", a=free_shape[0])
        elif len(free_shape) == 3:
            v = v.rearrange("p (a b c) -> p a b c", a=free_shape[0], b=free_shape[1])
        return v


def build_nc(nsteps=None, dbg=False, mstop=99, fstop=0):
    nc = bass.Bass("TRN2", target_bir_lowering=False)
    S = Sched()

    def din(name, shape):
        return nc.dram_tensor(name, list(shape), F32, kind="ExternalInput").ap()

    def dout(name, shape):
        return nc.dram_tensor(name, list(shape), F32, kind="ExternalOutput").ap()

    def dscr(name, shape):
        return nc.dram_tensor(name, list(shape), F32).ap()

    xp = din("xp", (2048, D))
    xs = din("xs", (128, D))
    caches = [din("c128", (16, 128, 2, 256)), din("c512", (16, 512, 2, 256)), din("c2048", (16, 2048, 2, 256))]
    sgla = din("sgla", (16, 4, 128, 256))
    w_ffn = [(din("f1g", (D, DFF)), din("f1u", (D, DFF)), din("f1d", (DFF, D))),
             (din("f2g", (D, DFF)), din("f2u", (D, DFF)), din("f2d", (DFF, D)))]
    w_in = din("w_in", (D, 7440))
    w_a_out = din("w_a_out", (256, D))
    w_b_out = din("w_b_out", (D, D))
    w_out = din("w_out", (D, D))
    g_up = din("g_up", (16, 512))
    c_ident = din("c_ident", (128, 128))
    c_tri2 = din("c_tri2", (128, 256))
    c_causal = din("c_causal", (128, 128))
    c_bd8 = din("c_bd8", (128, 128))
    c_mnew = din("c_mnew", (128, 3, 128))
    c_mc = din("c_mc", (128, 13, 32))
    c_blk = din("c_blk", (128, 16))
    c_onespad = din("c_onespad", (128, 2, 128))
    c_gnorm = din("c_gnorm", (128, 3, D))
    c_gqk = din("c_gqk", (128, 3, 512))
    c_gout = din("c_gout", (128, 2))
    c_abias = din("c_abias", (128, 4))

    yp = dout("yp", (2048, D))
    ys = dout("ys", (128, D))
    kvp = [dout("kv128p", (128, 2, 256)), dout("kv512p", (512, 2, 256)), dout("kv2048p", (2048, 2, 256))]
    kvs = [dout("kv128s", (128, 2, 256)), dout("kv512s", (128, 2, 256)), dout("kv2048s", (128, 2, 256))]
    glap = dout("glap", (4, 128, 256))
    glas = dout("glas", (16, 4, 128, 256))
    x1d = (dout if dbg else dscr)("x1d", (2176, D))
    x2d = (dout if dbg else dscr)("x2d", (2176, D))
    cinK = [dscr("cinK%d" % g, (128, 128 * DIL[g])) for g in range(3)]
    coutK = [dscr("coutK%d" % g, (256, 128 * DIL[g])) for g in range(3)]
    cinV = [dscr("cinV%d" % g, (128, 256 * DIL[g])) for g in range(3)]
    coutV = [dscr("coutV%d" % g, (256, 256 * DIL[g])) for g in range(3)]
    sin_d = dscr("sin_d", (128, 1024))
    sout_d = dscr("sout_d", (256, 1024))
    c_flag = din("c_flag", (128, 1))
    if dbg:
        dbgA = dout("dbgA", (128, 2, 2176))
        dbgO = dout("dbgO", (128, 8, 2176))
        dbgQ = dout("dbgQ", (128, 4, 2176))
        dbgV = dout("dbgV", (128, 17, 512))

    def sbt(name, shape, dt=F32):
        return nc.alloc_sbuf_tensor(name, list(shape), dt)

    ident = sbt("ident", (128, 128), BF16)
    tri2 = sbt("tri2", (128, 256), BF16)
    causal = sbt("causal", (128, 128), BF16)
    bd8 = sbt("bd8", (128, 128), BF16)
    mnew = sbt("mnew", (128, 3, 128), BF16)
    mc = sbt("mc", (128, 13, 32), BF16)
    blk = sbt("blk", (128, 16), BF16)
    onespad = sbt("onespad", (128, 2, 128), BF16)
    gqk = sbt("gqk", (128, 3, 512))
    gout = sbt("gout", (128, 2))
    nabias = sbt("nabias", (128, 4))
    ones256 = sbt("ones256", (128, 128))
    one1 = sbt("one1", (128, 1))
    ones512 = sbt("ones512", (128, 512))
    onesfull = sbt("onesfull", (128, 128), BF16)
    epsb = sbt("epsb", (128, 1))
    lnsc = sbt("lnsc", (128, 1))
    gup = sbt("gup", (16, 512))
    Sst = sbt("Sst", (128, 4, 256))
    Sbf = sbt("Sbf", (128, 4, 256), BF16)
    xnT = sbt("xnT", (128, 8, 2176), BF16)
    flag = sbt("flag", (128, 1))
    ctxm = sbt("ctxm", (128, 128), BF16)
    SAf = sbt("SAf", (128, 4, 256))
    SAb = sbt("SAb", (128, 4, 256), BF16)
    B_const = Buf("const")
    B_S = [Buf("S%d" % h) for h in range(4)]
    B_Sb = [Buf("Sb%d" % h) for h in range(4)]
    B_xnT = [Buf("xnT%d" % t) for t in range(17)]
    B_KTc = [Buf() for _ in range(3)]
    B_Vcx = [Buf() for _ in range(3)]

    arena = Arena(nc, int(nc.sbuf_bytes_remaining) - 2048)

    PB = [nc.alloc_psum_tensor("pb%d" % i, [128, 512], F32) for i in range(6)]
    PT = [nc.alloc_psum_tensor("pt%d" % i, [128, 1024], BF16) for i in range(2)]
    B_PB = [Buf("pb%d" % i, excl=True) for i in range(6)]
    B_PT = [Buf("pt%d" % i, excl=True) for i in range(2)]
    rr = {"pb": 0, "pt": 0}

    def bank():
        i = rr["pb"]
        rr["pb"] = (i + 1) % 4
        return PB[i], B_PB[i]

    def rbank(i):
        return PB[4 + i], B_PB[4 + i]

    PT32 = [PT[i][:, :].bitcast(F32) for i in range(2)]
    ALL8 = [(PB[i][:, :], B_PB[i]) for i in range(6)] + [(PT32[i], B_PT[i]) for i in range(2)]
    rr["b8"] = 0

    def bank8():
        i = rr["b8"]
        rr["b8"] = (i + 1) % 8
        return ALL8[i]

    def tbank():
        i = rr["pt"]
        rr["pt"] = (i + 1) % 2
        return PT[i], B_PT[i]

    def MM(out, lhsT, rhs, start, stop, reads, writes):
        S.pe(lambda e: e.matmul(out, lhsT=lhsT, rhs=rhs, start=start, stop=stop, skip_group_check=True), reads, writes)

    def TR(out, in_, reads, writes):
        S.pe(lambda e: e.transpose(out, in_, ident[:]), list(reads) + [B_const], writes)

    def ACT(out, in_, func, reads, writes, bias=0.0, scale=1.0, accum_out=None):
        if accum_out is None:
            S.act(lambda e: e.activation(out=out, in_=in_, func=func, bias=bias, scale=scale), reads, writes)
        else:
            S.act(lambda e: e.activation(out=out, in_=in_, func=func, bias=bias, scale=scale, accum_out=accum_out), reads, writes)

    def TT(eng, out, in0, in1, op, reads, writes):
        S.add(eng, lambda e: e.tensor_tensor(out=out, in0=in0, in1=in1, op=op), reads, writes)

    def STT(eng, out, in0, scalar, in1, op0, op1, reads, writes):
        S.add(eng, lambda e: e.scalar_tensor_tensor(out=out, in0=in0, scalar=scalar, in1=in1, op0=op0, op1=op1), reads, writes)

    def TS(eng, out, in0, s1, op0, reads, writes):
        S.add(eng, lambda e: e.tensor_scalar(out=out, in0=in0, scalar1=s1, scalar2=None, op0=op0), reads, writes)

    def CP(eng, out, in_, reads, writes):
        if eng == "act":
            S.act(lambda e: e.copy(out=out, in_=in_), reads, writes)
        else:
            S.add(eng, lambda e: e.tensor_copy(out=out, in_=in_), reads, writes)

    def RCP(out, in_, reads, writes):
        S.dve(lambda e: e.reciprocal(out=out, in_=in_), reads, writes)

    def MSET(eng, ap, val, writes):
        S.add(eng, lambda e: e.memset(ap, val), (), writes)

    def DMA(q, out, in_, reads, writes):
        S.dma(q, lambda e: e.dma_start(out=out, in_=in_), reads, writes)

    for dst, src in ((ident, c_ident), (causal, c_causal), (bd8, c_bd8), (mnew, c_mnew),
                     (mc, c_mc), (blk, c_blk), (onespad, c_onespad)):
        DMA("pool", dst[:], src, (), [Buf()])
    B_nab, B_flag, B_tri = Buf(), Buf(), Buf()
    DMA("sp", gqk[:], c_gqk, (), [Buf()])
    DMA("sp", gout[:], c_gout, (), [Buf()])
    DMA("sp", nabias[:], c_abias, (), [B_nab])
    DMA("sp", gup[:], g_up, (), [Buf()])
    DMA("sp", flag[:], c_flag, (), [B_flag])
    DMA("pool", tri2[:], c_tri2, (), [B_tri])
    S.dve(lambda e: e.tensor_scalar(out=nabias[:], in0=nabias[:], scalar1=-1.0, scalar2=None, op0=ALU.mult), [B_nab], [B_nab])
    MSET("dve", ones256[:], 1.0 / 256.0, [B_const])
    MSET("dve", one1[:], 1.0, [B_const])
    MSET("dve", ones512[:], 1.0, [B_const])
    MSET("dve", onesfull[:], 1.0, [B_const])
    MSET("dve", epsb[:], EPS, [B_const])
    MSET("dve", lnsc[:], math.log(128.0 ** -0.5), [B_const])
    S.dve(lambda e: e.tensor_scalar(out=ctxm[:], in0=tri2[:, 128:256], scalar1=flag[:, 0:1], scalar2=None, op0=ALU.mult), [B_flag, B_tri], [Buf()])
    MSET("dve", Sst[:], 0.0, B_S)
    MSET("dve", Sbf[:], 0.0, B_Sb)
    S.barrier()

    def wload(dst, w, c0, ncol, kc_n, writes, q="pool"):
        DMA(q, dst, w.rearrange("(kc p) n -> p kc n", p=128)[:, 0:kc_n, c0:c0 + ncol], (), writes)

    def row0(ps, t):
        return 2048 if t == 16 else 128 * t

    def groups(ntl):
        g = [(0, 4), (4, 4), (8, 4), (12, 4)]
        if ntl == 17:
            g.append((16, 1))
        return g

    def mpipe(n, stages, lags):
        for step in range(n + lags[-1]):
            for k in reversed(range(len(stages))):
                t = step - lags[k]
                if 0 <= t < n:
                    stages[k](t)

    def phase_norm(ps, ntl, src, gi):
        m = arena.top
        gain = arena.alloc((D,), F32)
        B_g = Buf()
        DMA("sp", gain, c_gnorm[:, gi, :], (), [B_g])
        NB = 6
        xt = [arena.alloc((D,), F32) for _ in range(NB)]
        B_xt = [Buf() for _ in range(NB)]
        junk = arena.alloc((D,), BF16)
        B_junk = Buf()
        xnb = [arena.alloc((D,), BF16) for _ in range(NB)]
        B_xnb = [Buf() for _ in range(NB)]
        st = [arena.alloc((4,), F32) for _ in range(NB)]
        B_st = [Buf() for _ in range(NB)]
        pts = {}

        def n0(t):
            i = t % NB
            DMA("sp", xt[i], src(ps, t), (), [B_xt[i]])
            MSET("pool", st[i][:, 0:1], 0.0, [B_st[i]])
            ACT(junk, xt[i], AF.Square, [B_xt[i]], [B_junk, B_st[i]], accum_out=st[i][:, 0:1])

        def n1(t):
            i = t % NB
            ACT(st[i][:, 1:2], st[i][:, 0:1], AF.Sqrt, [B_st[i], B_const], [B_st[i]], bias=epsb[:], scale=1.0 / D)

        def n2(t):
            i = t % NB
            RCP(st[i][:, 2:3], st[i][:, 1:2], [B_st[i]], [B_st[i]])
            STT("dve", xnb[i], xt[i], st[i][:, 2:3], gain, ALU.mult, ALU.mult, [B_xt[i], B_st[i], B_g], [B_xnb[i]])

        def n3(t):
            i = t % NB
            pt, bpt = tbank()
            for kc in range(8):
                TR(pt[:, kc * 128:(kc + 1) * 128], xnb[i][:, kc * 128:(kc + 1) * 128], [B_xnb[i]], [bpt])
            pts[t] = (pt, bpt)

        def n4(t):
            pt, bpt = pts.pop(t)
            CP("act" if t % 2 == 0 else "dve", xnT[:, :, t * 128:(t + 1) * 128],
               pt[:, 0:1024].rearrange("p (k n) -> p k n", k=8), [bpt], [B_xnT[t]])

        mpipe(ntl, [n0, n1, n2, n3, n4], [0, 1, 2, 3, 4])
        arena.top = m

    def phase_ffn(ps, ntl, wi, src, dst):
        Wg, Wu, Wd = w_ffn[wi]
        m = arena.top
        NT = ntl * 128
        hT = arena.alloc((11, 2176), BF16)
        B_h = [[Buf() for _ in range(5)] for _ in range(11)]
        wg = [arena.alloc((8, 128), BF16) for _ in range(2)]
        wu = [arena.alloc((8, 128), BF16) for _ in range(2)]
        B_wg = [Buf() for _ in range(2)]
        B_wu = [Buf() for _ in range(2)]
        wd = [arena.alloc((11, 512), BF16) for _ in range(2)]
        B_wd = [Buf() for _ in range(2)]
        sg = [arena.alloc((512,), F32) for _ in range(2)]
        B_sg = [Buf() for _ in range(2)]
        xio = [arena.alloc((512,), F32) for _ in range(6)]
        B_xio = [Buf() for _ in range(6)]
        B_dst = [[Buf() for _ in range(2)] for _ in range(17)]
        cnt = 0
        for fg in range(2):
            for f in range(11):
                fc = fg * 11 + f
                wi_ = fc % 2
                wload(wg[wi_], Wg, fc * 128, 128, 8, [B_wg[wi_]])
                wload(wu[wi_], Wu, fc * 128, 128, 8, [B_wu[wi_]])
                for gidx, (t0, n) in enumerate(groups(ntl)):
                    N = n * 128
                    rd = [B_xnT[t0 + j] for j in range(n)]
                    pg, bpg = bank8()
                    for kc in range(8):
                        MM(pg[:, 0:N], wg[wi_][:, kc, :], xnT[:, kc, t0 * 128:t0 * 128 + N], kc == 0, kc == 7, rd + [B_wg[wi_]], [bpg])
                    pu, bpu = bank8()
                    for kc in range(8):
                        MM(pu[:, 0:N], wu[wi_][:, kc, :], xnT[:, kc, t0 * 128:t0 * 128 + N], kc == 0, kc == 7, rd + [B_wu[wi_]], [bpu])
                    si = cnt % 2
                    cnt += 1
                    ACT(sg[si][:, 0:N], pg[:, 0:N], AF.Silu, [bpg], [B_sg[si]])
                    TT("dve", hT[:, f, t0 * 128:t0 * 128 + N], sg[si][:, 0:N], pu[:, 0:N], ALU.mult, [B_sg[si], bpu], [B_h[f][gidx]])
            for ch in range(2):
                wdi = ch
                DMA("pool", wd[wdi], Wd.rearrange("(fc p) n -> p fc n", p=128)[:, fg * 11:(fg + 1) * 11, ch * 512:(ch + 1) * 512], (), [B_wd[wdi]])
                for t in range(ntl):
                    po, bpo = bank8()
                    gidx = t // 4
                    for f in range(11):
                        MM(po[:, :], hT[:, f, t * 128:(t + 1) * 128], wd[wdi][:, f, :], f == 0, f == 10, [B_h[f][gidx], B_wd[wdi]], [bpo])
                    xi = cnt % 6
                    cnt += 1
                    s_ap = (src if fg == 0 else dst)(ps, t)[:, ch * 512:(ch + 1) * 512]
                    DMA("sp", xio[xi], s_ap, [B_dst[t][ch]] if fg == 1 else (), [B_xio[xi]])
                    STT("dve", xio[xi], po[:, :], 0.5, xio[xi], ALU.mult, ALU.add, [bpo, B_xio[xi]], [B_xio[xi]])
                    DMA("act", dst(ps, t)[:, ch * 512:(ch + 1) * 512], xio[xi], [B_xio[xi]], [B_dst[t][ch]])
        arena.top = m

    def phase_mixer(ps, ntl, src, dst):
        m0 = arena.top
        NT = ntl * 128
        samp = ntl == 17
        AT = arena.alloc((2, 2176), BF16)
        B_AT = Buf()
        mA = arena.top
        AN = arena.alloc((2, 2176), F32)
        AD = arena.alloc((2, 2176), F32)
        B_ANall = Buf()
        m1 = arena.top
        def pipeline(items, lag):
            n = len(items)
            for i in range(n + lag):
                if i < n:
                    items[i][0]()
                if i >= lag:
                    items[i - lag][1]()

        RG = [[0, 1], [2, 3], [4, 5], [6, 7]]
        for g in range(3):
            arena.top = m1
            d = DIL[g]
            nb = 16 // d
            QKT = arena.alloc((4, 2176), BF16)
            B_QK = [Buf() for _ in range(17)]
            Vpad = arena.alloc((17, 4, 128), BF16)
            B_V = [Buf() for _ in range(17)]
            MSET("pool", Vpad, 0.0, B_V)
            mW = arena.top
            wq = arena.alloc((8, 256), BF16)
            wk = arena.alloc((8, 256), BF16)
            wv = arena.alloc((8, 256), BF16)
            B_wq, B_wk, B_wv = Buf(), Buf(), Buf()
            wload(wq, w_in, C_QA + g * 256, 256, 8, [B_wq])
            wload(wk, w_in, C_KA + g * 256, 256, 8, [B_wk])
            wload(wv, w_in, C_VA + g * 256, 256, 8, [B_wv])
            NBUF = 5
            sq = [arena.alloc((512,), F32) for _ in range(NBUF)]
            B_sq = [Buf() for _ in range(NBUF)]
            qkn = [arena.alloc((512,), F32) for _ in range(NBUF)]
            B_qkn = [Buf() for _ in range(NBUF)]
            qkb = [arena.alloc((512,), BF16) for _ in range(NBUF)]
            B_qkb = [Buf() for _ in range(NBUF)]
            st8 = [arena.alloc((24,), F32) for _ in range(NBUF)]
            B_st8 = [Buf() for _ in range(NBUF)]
            vf = [arena.alloc((256,), F32) for _ in range(2)]
            B_vf = [Buf() for _ in range(2)]
            pqs = {}
            pts = {}

            def q0(t):
                i = t % NBUF
                pqk, bp = bank()
                for kc in range(8):
                    MM(pqk[:, 0:256], xnT[:, kc, t * 128:(t + 1) * 128], wq[:, kc, :], kc == 0, kc == 7, [B_xnT[t], B_wq], [bp])
                for kc in range(8):
                    MM(pqk[:, 256:512], xnT[:, kc, t * 128:(t + 1) * 128], wk[:, kc, :], kc == 0, kc == 7, [B_xnT[t], B_wk], [bp])
                ACT(sq[i], pqk[:, :], AF.Square, [bp], [B_sq[i]])
                pqs[t] = (pqk, bp)

            def q1(t):
                i = t % NBUF
                S.dve(lambda e, o=st8[i][:, 0:8], a=sq[i].rearrange("p (h d) -> p h d", h=8): e.tensor_reduce(out=o, in_=a, axis=AX.X, op=ALU.add), [B_sq[i]], [B_st8[i]])

            def q2(t):
                i = t % NBUF
                ACT(st8[i][:, 8:16], st8[i][:, 0:8], AF.Sqrt, [B_st8[i], B_const], [B_st8[i]], bias=epsb[:], scale=1.0 / 64)

            def q3(t):
                i = t % NBUF
                pqk, bp = pqs.pop(t)
                RCP(st8[i][:, 16:24], st8[i][:, 8:16], [B_st8[i]], [B_st8[i]])
                TT("dve", qkn[i].rearrange("p (h d) -> p h d", h=8), pqk[:, :].rearrange("p (h d) -> p h d", h=8),
                   st8[i][:, 16:24].unsqueeze(2).broadcast_to([128, 8, 64]), ALU.mult, [bp, B_st8[i]], [B_qkn[i]])

            def q4(t):
                i = t % NBUF
                TT("pool", qkn[i], qkn[i], gqk[:, g, :], ALU.mult, [B_qkn[i], B_const], [B_qkn[i]])

            def q5(t):
                i = t % NBUF
                CP("act", qkb[i], qkn[i], [B_qkn[i]], [B_qkb[i]])
                if t == 16:
                    DMA("sp", kvs[g][:, 0, :], qkn[i][:, 256:512], [B_qkn[i]], ())
                elif t >= 16 - d:
                    r0 = 128 * t - (2048 - 128 * d)
                    DMA("sp", kvp[g][r0:r0 + 128, 0, :], qkn[i][:, 256:512], [B_qkn[i]], ())

            def q6(t):
                i = t % NBUF
                pt, bpt = tbank()
                for j in range(4):
                    TR(pt[:, j * 128:(j + 1) * 128], qkb[i][:, j * 128:(j + 1) * 128], [B_qkb[i]], [bpt])
                pts[t] = (pt, bpt)

            def q7(t):
                pt, bpt = pts.pop(t)
                CP("dve", QKT[:, :, t * 128:(t + 1) * 128], pt[:, 0:512].rearrange("p (k n) -> p k n", k=4), [bpt], [B_QK[t]])

            mpipe(ntl, [q0, q1, q2, q3, q4, q5, q6, q7], [0, 1, 2, 3, 4, 5, 6, 7])

            def vtile(ti, lhs_of_kc, rd, out_ap):
                i = ti % 2
                pv, bp = bank()
                for kc in range(8):
                    MM(pv[:, 0:256], lhs_of_kc(kc), wv[:, kc, :], kc == 0, kc == 7, rd + [B_wv], [bp])
                if out_ap is not None:
                    CP("act", vf[i], pv[:, 0:256], [bp], [B_vf[i]])
                    DMA("sp", out_ap, vf[i], [B_vf[i]], ())
                    pv3 = vf[i].rearrange("p (h d) -> p h d", h=4)
                    rdv = [B_vf[i]]
                else:
                    pv3 = pv[:, 0:256].rearrange("p (h d) -> p h d", h=4)
                    rdv = [bp]
                CP("dve", Vpad[:, ti, 0:4:2, 0:64], pv3[:, 0:4:2, :], rdv, [B_V[ti]])
                CP("dve", Vpad[:, ti, 1:4:2, 64:128], pv3[:, 1:4:2, :], rdv, [B_V[ti]])

            for r in range(d):
                for ib in range(nb):
                    ti = r * nb + ib
                    lo = r + d * 128 * ib
                    hi = r + d * 128 * (ib + 1)
                    out_ap = None
                    if ib == nb - 1:
                        out_ap = kvp[g].rearrange("(j dd) two c -> dd j two c", dd=d)[r, :, 1, :]
                    vtile(ti, lambda kc, lo=lo, hi=hi: xnT[:, kc, lo:hi:d], [B_xnT[t] for t in range(16)], out_ap)
            if samp:
                vtile(16, lambda kc: xnT[:, kc, 2048:2176], [B_xnT[16]], kvs[g][:, 1, :])
            S.barrier()
            arena.top = mW
            KTcg = arena.alloc((2, 128 * d), BF16)
            Vcxg = arena.alloc((d, 4, 128), BF16)
            B_cinK, B_cinV, B_coutK, B_coutV = Buf(), Buf(), Buf(), Buf()
            DMA("sp", cinK[g].rearrange("p (a n) -> p a n", a=2),
                QKT[:, 2:4, 2048 - 128 * d:2048].bitcast(F32), B_QK[0:16], [B_cinK])
            DMA("sp", cinV[g].rearrange("p (r n) -> p r n", r=d),
                Vpad[:, 0:16, :, :].rearrange("p (r i) h e -> p r i (h e)", i=nb)[:, :, nb - 1, :].bitcast(F32), B_V[0:16], [B_cinV])
            S.coll(lambda e, a=cinK[g], b=coutK[g]: e.collective_compute("AllGather", ALU.bypass, replica_groups=RG,
                                                                         ins=[a.opt()], outs=[b.opt()]), [B_cinK], [B_coutK])
            S.coll(lambda e, a=cinV[g], b=coutV[g]: e.collective_compute("AllGather", ALU.bypass, replica_groups=RG,
                                                                         ins=[a.opt()], outs=[b.opt()]), [B_cinV], [B_coutV])
            DMA("sp", KTcg.bitcast(F32), coutK[g][0:128, :].rearrange("p (a n) -> p a n", a=2), [B_coutK], [B_KTc[g]])
            DMA("sp", Vcxg.rearrange("p r h e -> p r (h e)").bitcast(F32), coutV[g][0:128, :].rearrange("p (r n) -> p r n", r=d), [B_coutV], [B_Vcx[g]])
            LAG = 3
            NPB = LAG + 2
            ptb = [arena.alloc((256,), BF16) for _ in range(NPB)]
            B_ptb = [Buf() for _ in range(NPB)]
            ptm = [arena.alloc((256,), BF16) for _ in range(NPB)]
            B_ptm = [Buf() for _ in range(NPB)]
            allQK = B_QK[0:16]
            items = []
            state = {"cnt": 0, "b3": 0}

            def bank3():
                i = state["b3"]
                state["b3"] = (i + 1) % 4
                return PB[i], B_PB[i]
            for hp in range(2):
                for r in range(d):
                    accs = {}
                    started = {}
                    kbs = [-1] + list(range(nb))
                    for kb in kbs:
                        for hh in range(2):
                            def mkA(hp=hp, r=r, kb=kb, hh=hh, slot=len(items) % NPB):
                                nq = 1 if (kb == -1 or kb == nb - 1) else 2
                                N = 128 * nq
                                qlo = r + d * 128 * max(kb, 0)
                                p0 = hh * 64
                                pss, bps = bank3()
                                if kb == -1:
                                    kap = KTcg[p0:p0 + 64, hp, r:128 * d:d]
                                    krd = [B_KTc[g]]
                                    msk = ctxm[:]
                                else:
                                    kap = QKT[p0:p0 + 64, 2 + hp, qlo:qlo + d * 128:d]
                                    krd = allQK
                                    msk = tri2[:, 0:N]
                                MM(pss[:, 0:N], kap, QKT[p0:p0 + 64, hp, qlo:qlo + d * N:d], True, True, krd + allQK, [bps])
                                ACT(ptb[slot][:, 0:N], pss[:, 0:N], AF.Exp, [bps], [B_ptb[slot]], scale=0.125)
                                state["cnt"] += 1
                                TT("pool" if state["cnt"] % 2 else "dve", ptm[slot][:, 0:N], ptb[slot][:, 0:N], msk, ALU.mult,
                                   [B_ptb[slot], B_const], [B_ptm[slot]])

                            def mkB(hp=hp, r=r, kb=kb, hh=hh, slot=len(items) % NPB, accs=accs, started=started):
                                nq = 1 if (kb == -1 or kb == nb - 1) else 2
                                h = 2 * hp + hh
                                if kb == -1:
                                    vap = Vcxg[:, r, h, :]
                                    vrd = [B_Vcx[g]]
                                else:
                                    vap = Vpad[:, r * nb + kb, h, :]
                                    vrd = [B_V[r * nb + kb]]
                                targets = [(max(kb, 0), 0)]
                                if nq == 2:
                                    targets.append((kb + 1, 1))
                                for (ib, half) in targets:
                                    if ib not in accs:
                                        accs[ib] = rbank(ib % 2)
                                        started[ib] = False
                                    pa, bpa = accs[ib]
                                    last = (kb == ib) and hh == 1
                                    rhs = ptm[slot][:, half * 128:(half + 1) * 128]
                                    MM(pa[:, 0:128], vap, rhs, not started[ib], last, vrd + [B_ptm[slot]], [bpa])
                                    started[ib] = True
                                    MM(pa[:, 128:256], onespad[:, hh, :], rhs, False, last, [B_const, B_ptm[slot]], [bpa])
                                if kb >= 0 and hh == 1:
                                    pa, bpa = accs.pop(kb)
                                    lo = r + d * 128 * kb
                                    hi = lo + d * 128
                                    if g == 0:
                                        CP("act", AN[:, hp, lo:hi:d], pa[:, 0:128], [bpa], [B_ANall])
                                        CP("act", AD[:, hp, lo:hi:d], pa[:, 128:256], [bpa], [B_ANall])
                                    else:
                                        TT("dve", AN[:, hp, lo:hi:d], AN[:, hp, lo:hi:d], pa[:, 0:128], ALU.add, [bpa, B_ANall], [B_ANall])
                                        TT("dve", AD[:, hp, lo:hi:d], AD[:, hp, lo:hi:d], pa[:, 128:256], ALU.add, [bpa, B_ANall], [B_ANall])
                            items.append((mkA, mkB))
            band_thunks = []
            for i in range(len(items) + LAG):
                if i < len(items):
                    band_thunks.append(items[i][0])
                if i >= LAG:
                    band_thunks.append(items[i - LAG][1])
            samp_thunks = []
            if samp:
                nres = (1, 4, 8)[g]
                mbase = (0, 1, 5)[g]
                W = nres * 32
                Kc = [arena.alloc((nres, 256), BF16) for _ in range(2)]
                B_Kc = [Buf() for _ in range(2)]
                Vc = [arena.alloc((nres, 4, 128), BF16) for _ in range(2)]
                B_Vc = [[Buf() for _ in range(4)] for _ in range(2)]
                for vi in range(2):
                    MSET("pool", Vc[vi], 0.0, B_Vc[vi])
                KcT = [arena.alloc((nres, 2, 128), BF16) for _ in range(2)]
                B_KcT = [Buf() for _ in range(2)]
                pbS = [arena.alloc((256,), BF16) for _ in range(2)]
                B_pbS = [Buf() for _ in range(2)]
                pmS = [arena.alloc((256,), BF16) for _ in range(2)]
                B_pmS = [Buf() for _ in range(2)]
                pnS = [arena.alloc((128,), BF16) for _ in range(2)]
                B_pnS = [Buf() for _ in range(2)]
                sacc_t, bsacc = PT32[1], B_PT[1]
                sacc4 = sacc_t.rearrange("p (a b n) -> p a b n", a=2, b=2)
                for h in range(4):
                    hp, hh = h // 2, h % 2
                    p0 = hh * 64
                    pss, bps = bank3()
                    MM(pss[:, 0:128], QKT[p0:p0 + 64, 2 + hp, 2048:2176], QKT[p0:p0 + 64, hp, 2048:2176], True, True, [B_QK[16]], [bps])
                    bi = h % 2
                    ACT(pbS[bi][:, 0:128], pss[:, 0:128], AF.Exp, [bps], [B_pbS[bi]], scale=0.125)
                    TT("dve", pnS[bi], pbS[bi][:, 0:128], mnew[:, g, :], ALU.mult, [B_pbS[bi], B_const], [B_pnS[bi]])
                    MM(sacc4[:, hp, 0, :], Vpad[:, 16, h, :], pnS[bi], h == 0, False, [B_V[16], B_pnS[bi]], [bsacc])
                    MM(sacc4[:, hp, 1, :], onespad[:, hh, :], pnS[bi], False, False, [B_const, B_pnS[bi]], [bsacc])
                cch = caches[g]

                def sL(b):
                    ci = b % 2
                    cv = cch[b].rearrange("(k dd) two c -> k dd two c", dd=d)
                    DMA("pool", Kc[ci], cv[:, 0:nres, 0, :], (), [B_Kc[ci]])
                    for h in range(4):
                        o64 = (h % 2) * 64
                        DMA("pool", Vc[ci][:, :, h, o64:o64 + 64], cv[:, 0:nres, 1, h * 64:(h + 1) * 64], (), [B_Vc[ci][h]])

                def sA(b):
                    ci = b % 2
                    for r0 in range(0, nres, 4):
                        nr = min(4, nres - r0)
                        pt, bpt = PT[0], B_PT[0]
                        for rr_ in range(nr):
                            for j in range(2):
                                TR(pt[:, (2 * rr_ + j) * 128:(2 * rr_ + j + 1) * 128], Kc[ci][:, r0 + rr_, j * 128:(j + 1) * 128], [B_Kc[ci]], [bpt])
                        CP("act", KcT[ci][:, r0:r0 + nr, :, :], pt[:, 0:nr * 256].rearrange("p (r k n) -> p r k n", r=nr, k=2), [bpt], [B_KcT[ci]])
                    pss, bps = bank3()
                    for rr_ in range(nres):
                        for h in range(4):
                            p0 = (h % 2) * 64
                            MM(pss[:, rr_ * 32 + h * 8:rr_ * 32 + h * 8 + 8], KcT[ci][p0:p0 + 64, rr_, h // 2, :],
                               QKT[p0:p0 + 64, h // 2, 2048 + 8 * b:2048 + 8 * b + 8], True, True, [B_KcT[ci], B_QK[16]], [bps])
                    ACT(pbS[ci][:, 0:W], pss[:, 0:W], AF.Exp, [bps], [B_pbS[ci]], scale=0.125)
                    TT("dve", pmS[ci][:, 0:W], pbS[ci][:, 0:W], mc[:, mbase:mbase + nres, :].rearrange("p r n -> p (r n)"), ALU.mult,
                       [B_pbS[ci], B_const], [B_pmS[ci]])

                def sB(b):
                    ci = b % 2
                    for rr_ in range(nres):
                        for h in range(4):
                            rhs = pmS[ci][:, rr_ * 32 + h * 8:rr_ * 32 + h * 8 + 8]
                            MM(sacc4[:, h // 2, 0, 8 * b:8 * b + 8], Vc[ci][:, rr_, h, :], rhs, False, False, [B_Vc[ci][h], B_pmS[ci]], [bsacc])
                            MM(sacc4[:, h // 2, 1, 8 * b:8 * b + 8], onespad[:, h % 2, :], rhs, False, False, [B_const, B_pmS[ci]], [bsacc])

                sL(0)
                for st_ in range(17):
                    def thunk(st_=st_):
                        if st_ >= 1:
                            sB(st_ - 1)
                        if st_ + 1 < 16:
                            sL(st_ + 1)
                        if st_ < 16:
                            sA(st_)
                    samp_thunks.append(thunk)
            nbt, nst = len(band_thunks), len(samp_thunks)
            si_ = 0
            for bi_, th in enumerate(band_thunks):
                th()
                while si_ < nst and (si_ + 1) * nbt <= (bi_ + 1) * nst:
                    samp_thunks[si_]()
                    si_ += 1
            while si_ < nst:
                samp_thunks[si_]()
                si_ += 1
            if samp:
                for hp in range(2):
                    if g == 0:
                        CP("act", AN[:, hp, 2048:2176], sacc4[:, hp, 0, :], [bsacc], [B_ANall])
                        CP("act", AD[:, hp, 2048:2176], sacc4[:, hp, 1, :], [bsacc], [B_ANall])
                    else:
                        TT("dve", AN[:, hp, 2048:2176], AN[:, hp, 2048:2176], sacc4[:, hp, 0, :], ALU.add, [bsacc, B_ANall], [B_ANall])
                        TT("dve", AD[:, hp, 2048:2176], AD[:, hp, 2048:2176], sacc4[:, hp, 1, :], ALU.add, [bsacc, B_ANall], [B_ANall])
            S.barrier()
        for hp in range(2):
            RCP(AD[:, hp, 0:NT], AD[:, hp, 0:NT], [B_ANall], [B_ANall])
            TT("dve", AT[:, hp, 0:NT], AN[:, hp, 0:NT], AD[:, hp, 0:NT], ALU.mult, [B_ANall], [B_AT])
        if dbg:
            DMA("pool", dbgA[:, :, 0:NT], AT[:, :, 0:NT], [B_AT], ())
        S.barrier()
        arena.top = mA
        if mstop == 4:
            arena.top = m0; return
        onT = arena.alloc((8, 2176), BF16)
        B_on = [Buf() for _ in range(8)]
        mO = arena.top

        lrT = arena.alloc((2176,), F32)
        B_lr = Buf()
        wlr = arena.alloc((8, 16), BF16)
        B_wlr = Buf()
        wload(wlr, w_in, C_LR, 16, 8, [B_wlr])
        for (t0, n) in groups(ntl):
            N = n * 128
            pl, bp = bank()
            for kc in range(8):
                MM(pl[0:16, 0:N], wlr[:, kc, :], xnT[:, kc, t0 * 128:t0 * 128 + N], kc == 0, kc == 7, [B_xnT[t0 + j] for j in range(n)] + [B_wlr], [bp])
            CP("act", lrT[0:16, t0 * 128:t0 * 128 + N], pl[0:16, 0:N], [bp], [B_lr])
        qgT = arena.alloc((4, 2048), BF16)
        B_qg = [Buf() for _ in range(4)]
        dgall = arena.alloc((8,), F32)
        B_dg = Buf()
        osq = [arena.alloc((128,), F32) for _ in range(2)]
        B_osq = [Buf() for _ in range(2)]
        rs = [arena.alloc((128,), F32) for _ in range(2)]
        B_rs = [Buf() for _ in range(2)]

        def out_norm(h, po, bpo, c0, ci):
            pm, bpm = bank()
            for j in range(2):
                ACT(osq[j], po[j][:, 0:128], AF.Square, [bpo[j]], [B_osq[j]])
                MM(pm[:, 0:128], ones256[:], osq[j], j == 0, j == 1, [B_const, B_osq[j]], [bpm])
            ri = ci % 2
            ACT(rs[ri], pm[:, 0:128], AF.Sqrt, [bpm, B_const], [B_rs[ri]], bias=epsb[:], scale=1.0)
            RCP(rs[ri], rs[ri], [B_rs[ri]], [B_rs[ri]])
            for j in range(2):
                STT("dve", onT[:, 2 * h + j, c0:c0 + 128], po[j][:, 0:128], gout[:, j:j + 1], rs[ri], ALU.mult, ALU.mult,
                    [bpo[j], B_const, B_rs[ri]], [B_on[2 * h + j]])

        m2 = arena.top
        if mstop == 5:
            S.barrier(); arena.top = m0; return
        _hc = {}

        def ha(name, shape, dt):
            if name not in _hc:
                _hc[name] = arena.alloc(shape, dt)
            return _hc[name]

        def hb(name):
            if name not in _hc:
                _hc[name] = Buf()
            return _hc[name]

        for h in range(4):
            pass
            wqg = ha("wq%d" % (h % 2), (8, 128), BF16)
            wkg = ha("wk%d" % (h % 2), (8, 128), BF16)
            wvg = ha("wv", (8, 256), BF16)
            B_wq, B_wk, B_wv = hb("bwq%d" % (h % 2)), hb("bwk%d" % (h % 2)), hb("bwv")

            def load_head(hh_):
                wload(ha("wq%d" % (hh_ % 2), (8, 128), BF16), w_in, C_QG + hh_ * 128, 128, 8, [hb("bwq%d" % (hh_ % 2))])
                wload(ha("wk%d" % (hh_ % 2), (8, 128), BF16), w_in, C_KG + hh_ * 128, 128, 8, [hb("bwk%d" % (hh_ % 2))])
                wload(ha("wv", (8, 256), BF16), w_in, C_VG + hh_ * 256, 256, 8, [hb("bwv")])

            if h == 0:
                load_head(0)
            spb = ha("a4", (2176,), F32)
            B_sp = hb("b2")
            Bc = ha("a5", (2176,), F32)
            B_Bc = hb("b3")
            e1 = [ha("a6_%d" % _i, (512,), F32) for _i in range(2)]
            B_e1 = [hb("b4_%d" % _i) for _i in range(2)]
            qeT = ha("a7", (2176,), BF16)
            keT = ha("a8", (2176,), BF16)
            B_qe = hb("b5")
            B_ke = hb("b6")
            vh = ha("a9", (17, 256), BF16)
            B_vh = [hb("b7_%d" % _i) for _i in range(17)]
            offs = ha("a10", (64,), F32)
            B_off = hb("b8")
            cnt = 0
            for (t0, n) in groups(ntl):
                N = n * 128
                c0 = t0 * 128
                pl, bp = bank()
                MM(pl[:, 0:N], gup[0:16, h * 128:(h + 1) * 128], lrT[0:16, c0:c0 + N], True, True, [B_const, B_lr], [bp])
                i = cnt % 2
                cnt += 1
                ACT(e1[i][:, 0:N], pl[:, 0:N], AF.Exp, [bp, B_const], [B_e1[i]], bias=nabias[:, h:h + 1], scale=-1.0)
                ACT(spb[:, c0:c0 + N], e1[i][:, 0:N], AF.Ln, [B_e1[i], B_const], [B_sp], bias=one1[:], scale=1.0)
            for t in range(ntl):
                pv, bp = bank()
                for kc in range(8):
                    MM(pv[:, 0:256], xnT[:, kc, t * 128:(t + 1) * 128], wvg[:, kc, :], kc == 0, kc == 7, [B_xnT[t], B_wv], [bp])
                CP("act", vh[:, t, :], pv[:, 0:256], [bp], [B_vh[t]])
            for pc in range(4):
                S.dve(lambda e, o=Bc[:, 512 * pc:512 * pc + 512], a=ones512[:], b=spb[:, 512 * pc:512 * pc + 512],
                      ini=(0.0 if pc == 0 else Bc[:, 512 * pc - 1:512 * pc]):
                      e.tensor_tensor_scan(out=o, data0=a, data1=b, initial=ini, op0=ALU.mult, op1=ALU.add), [B_sp, B_const, B_Bc], [B_Bc])
            MSET("dve", offs[:, 0:32], 0.0, [B_off])
            CP("dve", offs[:, 1:16], Bc[:, 127:1920:128], [B_Bc], [B_off])
            TT("dve", Bc[:, 0:2048].rearrange("p (c s) -> p c s", c=16), Bc[:, 0:2048].rearrange("p (c s) -> p c s", c=16),
               offs[:, 0:16].unsqueeze(2).broadcast_to([128, 16, 128]), ALU.subtract, [B_Bc, B_off], [B_Bc])
            ACT(offs[:, 32:48], Bc[:, 127:2048:128], AF.Exp, [B_Bc], [B_off], scale=-1.0 / 16)
            if samp:
                S.dve(lambda e, o=Bc[:, 2048:2176], a=ones512[:, 0:128], b=spb[:, 2048:2176]:
                      e.tensor_tensor_scan(out=o, data0=a, data1=b, initial=0.0, op0=ALU.mult, op1=ALU.add), [B_sp, B_const], [B_Bc])
                CP("dve", offs[:, 17:32], Bc[:, 2048 + 7:2048 + 120:8], [B_Bc], [B_off])
                TT("dve", Bc[:, 2048:2176].rearrange("p (c s) -> p c s", c=16), Bc[:, 2048:2176].rearrange("p (c s) -> p c s", c=16),
                   offs[:, 16:32].unsqueeze(2).broadcast_to([128, 16, 8]), ALU.subtract, [B_Bc, B_off], [B_Bc])
                ACT(offs[:, 48:64], Bc[:, 2048 + 7:2176:8], AF.Exp, [B_Bc], [B_off], scale=-1.0 / 16)
            for (t0, n) in groups(ntl):
                N = n * 128
                c0 = t0 * 128
                rd = [B_xnT[t0 + j] for j in range(n)]
                pq, bp = bank()
                for kc in range(8):
                    MM(pq[:, 0:N], wqg[:, kc, :], xnT[:, kc, c0:c0 + N], kc == 0, kc == 7, rd + [B_wq], [bp])
                i = cnt % 2
                cnt += 1
                ACT(e1[i][:, 0:N], Bc[:, c0:c0 + N], AF.Exp, [B_Bc, B_const], [B_e1[i]], bias=lnsc[:], scale=-1.0 / 16)
                TT("dve", qeT[:, c0:c0 + N], pq[:, 0:N], e1[i][:, 0:N], ALU.mult, [bp, B_e1[i]], [B_qe])
                pk, bp = bank()
                for kc in range(8):
                    MM(pk[:, 0:N], wkg[:, kc, :], xnT[:, kc, c0:c0 + N], kc == 0, kc == 7, rd + [B_wk], [bp])
                i = cnt % 2
                cnt += 1
                ACT(e1[i][:, 0:N], Bc[:, c0:c0 + N], AF.Exp, [B_Bc], [B_e1[i]], scale=1.0 / 16)
                TT("dve", keT[:, c0:c0 + N], pk[:, 0:N], e1[i][:, 0:N], ALU.mult, [bp, B_e1[i]], [B_ke])
            eoff = ha("a11", (16,), F32)
            B_eo = hb("b9")
            ACT(eoff, offs[:, 0:16], AF.Exp, [B_off], [B_eo], scale=-1.0 / 16)
            TT("dve", qgT[:, h, :].rearrange("p (c s) -> p c s", c=16), qeT[:, 0:2048].rearrange("p (c s) -> p c s", c=16),
               eoff.unsqueeze(2).broadcast_to([128, 16, 128]), ALU.mult, [B_qe, B_eo], [B_qg[h]])
            TT("dve", dgall[:, h:h + 1], eoff[:, 15:16], offs[:, 47:48], ALU.mult, [B_eo, B_off], [B_dg])
            if mstop == 6:
                S.barrier(); arena.top = m0; return
            attm = [ha("a12_%d" % _i, (128,), BF16) for _i in range(3)]
            B_att = [hb("b10_%d" % _i) for _i in range(3)]
            kdT = [ha("a13_%d" % _i, (128,), BF16) for _i in range(2)]
            B_kdT = [hb("b11_%d" % _i) for _i in range(2)]
            kd = [ha("a14_%d" % _i, (128,), BF16) for _i in range(2)]
            B_kd = [hb("b12_%d" % _i) for _i in range(2)]
            if h + 1 < 4:
                load_head(h + 1)
            kdTall = ha("a15", (2048,), BF16)
            B_kdTall = hb("b13")
            kdall = ha("a16", (16, 128), BF16)
            B_kdall = [hb("b14"), hb("b15")]
            TT("pool", kdTall.rearrange("p (c s) -> p c s", c=16), keT[:, 0:2048].rearrange("p (c s) -> p c s", c=16),
               offs[:, 32:48].unsqueeze(2).broadcast_to([128, 16, 128]), ALU.mult, [B_ke, B_off], [B_kdTall])
            for hf in range(2):
                pt, bpt = tbank()
                for cc in range(8):
                    TR(pt[:, cc * 128:(cc + 1) * 128], kdTall[:, (hf * 8 + cc) * 128:(hf * 8 + cc + 1) * 128], [B_kdTall], [bpt])
                CP("act", kdall[:, hf * 8:(hf + 1) * 8, :], pt[:, 0:1024].rearrange("p (c n) -> p c n", c=8), [bpt], [B_kdall[hf]])
            pus = {}

            def gA(c):
                c0 = c * 128
                ai = c % 3
                pa, bpa = bank()
                MM(pa[:, 0:128], keT[:, c0:c0 + 128], qeT[:, c0:c0 + 128], True, True, [B_ke, B_qe], [bpa])
                TT("dve", attm[ai], pa[:, 0:128], causal[:], ALU.mult, [bpa, B_const], [B_att[ai]])
                pu, bpu = bank()
                MM(pu[:, 0:256], kdall[:, c, :], vh[:, c, :], True, True, [B_kdall[c // 8], B_vh[c]], [bpu])
                pus[c] = (pu, bpu)

            def gB(c):
                c0 = c * 128
                ai = c % 3
                for j in range(2):
                    po, bpo = rbank(j)
                    MM(po[:, 0:128], vh[:, c, j * 128:(j + 1) * 128], attm[ai], True, False, [B_vh[c], B_att[ai]], [bpo])
                    MM(po[:, 0:128], Sbf[:, h, j * 128:(j + 1) * 128], qeT[:, c0:c0 + 128], False, True, [B_Sb[h], B_qe], [bpo])
                    CP("act", onT[:, 2 * h + j, c0:c0 + 128], po[:, 0:128], [bpo], [B_on[2 * h + j]])
                pu, bpu = pus.pop(c)
                STT("dve", Sst[:, h, :], Sst[:, h, :], offs[:, 32 + c:33 + c], pu[:, 0:256], ALU.mult, ALU.add, [B_S[h], B_off, bpu], [B_S[h]])
                CP("dve", Sbf[:, h, :], Sst[:, h, :], [B_S[h]], [B_Sb[h]])

            pipeline([(lambda c=c: gA(c), lambda c=c: gB(c)) for c in range(16)], 1)
            if samp:
                c0 = 2048
                s0 = [ha("a17_%d" % _i, (256,), F32) for _i in range(4)]
                B_s0 = [hb("b16_%d" % _i) for _i in range(4)]
                s0b = [ha("a18_%d" % _i, (256,), BF16) for _i in range(4)]
                B_s0b = [hb("b17_%d" % _i) for _i in range(4)]
                vblk = ha("a19", (16, 256), BF16)
                B_vblk = hb("b18")
                pa, bpa = bank()
                MM(pa[:, 0:128], keT[:, c0:c0 + 128], qeT[:, c0:c0 + 128], True, True, [B_ke, B_qe], [bpa])
                TT("dve", attm[0], pa[:, 0:128], bd8[:], ALU.mult, [bpa, B_const], [B_att[0]])
                po, bpo = [None, None], [None, None]
                for j in range(2):
                    po[j], bpo[j] = rbank(j)
                    MM(po[j][:, 0:128], vh[:, 16, j * 128:(j + 1) * 128], attm[0], True, False, [B_vh[16], B_att[0]], [bpo[j]])
                TT("pool", kdT[0].rearrange("p (b s) -> p b s", b=16), keT[:, c0:c0 + 128].rearrange("p (b s) -> p b s", b=16),
                   offs[:, 48:64].unsqueeze(2).broadcast_to([128, 16, 8]), ALU.mult, [B_ke, B_off], [B_kdT[0]])
                pt, bpt = tbank()
                TR(pt[:, 0:128], kdT[0], [B_kdT[0]], [bpt])
                CP("act", kd[0], pt[:, 0:128], [bpt], [B_kd[0]])
                TT("dve", vblk, vh[:, 16, :].unsqueeze(1).broadcast_to([128, 16, 256]), blk[:].unsqueeze(2).broadcast_to([128, 16, 256]),
                   ALU.mult, [B_vh[16], B_const], [B_vblk])
                for b2 in range(8):
                    pu, bpu = bank()
                    MM(pu[:, 0:512], kd[0], vblk[:, 2 * b2:2 * b2 + 2, :], True, True, [B_kd[0], B_vblk], [bpu])
                    for bb in range(2):
                        b = 2 * b2 + bb
                        si = b % 4
                        DMA("pool", s0[si], sgla[b, h], (), [B_s0[si]])
                        CP("act", s0b[si], s0[si], [B_s0[si]], [B_s0b[si]])
                        for j in range(2):
                            MM(po[j][:, 8 * b:8 * b + 8], s0b[si][:, j * 128:(j + 1) * 128], qeT[:, c0 + 8 * b:c0 + 8 * b + 8], False, False,
                               [B_s0b[si], B_qe], [bpo[j]])
                        STT("dve", s0[si], s0[si], offs[:, 48 + b:49 + b], pu[:, bb * 256:(bb + 1) * 256], ALU.mult, ALU.add,
                            [B_s0[si], B_off, bpu, B_s0b[si]], [B_s0[si]])
                        DMA("sp", glas[b, h], s0[si], [B_s0[si]], ())
                out_norm(h, po, bpo, c0, 0)
        arena.top = m2
        B_sin = Buf()
        B_sout = Buf()
        B_SA = Buf()
        DMA("sp", sin_d.rearrange("p (h n) -> p h n", h=4), Sst[:, :, :], B_S, [B_sin])
        S.coll(lambda e: e.collective_compute("AllGather", ALU.bypass, replica_groups=[[0, 1], [2, 3], [4, 5], [6, 7]],
                                              ins=[sin_d.opt()], outs=[sout_d.opt()]), [B_sin], [B_sout])
        DMA("sp", SAf[:, :, :], sout_d[0:128, :].rearrange("p (h n) -> p h n", h=4), [B_sout], [B_SA])
        TS("dve", SAf[:, :, :], SAf[:, :, :], flag[:, 0:1], ALU.mult, [B_SA, B_const], [B_SA])
        CP("act", SAb[:, :, :], SAf[:, :, :], [B_SA], [B_SA])
        fsq = [[arena.alloc((512,), F32) for _ in range(2)] for _ in range(3)]
        B_fsq = [[Buf() for _ in range(2)] for _ in range(3)]
        frs = [arena.alloc((512,), F32) for _ in range(3)]
        B_frs = [Buf() for _ in range(3)]
        fpo = {}
        fpm = {}

        def f0(i):
            h, q4 = i // 4, i % 4
            c0 = q4 * 512
            si = i % 3
            lst = []
            for j in range(2):
                po, bpo = bank8()
                MM(po[:, 0:512], ident[:], onT[:, 2 * h + j, c0:c0 + 512], True, False, [B_const, B_on[2 * h + j]], [bpo])
                MM(po[:, 0:512], SAb[:, h, j * 128:(j + 1) * 128], qgT[:, h, c0:c0 + 512], False, True, [B_SA, B_qg[h]], [bpo])
                ACT(fsq[si][j], po[:, 0:512], AF.Square, [bpo], [B_fsq[si][j]])
                lst.append((po, bpo))
            fpo[i] = lst

        def f1(i):
            si = i % 3
            pm, bpm = bank8()
            for j in range(2):
                MM(pm[:, 0:512], ones256[:], fsq[si][j], j == 0, j == 1, [B_const, B_fsq[si][j]], [bpm])
            ACT(frs[si], pm[:, 0:512], AF.Sqrt, [bpm, B_const], [B_frs[si]], bias=epsb[:], scale=1.0)

        def f2(i):
            h, q4 = i // 4, i % 4
            c0 = q4 * 512
            si = i % 3
            lst = fpo.pop(i)
            RCP(frs[si], frs[si], [B_frs[si]], [B_frs[si]])
            for j in range(2):
                po, bpo = lst[j]
                STT("dve", onT[:, 2 * h + j, c0:c0 + 512], po[:, 0:512], gout[:, j:j + 1], frs[si], ALU.mult, ALU.mult,
                    [bpo, B_const, B_frs[si]], [B_on[2 * h + j]])

        mpipe(16, [f0, f1, f2], [0, 1, 2])
        for h in range(4):
            STT("dve", Sst[:, h, :], SAf[:, h, :], dgall[:, h:h + 1], Sst[:, h, :], ALU.mult, ALU.add, [B_SA, B_dg, B_S[h]], [B_S[h]])
            DMA("sp", glap[h], Sst[:, h, :], [B_S[h]], ())
        if dbg:
            DMA("pool", dbgO[:, :, 0:NT], onT[:, :, 0:NT], B_on, ())
        if mstop == 8:
            S.barrier(); arena.top = m0; return

        S.barrier()
        arena.top = mO
        allg = groups(ntl)
        wr = [arena.alloc((8, 128), BF16) for _ in range(3)]
        B_wr = [Buf() for _ in range(3)]
        sr = [arena.alloc((512,), BF16) for _ in range(3)]
        B_sr = [Buf() for _ in range(3)]
        cnt = 0
        for bk in range(8):
            wi_ = bk % 3
            wload(wr[wi_], w_in, C_RG + bk * 128, 128, 8, [B_wr[wi_]])
            for (t0, n) in allg:
                N = n * 128
                c0 = t0 * 128
                pr, bp = bank8()
                for kc in range(8):
                    MM(pr[:, 0:N], wr[wi_][:, kc, :], xnT[:, kc, c0:c0 + N], kc == 0, kc == 7, [B_xnT[t0 + j] for j in range(n)] + [B_wr[wi_]], [bp])
                i = cnt % 3
                cnt += 1
                ACT(sr[i][:, 0:N], pr[:, 0:N], AF.Silu, [bp], [B_sr[i]])
                TT("dve", onT[:, bk, c0:c0 + N], onT[:, bk, c0:c0 + N], sr[i][:, 0:N], ALU.mult, [B_sr[i], B_on[bk]], [B_on[bk]])
        mT = arena.alloc((8, 2176), BF16)
        B_m = [Buf() for _ in range(8)]
        NW = 3
        wga = [arena.alloc((8, 128), BF16) for _ in range(NW)]
        wgb = [arena.alloc((8, 128), BF16) for _ in range(NW)]
        wao = [arena.alloc((2, 128), BF16) for _ in range(NW)]
        wbo = [arena.alloc((8, 128), BF16) for _ in range(NW)]
        B_wga = [Buf() for _ in range(NW)]
        B_wgb = [Buf() for _ in range(NW)]
        B_wao = [Buf() for _ in range(NW)]
        B_wbo = [Buf() for _ in range(NW)]
        sa = [arena.alloc((512,), F32) for _ in range(4)]
        B_sa = [Buf() for _ in range(4)]
        t1 = [arena.alloc((512,), F32) for _ in range(3)]
        B_t1 = [Buf() for _ in range(3)]
        wo = arena.alloc((8, 1024), BF16)
        B_wo = Buf()
        xio = [arena.alloc((512,), F32) for _ in range(4)]
        B_xio = [Buf() for _ in range(4)]
        cnt = 0
        def load_fc(fc):
            w_ = fc % NW
            wload(wga[w_], w_in, C_GA + fc * 128, 128, 8, [B_wga[w_]])
            wload(wgb[w_], w_in, C_GB + fc * 128, 128, 8, [B_wgb[w_]])
            wload(wao[w_], w_a_out, fc * 128, 128, 2, [B_wao[w_]])
            wload(wbo[w_], w_b_out, fc * 128, 128, 8, [B_wbo[w_]])

        load_fc(0)
        load_fc(1)
        for fc in range(8):
            wi_ = fc % NW
            if fc + 2 < 8:
                load_fc(fc + 2)
            if fc == 2:
                wload(wo, w_out, 0, 1024, 8, [B_wo])
            for (t0, n) in allg:
                N = n * 128
                c0 = t0 * 128
                rdx = [B_xnT[t0 + j] for j in range(n)]
                pA, bA = bank8()
                for kc in range(8):
                    MM(pA[:, 0:N], wga[wi_][:, kc, :], xnT[:, kc, c0:c0 + N], kc == 0, kc == 7, rdx + [B_wga[wi_]], [bA])
                pC, bC = bank8()
                for hp in range(2):
                    MM(pC[:, 0:N], wao[wi_][:, hp, :], AT[:, hp, c0:c0 + N], hp == 0, hp == 1, [B_AT, B_wao[wi_]], [bC])
                ia = cnt % 4
                it = (cnt // 2) % 3
                cnt += 1
                ACT(sa[ia][:, 0:N], pA[:, 0:N], AF.Sigmoid, [bA], [B_sa[ia]])
                TT("dve", t1[it][:, 0:N], sa[ia][:, 0:N], pC[:, 0:N], ALU.mult, [B_sa[ia], bC], [B_t1[it]])
                pB, bB = bank8()
                for kc in range(8):
                    MM(pB[:, 0:N], wgb[wi_][:, kc, :], xnT[:, kc, c0:c0 + N], kc == 0, kc == 7, rdx + [B_wgb[wi_]], [bB])
                pD, bD = bank8()
                for bk in range(8):
                    MM(pD[:, 0:N], wbo[wi_][:, bk, :], onT[:, bk, c0:c0 + N], bk == 0, bk == 7, [B_on[bk], B_wbo[wi_]], [bD])
                ib = cnt % 4
                cnt += 1
                ACT(sa[ib][:, 0:N], pB[:, 0:N], AF.Sigmoid, [bB], [B_sa[ib]])
                TT("dve", sa[ib][:, 0:N], sa[ib][:, 0:N], pD[:, 0:N], ALU.mult, [B_sa[ib], bD], [B_sa[ib]])
                TT("pool", mT[:, fc, c0:c0 + N], t1[it][:, 0:N], sa[ib][:, 0:N], ALU.add, [B_t1[it], B_sa[ib]], [B_m[fc]])
        cnt = 0
        for t in range(ntl):
            for ch in range(2):
                po, bpo = bank8()
                for fc in range(8):
                    MM(po[:, :], mT[:, fc, t * 128:(t + 1) * 128], wo[:, fc, ch * 512:(ch + 1) * 512], fc == 0, fc == 7, [B_m[fc], B_wo], [bpo])
                xi = cnt % 4
                cnt += 1
                DMA("sp", xio[xi], src(ps, t)[:, ch * 512:(ch + 1) * 512], (), [B_xio[xi]])
                TT("dve", xio[xi], xio[xi], po[:, :], ALU.add, [bpo, B_xio[xi]], [B_xio[xi]])
                DMA("act", dst(ps, t)[:, ch * 512:(ch + 1) * 512], xio[xi], [B_xio[xi]], ())
        S.barrier()
        arena.top = m0

    def src_x(ps, t):
        return xs[0:128, :] if t == 16 else xp[128 * t:128 * (t + 1), :]

    def src_x1(ps, t):
        r = row0(ps, t)
        return x1d[r:r + 128, :]

    def src_x2(ps, t):
        r = row0(ps, t)
        return x2d[r:r + 128, :]

    def dst_y(ps, t):
        return ys[0:128, :] if t == 16 else yp[128 * t:128 * (t + 1), :]

    steps = []
    for ps in range(1):
        ntl = 17
        steps.append(lambda ps=ps, ntl=ntl: phase_norm(ps, ntl, src_x, 0))
        steps.append(lambda ps=ps, ntl=ntl: phase_ffn(ps, ntl, 0, src_x, src_x1))
        steps.append(lambda ps=ps, ntl=ntl: phase_norm(ps, ntl, src_x1, 1))
        steps.append(lambda ps=ps, ntl=ntl: phase_mixer(ps, ntl, src_x1, src_x2))
        steps.append(lambda ps=ps, ntl=ntl: phase_norm(ps, ntl, src_x2, 2))
        steps.append(lambda ps=ps, ntl=ntl: phase_ffn(ps, ntl, 1, src_x2, dst_y))
    for st_ in steps[:nsteps]:
        st_()
        S.barrier()
    S.emit(nc)
    return nc


def _consts():
    k = np.arange(128)[:, None]
    j = np.arange(256)[None, :]
    tri2 = np.where(j < 128, k <= j, k >= (j - 128)).astype(np.float32)
    q = np.arange(128)[None, :]
    causal = (k <= q).astype(np.float32)
    same = (k // 8) == (q // 8)
    bd8 = (same & (k <= q)).astype(np.float32)
    mnew = np.zeros((128, 3, 128), np.float32)
    for g, d in enumerate(DIL):
        mnew[:, g, :] = (same & (k <= q) & (((q - k) % d) == 0)).astype(np.float32)
    mc = np.zeros((128, 13, 32), np.float32)
    t = np.arange(8)[None, :]
    kk = np.arange(128)[:, None]
    m0 = (kk >= t).astype(np.float32)
    mc[:, 0, :] = np.tile(m0, (1, 4))
    for r in range(4):
        mm = (((t % 4) == r) & ((4 * kk + r) >= t)).astype(np.float32)
        mc[:, 1 + r, :] = np.tile(mm, (1, 4))
    for r in range(8):
        mm = ((t == r) & (kk >= 0)).astype(np.float32)
        mc[:, 5 + r, :] = np.tile(mm, (1, 4))
    blk = ((np.arange(128)[:, None] // 8) == np.arange(16)[None, :]).astype(np.float32)
    onespad = np.zeros((128, 2, 128), np.float32)
    onespad[:, 0, 0:64] = 1.0
    onespad[:, 1, 64:128] = 1.0
    return dict(c_ident=np.eye(128, dtype=np.float32), c_tri2=tri2, c_causal=causal, c_bd8=bd8, c_mnew=mnew,
                c_mc=mc, c_blk=blk, c_onespad=onespad)


_NC = None


def kernel(x_prompt, x_sample, cache_swa128_kv, cache_swa512_kv, cache_swa2048_kv, state_gla,
           norm_ffn1, ffn1_gate, ffn1_up, ffn1_down, norm_mix, w_in, a_q_norm, a_k_norm,
           g_alpha_up, g_alpha_bias, g_out_norm, w_a_out, w_b_out, w_out,
           norm_ffn2, ffn2_gate, ffn2_up, ffn2_down):
    global _NC
    f = lambda a: np.ascontiguousarray(np.asarray(a, dtype=np.float32))
    if _NC is None:
        _NC = build_nc()
    nc = _NC
    shared = dict(_consts())
    shared.update(f1g=f(ffn1_gate), f1u=f(ffn1_up), f1d=f(ffn1_down), f2g=f(ffn2_gate), f2u=f(ffn2_up), f2d=f(ffn2_down),
                  w_in=f(w_in), w_a_out=f(w_a_out), w_b_out=f(w_b_out), w_out=f(w_out), g_up=f(g_alpha_up))
    gn = np.stack([f(norm_ffn1), f(norm_mix), f(norm_ffn2)], 0)
    shared["c_gnorm"] = np.ascontiguousarray(np.broadcast_to(gn[None], (128, 3, D)))
    aq, ak = f(a_q_norm), f(a_k_norm)
    gqk = np.concatenate([np.tile(aq[:, None, :], (1, 4, 1)).reshape(3, 256), np.tile(ak[:, None, :], (1, 4, 1)).reshape(3, 256)], axis=1)
    shared["c_gqk"] = np.ascontiguousarray(np.broadcast_to(gqk[None], (128, 3, 512)))
    shared["c_gout"] = np.ascontiguousarray(f(g_out_norm).reshape(2, 128).T)
    shared["c_abias"] = np.ascontiguousarray(f(g_alpha_bias).reshape(4, 128).T)
    xp_, xs_ = f(x_prompt), f(x_sample)
    c1, c2, c3, sg = f(cache_swa128_kv), f(cache_swa512_kv), f(cache_swa2048_kv), f(state_gla)
    in_maps = []
    for c in range(8):
        m = dict(shared)
        m["xp"] = np.ascontiguousarray(xp_[c // 2, (c % 2) * 2048:(c % 2 + 1) * 2048])
        m["c_flag"] = np.full((128, 1), float(c % 2), np.float32)
        m["xs"] = xs_[16 * c:16 * c + 16].reshape(128, D)
        m["c128"] = c1[16 * c:16 * c + 16].reshape(16, 128, 2, 256)
        m["c512"] = c2[16 * c:16 * c + 16].reshape(16, 512, 2, 256)
        m["c2048"] = c3[16 * c:16 * c + 16].reshape(16, 2048, 2, 256)
        m["sgla"] = sg[16 * c:16 * c + 16]
        in_maps.append(m)
    res = run_bass_kernel_spmd(nc, in_maps, core_ids=list(range(8)))
    R = res.results
    y_prompt = np.stack([np.concatenate([R[2 * b]["yp"], R[2 * b + 1]["yp"]], 0) for b in range(4)], 0)
    y_sample = np.concatenate([R[c]["ys"].reshape(16, 8, D) for c in range(8)], 0)
    outs = [y_prompt, y_sample]
    for nm, w in (("kv128p", 128), ("kv512p", 512), ("kv2048p", 2048)):
        outs.append(np.stack([R[2 * b + 1][nm].reshape(w, 2, 4, 64) for b in range(4)], 0))
    outs.append(np.stack([R[2 * b + 1]["glap"] for b in range(4)], 0))
    for nm in ("kv128s", "kv512s", "kv2048s"):
        outs.append(np.concatenate([R[c][nm].reshape(16, 8, 2, 4, 64) for c in range(8)], 0))
    outs.append(np.concatenate([R[c]["glas"] for c in range(8)], 0))
    return tuple(np.ascontiguousarray(o, dtype=np.float32) for o in outs)
```

```python
import math
from contextlib import ExitStack
import numpy as np
import concourse.bass as bass
import concourse.mybir as mybir
from concourse.bass_utils import run_bass_kernel_spmd

F32 = mybir.dt.float32
BF16 = mybir.dt.bfloat16
AF = mybir.ActivationFunctionType
ALU = mybir.AluOpType
AX = mybir.AxisListType

N_DMA_SEMS = 12
D = 1024
DFF = 2816
EPS = 1e-6
C_QA, C_KA, C_VA, C_QG, C_KG, C_VG, C_RG, C_LR, C_GA, C_GB = 0, 768, 1536, 2304, 2816, 3328, 4352, 5376, 5392, 6416
DIL = (1, 4, 16)


class Buf:
    __slots__ = ("name", "w", "rs", "excl")

    def __init__(self, name="", excl=False):
        self.name = name
        self.w = None
        self.rs = []
        self.excl = excl


class Op:
    __slots__ = ("eng", "fn", "deps", "sig", "tick", "dma", "semi", "semv", "gi", "cc")


class Sched:
    ENGS = ("pe", "act", "dve", "pool", "sp")

    def __init__(self):
        self.ops = []
        self.last = {}
        self.dmas_since = []
        self.cur_bar = None
        self.bar_seen = set()
        self.ncc = 0

    def add(self, eng, fn, reads=(), writes=(), dma=False, extra=(), cc=False):
        op = Op()
        op.eng = eng
        op.fn = fn
        op.dma = dma
        op.cc = -1
        if cc:
            op.cc = self.ncc
            self.ncc += 1
        op.sig = False
        op.tick = 0
        op.semi = -1
        op.semv = 0
        op.gi = len(self.ops)
        deps = {}
        for b in reads:
            if b.w is not None:
                deps[b.w.gi] = b.w
            if b.excl:
                for r in b.rs:
                    if r.eng != eng:
                        deps[r.gi] = r
        for b in writes:
            if b.w is not None:
                deps[b.w.gi] = b.w
            for r in b.rs:
                deps[r.gi] = r
        for d in extra:
            deps[d.gi] = d
        if self.cur_bar is not None and eng not in self.bar_seen:
            deps[self.cur_bar.gi] = self.cur_bar
            self.bar_seen.add(eng)
        for b in reads:
            b.rs.append(op)
        for b in writes:
            b.w = op
            b.rs = []
        op.deps = [d for d in deps.values() if not (eng == "pe" and d.eng == "pe" and not d.dma)]
        self.ops.append(op)
        if dma or cc:
            self.dmas_since.append(op)
        else:
            self.last[eng] = op
        return op

    def coll(self, fn, reads=(), writes=()):
        return self.add("pool", fn, reads, writes, cc=True)

    def pe(self, fn, reads=(), writes=()):
        return self.add("pe", fn, reads, writes)

    def act(self, fn, reads=(), writes=()):
        return self.add("act", fn, reads, writes)

    def dve(self, fn, reads=(), writes=()):
        return self.add("dve", fn, reads, writes)

    def pool(self, fn, reads=(), writes=()):
        return self.add("pool", fn, reads, writes)

    def dma(self, q, fn, reads=(), writes=()):
        return self.add(q, fn, reads, writes, dma=True)

    def barrier(self):
        deps = [o for o in self.last.values()] + list(self.dmas_since)
        self.cur_bar = None
        op = self.add("sp", lambda e: e.nop(), extra=deps)
        self.cur_bar = op
        self.bar_seen = {"sp"}
        self.dmas_since = []
        self.last = {"sp": op}
        return op

    def emit(self, nc):
        ops = self.ops
        qcount = {e: 0 for e in self.ENGS}
        qhist = {e: [] for e in self.ENGS}
        for op in ops:
            if op.dma:
                i = qcount[op.eng]
                qcount[op.eng] += 1
                op.semi = i % N_DMA_SEMS
                op.semv = 16 * (i // N_DMA_SEMS + 1)
                if i >= N_DMA_SEMS:
                    op.deps.append(qhist[op.eng][i - N_DMA_SEMS])
                qhist[op.eng].append(op)
                op.sig = True
        for op in ops:
            for d in op.deps:
                d.sig = True
        ticks = {e: 0 for e in self.ENGS}
        for op in ops:
            if op.cc >= 0:
                op.sig = True
            elif op.sig and not op.dma:
                ticks[op.eng] += 1
                op.tick = ticks[op.eng]
        per_eng = {e: [o for o in ops if o.eng == e] for e in self.ENGS}
        with ExitStack() as st:
            csem = {e: st.enter_context(nc.semaphore("c_" + e)) for e in self.ENGS}
            dsem = {
                e: [st.enter_context(nc.semaphore("d_%s_%d" % (e, j))) for j in range(N_DMA_SEMS)]
                for e in self.ENGS
                if qcount[e] > 0
            }
            ccsem = [st.enter_context(nc.semaphore("cc_%d" % j)) for j in range(self.ncc)]
            block = st.enter_context(nc.Block())

            def run(engname, eng):
                known_c = {e: 0 for e in self.ENGS}
                known_d = {}
                for op in per_eng[engname]:
                    need_c = {}
                    need_d = {}
                    for d in op.deps:
                        if d.cc >= 0:
                            need_d[("cc", d.cc)] = 1
                        elif d.dma:
                            key = (d.eng, d.semi)
                            if d.semv > need_d.get(key, 0):
                                need_d[key] = d.semv
                        else:
                            if d.tick > need_c.get(d.eng, 0):
                                need_c[d.eng] = d.tick
                    for e2, tk in need_c.items():
                        if tk > known_c[e2]:
                            eng.wait_ge(csem[e2], tk)
                            known_c[e2] = tk
                    for key, sv in need_d.items():
                        if sv > known_d.get(key, 0):
                            if key[0] == "cc":
                                eng.wait_ge(ccsem[key[1]], 1)
                            else:
                                eng.wait_ge(dsem[key[0]][key[1]], sv)
                            known_d[key] = sv
                    ins = op.fn(eng)
                    if op.sig:
                        if op.cc >= 0:
                            ins.then_inc(ccsem[op.cc])
                        elif op.dma:
                            ins.then_inc(dsem[op.eng][op.semi], 16)
                        else:
                            ins.then_inc(csem[op.eng], 1)

            if per_eng["pe"]:
                @block.tensor
                def _(e):
                    run("pe", e)
            if per_eng["act"]:
                @block.scalar
                def _(e):
                    run("act", e)
            if per_eng["dve"]:
                @block.vector
                def _(e):
                    run("dve", e)
            if per_eng["pool"]:
                @block.gpsimd
                def _(e):
                    run("pool", e)
            if per_eng["sp"]:
                @block.sync
                def _(e):
                    run("sp", e)


class Arena:
    def __init__(self, nc, nbytes):
        self.t = nc.alloc_sbuf_tensor("arena", [128, nbytes // 4], F32)
        self.cap = nbytes
        self.top = 0

    def alloc(self, free_shape, dt):
        n = 1
        for s in free_shape:
            n *= s
        nb = n * (2 if dt == BF16 else 4)
        nb = (nb + 31) // 32 * 32
        off = self.top
        assert off + nb <= self.cap, ("arena overflow", off, nb, self.cap)
        self.top = off + nb
        v = self.t[:, off // 4:(off + nb) // 4]
        if dt == BF16:
            v = v.bitcast(BF16)
        v = v[:, 0:n]
        if len(free_shape) == 2:
            v = v.rearrange("p (a b) -> p a b", a=free_shape[0])
        elif len(free_shape) == 3:
            v = v.rearrange("p (a b c) -> p a b c", a=free_shape[0], b=free_shape[1])
        return v


def build_nc(nsteps=None, dbg=False, mstop=99, fstop=0):
    nc = bass.Bass("TRN2", target_bir_lowering=False)
    S = Sched()

    def din(name, shape):
        return nc.dram_tensor(name, list(shape), F32, kind="ExternalInput").ap()

    def dout(name, shape):
        return nc.dram_tensor(name, list(shape), F32, kind="ExternalOutput").ap()

    def dscr(name, shape):
        return nc.dram_tensor(name, list(shape), F32).ap()

    xp = din("xp", (2048, D))
    xs = din("xs", (128, D))
    caches = [din("c128", (16, 128, 2, 256)), din("c512", (16, 512, 2, 256)), din("c2048", (16, 2048, 2, 256))]
    sgla = din("sgla", (16, 4, 128, 256))
    w_ffn = [(din("f1g", (D, DFF)), din("f1u", (D, DFF)), din("f1d", (DFF, D))),
             (din("f2g", (D, DFF)), din("f2u", (D, DFF)), din("f2d", (DFF, D)))]
    w_in = din("w_in", (D, 7440))
    w_a_out = din("w_a_out", (256, D))
    w_b_out = din("w_b_out", (D, D))
    w_out = din("w_out", (D, D))
    g_up = din("g_up", (16, 512))
    c_ident = din("c_ident", (128, 128))
    c_tri2 = din("c_tri2", (128, 256))
    c_causal = din("c_causal", (128, 128))
    c_bd8 = din("c_bd8", (128, 128))
    c_mnew = din("c_mnew", (128, 3, 128))
    c_mc = din("c_mc", (128, 13, 32))
    c_blk = din("c_blk", (128, 16))
    c_onespad = din("c_onespad", (128, 2, 128))
    c_gnorm = din("c_gnorm", (128, 3, D))
    c_gqk = din("c_gqk", (128, 3, 512))
    c_gout = din("c_gout", (128, 2))
    c_abias = din("c_abias", (128, 4))

    yp = dout("yp", (2048, D))
    ys = dout("ys", (128, D))
    kvp = [dout("kv128p", (128, 2, 256)), dout("kv512p", (512, 2, 256)), dout("kv2048p", (2048, 2, 256))]
    kvs = [dout("kv128s", (128, 2, 256)), dout("kv512s", (128, 2, 256)), dout("kv2048s", (128, 2, 256))]
    glap = dout("glap", (4, 128, 256))
    glas = dout("glas", (16, 4, 128, 256))
    x1d = (dout if dbg else dscr)("x1d", (2176, D))
    x2d = (dout if dbg else dscr)("x2d", (2176, D))
    cinK = [dscr("cinK%d" % g, (128, 128 * DIL[g])) for g in range(3)]
    coutK = [dscr("coutK%d" % g, (256, 128 * DIL[g])) for g in range(3)]
    cinV = [dscr("cinV%d" % g, (128, 256 * DIL[g])) for g in range(3)]
    coutV = [dscr("coutV%d" % g, (256, 256 * DIL[g])) for g in range(3)]
    sin_d = dscr("sin_d", (128, 1024))
    sout_d = dscr("sout_d", (256, 1024))
    c_flag = din("c_flag", (128, 1))
    if dbg:
        dbgA = dout("dbgA", (128, 2, 2176))
        dbgO = dout("dbgO", (128, 8, 2176))
        dbgQ = dout("dbgQ", (128, 4, 2176))
        dbgV = dout("dbgV", (128, 17, 512))

    def sbt(name, shape, dt=F32):
        return nc.alloc_sbuf_tensor(name, list(shape), dt)

    ident = sbt("ident", (128, 128), BF16)
    tri2 = sbt("tri2", (128, 256), BF16)
    causal = sbt("causal", (128, 128), BF16)
    bd8 = sbt("bd8", (128, 128), BF16)
    mnew = sbt("mnew", (128, 3, 128), BF16)
    mc = sbt("mc", (128, 13, 32), BF16)
    blk = sbt("blk", (128, 16), BF16)
    onespad = sbt("onespad", (128, 2, 128), BF16)
    gqk = sbt("gqk", (128, 3, 512))
    gout = sbt("gout", (128, 2))
    nabias = sbt("nabias", (128, 4))
    ones256 = sbt("ones256", (128, 128))
    one1 = sbt("one1", (128, 1))
    ones512 = sbt("ones512", (128, 512))
    onesfull = sbt("onesfull", (128, 128), BF16)
    epsb = sbt("epsb", (128, 1))
    lnsc = sbt("lnsc", (128, 1))
    gup = sbt("gup", (16, 512))
    Sst = sbt("Sst", (128, 4, 256))
    Sbf = sbt("Sbf", (128, 4, 256), BF16)
    xnT = sbt("xnT", (128, 8, 2176), BF16)
    flag = sbt("flag", (128, 1))
    ctxm = sbt("ctxm", (128, 128), BF16)
    SAf = sbt("SAf", (128, 4, 256))
    SAb = sbt("SAb", (128, 4, 256), BF16)
    B_const = Buf("const")
    B_S = [Buf("S%d" % h) for h in range(4)]
    B_Sb = [Buf("Sb%d" % h) for h in range(4)]
    B_xnT = [Buf("xnT%d" % t) for t in range(17)]
    B_KTc = [Buf() for _ in range(3)]
    B_Vcx = [Buf() for _ in range(3)]

    arena = Arena(nc, int(nc.sbuf_bytes_remaining) - 2048)

    PB = [nc.alloc_psum_tensor("pb%d" % i, [128, 512], F32) for i in range(6)]
    PT = [nc.alloc_psum_tensor("pt%d" % i, [128, 1024], BF16) for i in range(2)]
    B_PB = [Buf("pb%d" % i, excl=True) for i in range(6)]
    B_PT = [Buf("pt%d" % i, excl=True) for i in range(2)]
    rr = {"pb": 0, "pt": 0}

    def bank():
        i = rr["pb"]
        rr["pb"] = (i + 1) % 4
        return PB[i], B_PB[i]

    def rbank(i):
        return PB[4 + i], B_PB[4 + i]

    PT32 = [PT[i][:, :].bitcast(F32) for i in range(2)]
    ALL8 = [(PB[i][:, :], B_PB[i]) for i in range(6)] + [(PT32[i], B_PT[i]) for i in range(2)]
    rr["b8"] = 0

    def bank8():
        i = rr["b8"]
        rr["b8"] = (i + 1) % 8
        return ALL8[i]

    def tbank():
        i = rr["pt"]
        rr["pt"] = (i + 1) % 2
        return PT[i], B_PT[i]

    def MM(out, lhsT, rhs, start, stop, reads, writes):
        S.pe(lambda e: e.matmul(out, lhsT=lhsT, rhs=rhs, start=start, stop=stop, skip_group_check=True), reads, writes)

    def TR(out, in_, reads, writes):
        S.pe(lambda e: e.transpose(out, in_, ident[:]), list(reads) + [B_const], writes)

    def ACT(out, in_, func, reads, writes, bias=0.0, scale=1.0, accum_out=None):
        if accum_out is None:
            S.act(lambda e: e.activation(out=out, in_=in_, func=func, bias=bias, scale=scale), reads, writes)
        else:
            S.act(lambda e: e.activation(out=out, in_=in_, func=func, bias=bias, scale=scale, accum_out=accum_out), reads, writes)

    def TT(eng, out, in0, in1, op, reads, writes):
        S.add(eng, lambda e: e.tensor_tensor(out=out, in0=in0, in1=in1, op=op), reads, writes)

    def STT(eng, out, in0, scalar, in1, op0, op1, reads, writes):
        S.add(eng, lambda e: e.scalar_tensor_tensor(out=out, in0=in0, scalar=scalar, in1=in1, op0=op0, op1=op1), reads, writes)

    def TS(eng, out, in0, s1, op0, reads, writes):
        S.add(eng, lambda e: e.tensor_scalar(out=out, in0=in0, scalar1=s1, scalar2=None, op0=op0), reads, writes)

    def CP(eng, out, in_, reads, writes):
        if eng == "act":
            S.act(lambda e: e.copy(out=out, in_=in_), reads, writes)
        else:
            S.add(eng, lambda e: e.tensor_copy(out=out, in_=in_), reads, writes)

    def RCP(out, in_, reads, writes):
        S.dve(lambda e: e.reciprocal(out=out, in_=in_), reads, writes)

    def MSET(eng, ap, val, writes):
        S.add(eng, lambda e: e.memset(ap, val), (), writes)

    def DMA(q, out, in_, reads, writes):
        S.dma(q, lambda e: e.dma_start(out=out, in_=in_), reads, writes)

    for dst, src in ((ident, c_ident), (causal, c_causal), (bd8, c_bd8), (mnew, c_mnew),
                     (mc, c_mc), (blk, c_blk), (onespad, c_onespad)):
        DMA("pool", dst[:], src, (), [Buf()])
    B_nab, B_flag, B_tri = Buf(), Buf(), Buf()
    DMA("sp", gqk[:], c_gqk, (), [Buf()])
    DMA("sp", gout[:], c_gout, (), [Buf()])
    DMA("sp", nabias[:], c_abias, (), [B_nab])
    DMA("sp", gup[:], g_up, (), [Buf()])
    DMA("sp", flag[:], c_flag, (), [B_flag])
    DMA("pool", tri2[:], c_tri2, (), [B_tri])
    S.dve(lambda e: e.tensor_scalar(out=nabias[:], in0=nabias[:], scalar1=-1.0, scalar2=None, op0=ALU.mult), [B_nab], [B_nab])
    MSET("dve", ones256[:], 1.0 / 256.0, [B_const])
    MSET("dve", one1[:], 1.0, [B_const])
    MSET("dve", ones512[:], 1.0, [B_const])
    MSET("dve", onesfull[:], 1.0, [B_const])
    MSET("dve", epsb[:], EPS, [B_const])
    MSET("dve", lnsc[:], math.log(128.0 ** -0.5), [B_const])
    S.dve(lambda e: e.tensor_scalar(out=ctxm[:], in0=tri2[:, 128:256], scalar1=flag[:, 0:1], scalar2=None, op0=ALU.mult), [B_flag, B_tri], [Buf()])
    MSET("dve", Sst[:], 0.0, B_S)
    MSET("dve", Sbf[:], 0.0, B_Sb)
    S.barrier()

    def wload(dst, w, c0, ncol, kc_n, writes, q="pool"):
        DMA(q, dst, w.rearrange("(kc p) n -> p kc n", p=128)[:, 0:kc_n, c0:c0 + ncol], (), writes)

    def row0(ps, t):
        return 2048 if t == 16 else 128 * t

    def groups(ntl):
        g = [(0, 4), (4, 4), (8, 4), (12, 4)]
        if ntl == 17:
            g.append((16, 1))
        return g

    def mpipe(n, stages, lags):
        for step in range(n + lags[-1]):
            for k in reversed(range(len(stages))):
                t = step - lags[k]
                if 0 <= t < n:
                    stages[k](t)

    def phase_norm(ps, ntl, src, gi):
        m = arena.top
        gain = arena.alloc((D,), F32)
        B_g = Buf()
        DMA("sp", gain, c_gnorm[:, gi, :], (), [B_g])
        NB = 6
        xt = [arena.alloc((D,), F32) for _ in range(NB)]
        B_xt = [Buf() for _ in range(NB)]
        junk = arena.alloc((D,), BF16)
        B_junk = Buf()
        xnb = [arena.alloc((D,), BF16) for _ in range(NB)]
        B_xnb = [Buf() for _ in range(NB)]
        st = [arena.alloc((4,), F32) for _ in range(NB)]
        B_st = [Buf() for _ in range(NB)]
        pts = {}

        def n0(t):
            i = t % NB
            DMA("sp", xt[i], src(ps, t), (), [B_xt[i]])
            MSET("pool", st[i][:, 0:1], 0.0, [B_st[i]])
            ACT(junk, xt[i], AF.Square, [B_xt[i]], [B_junk, B_st[i]], accum_out=st[i][:, 0:1])

        def n1(t):
            i = t % NB
            ACT(st[i][:, 1:2], st[i][:, 0:1], AF.Sqrt, [B_st[i], B_const], [B_st[i]], bias=epsb[:], scale=1.0 / D)

        def n2(t):
            i = t % NB
            RCP(st[i][:, 2:3], st[i][:, 1:2], [B_st[i]], [B_st[i]])
            STT("dve", xnb[i], xt[i], st[i][:, 2:3], gain, ALU.mult, ALU.mult, [B_xt[i], B_st[i], B_g], [B_xnb[i]])

        def n3(t):
            i = t % NB
            pt, bpt = tbank()
            for kc in range(8):
                TR(pt[:, kc * 128:(kc + 1) * 128], xnb[i][:, kc * 128:(kc + 1) * 128], [B_xnb[i]], [bpt])
            pts[t] = (pt, bpt)

        def n4(t):
            pt, bpt = pts.pop(t)
            CP("act" if t % 2 == 0 else "dve", xnT[:, :, t * 128:(t + 1) * 128],
               pt[:, 0:1024].rearrange("p (k n) -> p k n", k=8), [bpt], [B_xnT[t]])

        mpipe(ntl, [n0, n1, n2, n3, n4], [0, 1, 2, 3, 4])
        arena.top = m

    def phase_ffn(ps, ntl, wi, src, dst):
        Wg, Wu, Wd = w_ffn[wi]
        m = arena.top
        NT = ntl * 128
        hT = arena.alloc((11, 2176), BF16)
        B_h = [[Buf() for _ in range(5)] for _ in range(11)]
        wg = [arena.alloc((8, 128), BF16) for _ in range(2)]
        wu = [arena.alloc((8, 128), BF16) for _ in range(2)]
        B_wg = [Buf() for _ in range(2)]
        B_wu = [Buf() for _ in range(2)]
        wd = [arena.alloc((11, 512), BF16) for _ in range(2)]
        B_wd = [Buf() for _ in range(2)]
        sg = [arena.alloc((512,), F32) for _ in range(2)]
        B_sg = [Buf() for _ in range(2)]
        xio = [arena.alloc((512,), F32) for _ in range(6)]
        B_xio = [Buf() for _ in range(6)]
        B_dst = [[Buf() for _ in range(2)] for _ in range(17)]
        cnt = 0
        for fg in range(2):
            for f in range(11):
                fc = fg * 11 + f
                wi_ = fc % 2
                wload(wg[wi_], Wg, fc * 128, 128, 8, [B_wg[wi_]])
                wload(wu[wi_], Wu, fc * 128, 128, 8, [B_wu[wi_]])
                for gidx, (t0, n) in enumerate(groups(ntl)):
                    N = n * 128
                    rd = [B_xnT[t0 + j] for j in range(n)]
                    pg, bpg = bank8()
                    for kc in range(8):
                        MM(pg[:, 0:N], wg[wi_][:, kc, :], xnT[:, kc, t0 * 128:t0 * 128 + N], kc == 0, kc == 7, rd + [B_wg[wi_]], [bpg])
                    pu, bpu = bank8()
                    for kc in range(8):
                        MM(pu[:, 0:N], wu[wi_][:, kc, :], xnT[:, kc, t0 * 128:t0 * 128 + N], kc == 0, kc == 7, rd + [B_wu[wi_]], [bpu])
                    si = cnt % 2
                    cnt += 1
                    ACT(sg[si][:, 0:N], pg[:, 0:N], AF.Silu, [bpg], [B_sg[si]])
                    TT("dve", hT[:, f, t0 * 128:t0 * 128 + N], sg[si][:, 0:N], pu[:, 0:N], ALU.mult, [B_sg[si], bpu], [B_h[f][gidx]])
            for ch in range(2):
                wdi = ch
                DMA("pool", wd[wdi], Wd.rearrange("(fc p) n -> p fc n", p=128)[:, fg * 11:(fg + 1) * 11, ch * 512:(ch + 1) * 512], (), [B_wd[wdi]])
                for t in range(ntl):
                    po, bpo = bank8()
                    gidx = t // 4
                    for f in range(11):
                        MM(po[:, :], hT[:, f, t * 128:(t + 1) * 128], wd[wdi][:, f, :], f == 0, f == 10, [B_h[f][gidx], B_wd[wdi]], [bpo])
                    xi = cnt % 6
                    cnt += 1
                    s_ap = (src if fg == 0 else dst)(ps, t)[:, ch * 512:(ch + 1) * 512]
                    DMA("sp", xio[xi], s_ap, [B_dst[t][ch]] if fg == 1 else (), [B_xio[xi]])
                    STT("dve", xio[xi], po[:, :], 0.5, xio[xi], ALU.mult, ALU.add, [bpo, B_xio[xi]], [B_xio[xi]])
                    DMA("act", dst(ps, t)[:, ch * 512:(ch + 1) * 512], xio[xi], [B_xio[xi]], [B_dst[t][ch]])
        arena.top = m

    def phase_mixer(ps, ntl, src, dst):
        m0 = arena.top
        NT = ntl * 128
        samp = ntl == 17
        AT = arena.alloc((2, 2176), BF16)
        B_AT = Buf()
        mA = arena.top
        AN = arena.alloc((2, 2176), F32)
        AD = arena.alloc((2, 2176), F32)
        B_ANall = Buf()
        m1 = arena.top
        def pipeline(items, lag):
            n = len(items)
            for i in range(n + lag):
                if i < n:
                    items[i][0]()
                if i >= lag:
                    items[i - lag][1]()

        RG = [[0, 1], [2, 3], [4, 5], [6, 7]]
        for g in range(3):
            arena.top = m1
            d = DIL[g]
            nb = 16 // d
            QKT = arena.alloc((4, 2176), BF16)
            B_QK = [Buf() for _ in range(17)]
            Vpad = arena.alloc((17, 4, 128), BF16)
            B_V = [Buf() for _ in range(17)]
            mW = arena.top
            wq = arena.alloc((8, 256), BF16)
            wk = arena.alloc((8, 256), BF16)
            wv = arena.alloc((8, 256), BF16)
            B_wq, B_wk, B_wv = Buf(), Buf(), Buf()
            wload(wq, w_in, C_QA + g * 256, 256, 8, [B_wq])
            wload(wk, w_in, C_KA + g * 256, 256, 8, [B_wk])
            wload(wv, w_in, C_VA + g * 256, 256, 8, [B_wv])
            MSET("pool", Vpad, 0.0, B_V)
            NBUF = 5
            sq = [arena.alloc((512,), F32) for _ in range(NBUF)]
            B_sq = [Buf() for _ in range(NBUF)]
            qkn = [arena.alloc((512,), F32) for _ in range(NBUF)]
            B_qkn = [Buf() for _ in range(NBUF)]
            qkb = [arena.alloc((512,), BF16) for _ in range(NBUF)]
            B_qkb = [Buf() for _ in range(NBUF)]
            st8 = [arena.alloc((24,), F32) for _ in range(NBUF)]
            B_st8 = [Buf() for _ in range(NBUF)]
            vf = [arena.alloc((256,), F32) for _ in range(2)]
            B_vf = [Buf() for _ in range(2)]
            pqs = {}
            pts = {}

            def q0(t):
                i = t % NBUF
                pqk, bp = bank()
                for kc in range(8):
                    MM(pqk[:, 0:256], xnT[:, kc, t * 128:(t + 1) * 128], wq[:, kc, :], kc == 0, kc == 7, [B_xnT[t], B_wq], [bp])
                for kc in range(8):
                    MM(pqk[:, 256:512], xnT[:, kc, t * 128:(t + 1) * 128], wk[:, kc, :], kc == 0, kc == 7, [B_xnT[t], B_wk], [bp])
                ACT(sq[i], pqk[:, :], AF.Square, [bp], [B_sq[i]])
                pqs[t] = (pqk, bp)

            def q1(t):
                i = t % NBUF
                S.dve(lambda e, o=st8[i][:, 0:8], a=sq[i].rearrange("p (h d) -> p h d", h=8): e.tensor_reduce(out=o, in_=a, axis=AX.X, op=ALU.add), [B_sq[i]], [B_st8[i]])

            def q2(t):
                i = t % NBUF
                ACT(st8[i][:, 8:16], st8[i][:, 0:8], AF.Sqrt, [B_st8[i], B_const], [B_st8[i]], bias=epsb[:], scale=1.0 / 64)

            def q3(t):
                i = t % NBUF
                pqk, bp = pqs.pop(t)
                RCP(st8[i][:, 16:24], st8[i][:, 8:16], [B_st8[i]], [B_st8[i]])
                TT("dve", qkn[i].rearrange("p (h d) -> p h d", h=8), pqk[:, :].rearrange("p (h d) -> p h d", h=8),
                   st8[i][:, 16:24].unsqueeze(2).broadcast_to([128, 8, 64]), ALU.mult, [bp, B_st8[i]], [B_qkn[i]])

            def q4(t):
                i = t % NBUF
                TT("pool", qkn[i], qkn[i], gqk[:, g, :], ALU.mult, [B_qkn[i], B_const], [B_qkn[i]])

            def q5(t):
                i = t % NBUF
                CP("act", qkb[i], qkn[i], [B_qkn[i]], [B_qkb[i]])
                if t == 16:
                    DMA("sp", kvs[g][:, 0, :], qkn[i][:, 256:512], [B_qkn[i]], ())
                elif t >= 16 - d:
                    r0 = 128 * t - (2048 - 128 * d)
                    DMA("sp", kvp[g][r0:r0 + 128, 0, :], qkn[i][:, 256:512], [B_qkn[i]], ())

            def q6(t):
                i = t % NBUF
                pt, bpt = tbank()
                for j in range(4):
                    TR(pt[:, j * 128:(j + 1) * 128], qkb[i][:, j * 128:(j + 1) * 128], [B_qkb[i]], [bpt])
                pts[t] = (pt, bpt)

            def q7(t):
                pt, bpt = pts.pop(t)
                CP("dve", QKT[:, :, t * 128:(t + 1) * 128], pt[:, 0:512].rearrange("p (k n) -> p k n", k=4), [bpt], [B_QK[t]])

            mpipe(ntl, [q0, q1, q2, q3, q4, q5, q6, q7], [0, 1, 2, 3, 4, 5, 6, 7])

            def vtile(ti, lhs_of_kc, rd, out_ap):
                i = ti % 2
                pv, bp = bank()
                for kc in range(8):
                    MM(pv[:, 0:256], lhs_of_kc(kc), wv[:, kc, :], kc == 0, kc == 7, rd + [B_wv], [bp])
                if out_ap is not None:
                    CP("act", vf[i], pv[:, 0:256], [bp], [B_vf[i]])
                    DMA("sp", out_ap, vf[i], [B_vf[i]], ())
                    pv3 = vf[i].rearrange("p (h d) -> p h d", h=4)
                    rdv = [B_vf[i]]
                else:
                    pv3 = pv[:, 0:256].rearrange("p (h d) -> p h d", h=4)
                    rdv = [bp]
                CP("dve", Vpad[:, ti, 0:4:2, 0:64], pv3[:, 0:4:2, :], rdv, [B_V[ti]])
                CP("dve", Vpad[:, ti, 1:4:2, 64:128], pv3[:, 1:4:2, :], rdv, [B_V[ti]])

            for r in range(d):
                for ib in range(nb):
                    ti = r * nb + ib
                    lo = r + d * 128 * ib
                    hi = r + d * 128 * (ib + 1)
                    out_ap = None
                    if ib == nb - 1:
                        out_ap = kvp[g].rearrange("(j dd) two c -> dd j two c", dd=d)[r, :, 1, :]
                    vtile(ti, lambda kc, lo=lo, hi=hi: xnT[:, kc, lo:hi:d], [B_xnT[t] for t in range(16)], out_ap)
            if samp:
                vtile(16, lambda kc: xnT[:, kc, 2048:2176], [B_xnT[16]], kvs[g][:, 1, :])
            S.barrier()
            arena.top = mW
            KTcg = arena.alloc((2, 128 * d), BF16)
            Vcxg = arena.alloc((d, 4, 128), BF16)
            B_cinK, B_cinV, B_coutK, B_coutV = Buf(), Buf(), Buf(), Buf()
            DMA("sp", cinK[g].rearrange("p (a n) -> p a n", a=2),
                QKT[:, 2:4, 2048 - 128 * d:2048].bitcast(F32), B_QK[0:16], [B_cinK])
            DMA("sp", cinV[g].rearrange("p (r n) -> p r n", r=d),
                Vpad[:, 0:16, :, :].rearrange("p (r i) h e -> p r i (h e)", i=nb)[:, :, nb - 1, :].bitcast(F32), B_V[0:16], [B_cinV])
            S.coll(lambda e, a=cinK[g], b=coutK[g]: e.collective_compute("AllGather", ALU.bypass, replica_groups=RG,
                                                                         ins=[a.opt()], outs=[b.opt()]), [B_cinK], [B_coutK])
            S.coll(lambda e, a=cinV[g], b=coutV[g]: e.collective_compute("AllGather", ALU.bypass, replica_groups=RG,
                                                                         ins=[a.opt()], outs=[b.opt()]), [B_cinV], [B_coutV])
            DMA("sp", KTcg.bitcast(F32), coutK[g][0:128, :].rearrange("p (a n) -> p a n", a=2), [B_coutK], [B_KTc[g]])
            DMA("sp", Vcxg.rearrange("p r h e -> p r (h e)").bitcast(F32), coutV[g][0:128, :].rearrange("p (r n) -> p r n", r=d), [B_coutV], [B_Vcx[g]])
            LAG = 3
            NPB = LAG + 2
            ptb = [arena.alloc((256,), BF16) for _ in range(NPB)]
            B_ptb = [Buf() for _ in range(NPB)]
            ptm = [arena.alloc((256,), BF16) for _ in range(NPB)]
            B_ptm = [Buf() for _ in range(NPB)]
            allQK = B_QK[0:16]
            items = []
            state = {"cnt": 0, "b3": 0}

            def bank3():
                i = state["b3"]
                state["b3"] = (i + 1) % 4
                return PB[i], B_PB[i]
            for hp in range(2):
                for r in range(d):
                    accs = {}
                    started = {}
                    kbs = [-1] + list(range(nb))
                    for kb in kbs:
                        for hh in range(2):
                            def mkA(hp=hp, r=r, kb=kb, hh=hh, slot=len(items) % NPB):
                                nq = 1 if (kb == -1 or kb == nb - 1) else 2
                                N = 128 * nq
                                qlo = r + d * 128 * max(kb, 0)
                                p0 = hh * 64
                                pss, bps = bank3()
                                if kb == -1:
                                    kap = KTcg[p0:p0 + 64, hp, r:128 * d:d]
                                    krd = [B_KTc[g]]
                                    msk = ctxm[:]
                                else:
                                    kap = QKT[p0:p0 + 64, 2 + hp, qlo:qlo + d * 128:d]
                                    krd = allQK
                                    msk = tri2[:, 0:N]
                                MM(pss[:, 0:N], kap, QKT[p0:p0 + 64, hp, qlo:qlo + d * N:d], True, True, krd + allQK, [bps])
                                ACT(ptb[slot][:, 0:N], pss[:, 0:N], AF.Exp, [bps], [B_ptb[slot]], scale=0.125)
                                state["cnt"] += 1
                                TT("pool" if state["cnt"] % 2 else "dve", ptm[slot][:, 0:N], ptb[slot][:, 0:N], msk, ALU.mult,
                                   [B_ptb[slot], B_const], [B_ptm[slot]])

                            def mkB(hp=hp, r=r, kb=kb, hh=hh, slot=len(items) % NPB, accs=accs, started=started):
                                nq = 1 if (kb == -1 or kb == nb - 1) else 2
                                h = 2 * hp + hh
                                if kb == -1:
                                    vap = Vcxg[:, r, h, :]
                                    vrd = [B_Vcx[g]]
                                else:
                                    vap = Vpad[:, r * nb + kb, h, :]
                                    vrd = [B_V[r * nb + kb]]
                                targets = [(max(kb, 0), 0)]
                                if nq == 2:
                                    targets.append((kb + 1, 1))
                                for (ib, half) in targets:
                                    if ib not in accs:
                                        accs[ib] = rbank(ib % 2)
                                        started[ib] = False
                                    pa, bpa = accs[ib]
                                    last = (kb == ib) and hh == 1
                                    rhs = ptm[slot][:, half * 128:(half + 1) * 128]
                                    MM(pa[:, 0:128], vap, rhs, not started[ib], last, vrd + [B_ptm[slot]], [bpa])
                                    started[ib] = True
                                    MM(pa[:, 128:256], onespad[:, hh, :], rhs, False, last, [B_const, B_ptm[slot]], [bpa])
                                if kb >= 0 and hh == 1:
                                    pa, bpa = accs.pop(kb)
                                    lo = r + d * 128 * kb
                                    hi = lo + d * 128
                                    if g == 0:
                                        CP("act", AN[:, hp, lo:hi:d], pa[:, 0:128], [bpa], [B_ANall])
                                        CP("act", AD[:, hp, lo:hi:d], pa[:, 128:256], [bpa], [B_ANall])
                                    else:
                                        TT("dve", AN[:, hp, lo:hi:d], AN[:, hp, lo:hi:d], pa[:, 0:128], ALU.add, [bpa, B_ANall], [B_ANall])
                                        TT("dve", AD[:, hp, lo:hi:d], AD[:, hp, lo:hi:d], pa[:, 128:256], ALU.add, [bpa, B_ANall], [B_ANall])
                            items.append((mkA, mkB))
            band_thunks = []
            for i in range(len(items) + LAG):
                if i < len(items):
                    band_thunks.append(items[i][0])
                if i >= LAG:
                    band_thunks.append(items[i - LAG][1])
            samp_thunks = []
            if samp:
                nres = (1, 4, 8)[g]
                mbase = (0, 1, 5)[g]
                W = nres * 32
                Kc = [arena.alloc((nres, 256), BF16) for _ in range(2)]
                B_Kc = [Buf() for _ in range(2)]
                Vc = [arena.alloc((nres, 4, 128), BF16) for _ in range(2)]
                B_Vc = [[Buf() for _ in range(4)] for _ in range(2)]
                for vi in range(2):
                    MSET("pool", Vc[vi], 0.0, B_Vc[vi])
                KcT = [arena.alloc((nres, 2, 128), BF16) for _ in range(2)]
                B_KcT = [Buf() for _ in range(2)]
                pbS = [arena.alloc((256,), BF16) for _ in range(2)]
                B_pbS = [Buf() for _ in range(2)]
                pmS = [arena.alloc((256,), BF16) for _ in range(2)]
                B_pmS = [Buf() for _ in range(2)]
                pnS = [arena.alloc((128,), BF16) for _ in range(2)]
                B_pnS = [Buf() for _ in range(2)]
                sacc_t, bsacc = PT32[1], B_PT[1]
                sacc4 = sacc_t.rearrange("p (a b n) -> p a b n", a=2, b=2)
                for h in range(4):
                    hp, hh = h // 2, h % 2
                    p0 = hh * 64
                    pss, bps = bank3()
                    MM(pss[:, 0:128], QKT[p0:p0 + 64, 2 + hp, 2048:2176], QKT[p0:p0 + 64, hp, 2048:2176], True, True, [B_QK[16]], [bps])
                    bi = h % 2
                    ACT(pbS[bi][:, 0:128], pss[:, 0:128], AF.Exp, [bps], [B_pbS[bi]], scale=0.125)
                    TT("dve", pnS[bi], pbS[bi][:, 0:128], mnew[:, g, :], ALU.mult, [B_pbS[bi], B_const], [B_pnS[bi]])
                    MM(sacc4[:, hp, 0, :], Vpad[:, 16, h, :], pnS[bi], h == 0, False, [B_V[16], B_pnS[bi]], [bsacc])
                    MM(sacc4[:, hp, 1, :], onespad[:, hh, :], pnS[bi], False, False, [B_const, B_pnS[bi]], [bsacc])
                cch = caches[g]

                def sL(b):
                    ci = b % 2
                    cv = cch[b].rearrange("(k dd) two c -> k dd two c", dd=d)
                    DMA("pool", Kc[ci], cv[:, 0:nres, 0, :], (), [B_Kc[ci]])
                    for h in range(4):
                        o64 = (h % 2) * 64
                        DMA("pool", Vc[ci][:, :, h, o64:o64 + 64], cv[:, 0:nres, 1, h * 64:(h + 1) * 64], (), [B_Vc[ci][h]])

                def sA(b):
                    ci = b % 2
                    for r0 in range(0, nres, 4):
                        nr = min(4, nres - r0)
                        pt, bpt = PT[0], B_PT[0]
                        for rr_ in range(nr):
                            for j in range(2):
                                TR(pt[:, (2 * rr_ + j) * 128:(2 * rr_ + j + 1) * 128], Kc[ci][:, r0 + rr_, j * 128:(j + 1) * 128], [B_Kc[ci]], [bpt])
                        CP("act", KcT[ci][:, r0:r0 + nr, :, :], pt[:, 0:nr * 256].rearrange("p (r k n) -> p r k n", r=nr, k=2), [bpt], [B_KcT[ci]])
                    pss, bps = bank3()
                    for rr_ in range(nres):
                        for h in range(4):
                            p0 = (h % 2) * 64
                            MM(pss[:, rr_ * 32 + h * 8:rr_ * 32 + h * 8 + 8], KcT[ci][p0:p0 + 64, rr_, h // 2, :],
                               QKT[p0:p0 + 64, h // 2, 2048 + 8 * b:2048 + 8 * b + 8], True, True, [B_KcT[ci], B_QK[16]], [bps])
                    ACT(pbS[ci][:, 0:W], pss[:, 0:W], AF.Exp, [bps], [B_pbS[ci]], scale=0.125)
                    TT("dve", pmS[ci][:, 0:W], pbS[ci][:, 0:W], mc[:, mbase:mbase + nres, :].rearrange("p r n -> p (r n)"), ALU.mult,
                       [B_pbS[ci], B_const], [B_pmS[ci]])

                def sB(b):
                    ci = b % 2
                    for rr_ in range(nres):
                        for h in range(4):
                            rhs = pmS[ci][:, rr_ * 32 + h * 8:rr_ * 32 + h * 8 + 8]
                            MM(sacc4[:, h // 2, 0, 8 * b:8 * b + 8], Vc[ci][:, rr_, h, :], rhs, False, False, [B_Vc[ci][h], B_pmS[ci]], [bsacc])
                            MM(sacc4[:, h // 2, 1, 8 * b:8 * b + 8], onespad[:, h % 2, :], rhs, False, False, [B_const, B_pmS[ci]], [bsacc])

                sL(0)
                for st_ in range(17):
                    def thunk(st_=st_):
                        if st_ >= 1:
                            sB(st_ - 1)
                        if st_ + 1 < 16:
                            sL(st_ + 1)
                        if st_ < 16:
                            sA(st_)
                    samp_thunks.append(thunk)
            nbt, nst = len(band_thunks), len(samp_thunks)
            si_ = 0
            for bi_, th in enumerate(band_thunks):
                th()
                while si_ < nst and (si_ + 1) * nbt <= (bi_ + 1) * nst:
                    samp_thunks[si_]()
                    si_ += 1
            while si_ < nst:
                samp_thunks[si_]()
                si_ += 1
            if samp:
                for hp in range(2):
                    if g == 0:
                        CP("act", AN[:, hp, 2048:2176], sacc4[:, hp, 0, :], [bsacc], [B_ANall])
                        CP("act", AD[:, hp, 2048:2176], sacc4[:, hp, 1, :], [bsacc], [B_ANall])
                    else:
                        TT("dve", AN[:, hp, 2048:2176], AN[:, hp, 2048:2176], sacc4[:, hp, 0, :], ALU.add, [bsacc, B_ANall], [B_ANall])
                        TT("dve", AD[:, hp, 2048:2176], AD[:, hp, 2048:2176], sacc4[:, hp, 1, :], ALU.add, [bsacc, B_ANall], [B_ANall])
            S.barrier()
        for hp in range(2):
            RCP(AD[:, hp, 0:NT], AD[:, hp, 0:NT], [B_ANall], [B_ANall])
            TT("dve", AT[:, hp, 0:NT], AN[:, hp, 0:NT], AD[:, hp, 0:NT], ALU.mult, [B_ANall], [B_AT])
        if dbg:
            DMA("pool", dbgA[:, :, 0:NT], AT[:, :, 0:NT], [B_AT], ())
        S.barrier()
        arena.top = mA
        if mstop == 4:
            arena.top = m0; return
        onT = arena.alloc((8, 2176), BF16)
        B_on = [Buf() for _ in range(8)]
        mO = arena.top

        lrT = arena.alloc((2176,), F32)
        B_lr = Buf()
        wlr = arena.alloc((8, 16), BF16)
        B_wlr = Buf()
        wload(wlr, w_in, C_LR, 16, 8, [B_wlr])
        for (t0, n) in groups(ntl):
            N = n * 128
            pl, bp = bank()
            for kc in range(8):
                MM(pl[0:16, 0:N], wlr[:, kc, :], xnT[:, kc, t0 * 128:t0 * 128 + N], kc == 0, kc == 7, [B_xnT[t0 + j] for j in range(n)] + [B_wlr], [bp])
            CP("act", lrT[0:16, t0 * 128:t0 * 128 + N], pl[0:16, 0:N], [bp], [B_lr])
        qgT = arena.alloc((4, 2048), BF16)
        B_qg = [Buf() for _ in range(4)]
        dgall = arena.alloc((8,), F32)
        B_dg = Buf()
        osq = [arena.alloc((128,), F32) for _ in range(2)]
        B_osq = [Buf() for _ in range(2)]
        rs = [arena.alloc((128,), F32) for _ in range(2)]
        B_rs = [Buf() for _ in range(2)]

        def out_norm(h, po, bpo, c0, ci):
            pm, bpm = bank()
            for j in range(2):
                ACT(osq[j], po[j][:, 0:128], AF.Square, [bpo[j]], [B_osq[j]])
                MM(pm[:, 0:128], ones256[:], osq[j], j == 0, j == 1, [B_const, B_osq[j]], [bpm])
            ri = ci % 2
            ACT(rs[ri], pm[:, 0:128], AF.Sqrt, [bpm, B_const], [B_rs[ri]], bias=epsb[:], scale=1.0)
            RCP(rs[ri], rs[ri], [B_rs[ri]], [B_rs[ri]])
            for j in range(2):
                STT("dve", onT[:, 2 * h + j, c0:c0 + 128], po[j][:, 0:128], gout[:, j:j + 1], rs[ri], ALU.mult, ALU.mult,
                    [bpo[j], B_const, B_rs[ri]], [B_on[2 * h + j]])

        m2 = arena.top
        if mstop == 5:
            S.barrier(); arena.top = m0; return
        _hc = {}

        def ha(name, shape, dt):
            if name not in _hc:
                _hc[name] = arena.alloc(shape, dt)
            return _hc[name]

        def hb(name):
            if name not in _hc:
                _hc[name] = Buf()
            return _hc[name]

        for h in range(4):
            pass
            wqg = ha("wq%d" % (h % 2), (8, 128), BF16)
            wkg = ha("wk%d" % (h % 2), (8, 128), BF16)
            wvg = ha("wv", (8, 256), BF16)
            B_wq, B_wk, B_wv = hb("bwq%d" % (h % 2)), hb("bwk%d" % (h % 2)), hb("bwv")

            def load_head(hh_):
                wload(ha("wq%d" % (hh_ % 2), (8, 128), BF16), w_in, C_QG + hh_ * 128, 128, 8, [hb("bwq%d" % (hh_ % 2))])
                wload(ha("wk%d" % (hh_ % 2), (8, 128), BF16), w_in, C_KG + hh_ * 128, 128, 8, [hb("bwk%d" % (hh_ % 2))])
                wload(ha("wv", (8, 256), BF16), w_in, C_VG + hh_ * 256, 256, 8, [hb("bwv")])

            if h == 0:
                load_head(0)
            spb = ha("a4", (2176,), F32)
            B_sp = hb("b2")
            Bc = ha("a5", (2176,), F32)
            B_Bc = hb("b3")
            e1 = [ha("a6_%d" % _i, (512,), F32) for _i in range(2)]
            B_e1 = [hb("b4_%d" % _i) for _i in range(2)]
            qeT = ha("a7", (2176,), BF16)
            keT = ha("a8", (2176,), BF16)
            B_qe = hb("b5")
            B_ke = hb("b6")
            vh = ha("a9", (17, 256), BF16)
            B_vh = [hb("b7_%d" % _i) for _i in range(17)]
            offs = ha("a10", (64,), F32)
            B_off = hb("b8")
            cnt = 0
            for (t0, n) in groups(ntl):
                N = n * 128
                c0 = t0 * 128
                pl, bp = bank()
                MM(pl[:, 0:N], gup[0:16, h * 128:(h + 1) * 128], lrT[0:16, c0:c0 + N], True, True, [B_const, B_lr], [bp])
                i = cnt % 2
                cnt += 1
                ACT(e1[i][:, 0:N], pl[:, 0:N], AF.Exp, [bp, B_const], [B_e1[i]], bias=nabias[:, h:h + 1], scale=-1.0)
                ACT(spb[:, c0:c0 + N], e1[i][:, 0:N], AF.Ln, [B_e1[i], B_const], [B_sp], bias=one1[:], scale=1.0)
            for t in range(ntl):
                pv, bp = bank()
                for kc in range(8):
                    MM(pv[:, 0:256], xnT[:, kc, t * 128:(t + 1) * 128], wvg[:, kc, :], kc == 0, kc == 7, [B_xnT[t], B_wv], [bp])
                CP("act", vh[:, t, :], pv[:, 0:256], [bp], [B_vh[t]])
            for pc in range(4):
                S.dve(lambda e, o=Bc[:, 512 * pc:512 * pc + 512], a=ones512[:], b=spb[:, 512 * pc:512 * pc + 512],
                      ini=(0.0 if pc == 0 else Bc[:, 512 * pc - 1:512 * pc]):
                      e.tensor_tensor_scan(out=o, data0=a, data1=b, initial=ini, op0=ALU.mult, op1=ALU.add), [B_sp, B_const, B_Bc], [B_Bc])
            MSET("dve", offs[:, 0:32], 0.0, [B_off])
            CP("dve", offs[:, 1:16], Bc[:, 127:1920:128], [B_Bc], [B_off])
            TT("dve", Bc[:, 0:2048].rearrange("p (c s) -> p c s", c=16), Bc[:, 0:2048].rearrange("p (c s) -> p c s", c=16),
               offs[:, 0:16].unsqueeze(2).broadcast_to([128, 16, 128]), ALU.subtract, [B_Bc, B_off], [B_Bc])
            ACT(offs[:, 32:48], Bc[:, 127:2048:128], AF.Exp, [B_Bc], [B_off], scale=-1.0 / 16)
            if samp:
                S.dve(lambda e, o=Bc[:, 2048:2176], a=ones512[:, 0:128], b=spb[:, 2048:2176]:
                      e.tensor_tensor_scan(out=o, data0=a, data1=b, initial=0.0, op0=ALU.mult, op1=ALU.add), [B_sp, B_const], [B_Bc])
                CP("dve", offs[:, 17:32], Bc[:, 2048 + 7:2048 + 120:8], [B_Bc], [B_off])
                TT("dve", Bc[:, 2048:2176].rearrange("p (c s) -> p c s", c=16), Bc[:, 2048:2176].rearrange("p (c s) -> p c s", c=16),
                   offs[:, 16:32].unsqueeze(2).broadcast_to([128, 16, 8]), ALU.subtract, [B_Bc, B_off], [B_Bc])
                ACT(offs[:, 48:64], Bc[:, 2048 + 7:2176:8], AF.Exp, [B_Bc], [B_off], scale=-1.0 / 16)
            for (t0, n) in groups(ntl):
                N = n * 128
                c0 = t0 * 128
                rd = [B_xnT[t0 + j] for j in range(n)]
                pq, bp = bank()
                for kc in range(8):
                    MM(pq[:, 0:N], wqg[:, kc, :], xnT[:, kc, c0:c0 + N], kc == 0, kc == 7, rd + [B_wq], [bp])
                i = cnt % 2
                cnt += 1
                ACT(e1[i][:, 0:N], Bc[:, c0:c0 + N], AF.Exp, [B_Bc, B_const], [B_e1[i]], bias=lnsc[:], scale=-1.0 / 16)
                TT("dve", qeT[:, c0:c0 + N], pq[:, 0:N], e1[i][:, 0:N], ALU.mult, [bp, B_e1[i]], [B_qe])
                pk, bp = bank()
                for kc in range(8):
                    MM(pk[:, 0:N], wkg[:, kc, :], xnT[:, kc, c0:c0 + N], kc == 0, kc == 7, rd + [B_wk], [bp])
                i = cnt % 2
                cnt += 1
                ACT(e1[i][:, 0:N], Bc[:, c0:c0 + N], AF.Exp, [B_Bc], [B_e1[i]], scale=1.0 / 16)
                TT("dve", keT[:, c0:c0 + N], pk[:, 0:N], e1[i][:, 0:N], ALU.mult, [bp, B_e1[i]], [B_ke])
            eoff = ha("a11", (16,), F32)
            B_eo = hb("b9")
            ACT(eoff, offs[:, 0:16], AF.Exp, [B_off], [B_eo], scale=-1.0 / 16)
            TT("dve", qgT[:, h, :].rearrange("p (c s) -> p c s", c=16), qeT[:, 0:2048].rearrange("p (c s) -> p c s", c=16),
               eoff.unsqueeze(2).broadcast_to([128, 16, 128]), ALU.mult, [B_qe, B_eo], [B_qg[h]])
            TT("dve", dgall[:, h:h + 1], eoff[:, 15:16], offs[:, 47:48], ALU.mult, [B_eo, B_off], [B_dg])
            if mstop == 6:
                S.barrier(); arena.top = m0; return
            attm = [ha("a12_%d" % _i, (128,), BF16) for _i in range(3)]
            B_att = [hb("b10_%d" % _i) for _i in range(3)]
            kdT = [ha("a13_%d" % _i, (128,), BF16) for _i in range(2)]
            B_kdT = [hb("b11_%d" % _i) for _i in range(2)]
            kd = [ha("a14_%d" % _i, (128,), BF16) for _i in range(2)]
            B_kd = [hb("b12_%d" % _i) for _i in range(2)]
            if h + 1 < 4:
                load_head(h + 1)
            kdTall = ha("a15", (2048,), BF16)
            B_kdTall = hb("b13")
            kdall = ha("a16", (16, 128), BF16)
            B_kdall = [hb("b14"), hb("b15")]
            TT("pool", kdTall.rearrange("p (c s) -> p c s", c=16), keT[:, 0:2048].rearrange("p (c s) -> p c s", c=16),
               offs[:, 32:48].unsqueeze(2).broadcast_to([128, 16, 128]), ALU.mult, [B_ke, B_off], [B_kdTall])
            for hf in range(2):
                pt, bpt = tbank()
                for cc in range(8):
                    TR(pt[:, cc * 128:(cc + 1) * 128], kdTall[:, (hf * 8 + cc) * 128:(hf * 8 + cc + 1) * 128], [B_kdTall], [bpt])
                CP("act", kdall[:, hf * 8:(hf + 1) * 8, :], pt[:, 0:1024].rearrange("p (c n) -> p c n", c=8), [bpt], [B_kdall[hf]])
            pus = {}

            def gA(c):
                c0 = c * 128
                ai = c % 3
                pa, bpa = bank()
                MM(pa[:, 0:128], keT[:, c0:c0 + 128], qeT[:, c0:c0 + 128], True, True, [B_ke, B_qe], [bpa])
                TT("dve", attm[ai], pa[:, 0:128], causal[:], ALU.mult, [bpa, B_const], [B_att[ai]])
                pu, bpu = bank()
                MM(pu[:, 0:256], kdall[:, c, :], vh[:, c, :], True, True, [B_kdall[c // 8], B_vh[c]], [bpu])
                pus[c] = (pu, bpu)

            def gB(c):
                c0 = c * 128
                ai = c % 3
                for j in range(2):
                    po, bpo = rbank(j)
                    MM(po[:, 0:128], vh[:, c, j * 128:(j + 1) * 128], attm[ai], True, False, [B_vh[c], B_att[ai]], [bpo])
                    MM(po[:, 0:128], Sbf[:, h, j * 128:(j + 1) * 128], qeT[:, c0:c0 + 128], False, True, [B_Sb[h], B_qe], [bpo])
                    CP("act", onT[:, 2 * h + j, c0:c0 + 128], po[:, 0:128], [bpo], [B_on[2 * h + j]])
                pu, bpu = pus.pop(c)
                STT("dve", Sst[:, h, :], Sst[:, h, :], offs[:, 32 + c:33 + c], pu[:, 0:256], ALU.mult, ALU.add, [B_S[h], B_off, bpu], [B_S[h]])
                CP("dve", Sbf[:, h, :], Sst[:, h, :], [B_S[h]], [B_Sb[h]])

            pipeline([(lambda c=c: gA(c), lambda c=c: gB(c)) for c in range(16)], 1)
            if samp:
                c0 = 2048
                s0 = [ha("a17_%d" % _i, (256,), F32) for _i in range(4)]
                B_s0 = [hb("b16_%d" % _i) for _i in range(4)]
                s0b = [ha("a18_%d" % _i, (256,), BF16) for _i in range(4)]
                B_s0b = [hb("b17_%d" % _i) for _i in range(4)]
                vblk = ha("a19", (16, 256), BF16)
                B_vblk = hb("b18")
                pa, bpa = bank()
                MM(pa[:, 0:128], keT[:, c0:c0 + 128], qeT[:, c0:c0 + 128], True, True, [B_ke, B_qe], [bpa])
                TT("dve", attm[0], pa[:, 0:128], bd8[:], ALU.mult, [bpa, B_const], [B_att[0]])
                po, bpo = [None, None], [None, None]
                for j in range(2):
                    po[j], bpo[j] = rbank(j)
                    MM(po[j][:, 0:128], vh[:, 16, j * 128:(j + 1) * 128], attm[0], True, False, [B_vh[16], B_att[0]], [bpo[j]])
                TT("pool", kdT[0].rearrange("p (b s) -> p b s", b=16), keT[:, c0:c0 + 128].rearrange("p (b s) -> p b s", b=16),
                   offs[:, 48:64].unsqueeze(2).broadcast_to([128, 16, 8]), ALU.mult, [B_ke, B_off], [B_kdT[0]])
                pt, bpt = tbank()
                TR(pt[:, 0:128], kdT[0], [B_kdT[0]], [bpt])
                CP("act", kd[0], pt[:, 0:128], [bpt], [B_kd[0]])
                TT("dve", vblk, vh[:, 16, :].unsqueeze(1).broadcast_to([128, 16, 256]), blk[:].unsqueeze(2).broadcast_to([128, 16, 256]),
                   ALU.mult, [B_vh[16], B_const], [B_vblk])
                for b2 in range(8):
                    pu, bpu = bank()
                    MM(pu[:, 0:512], kd[0], vblk[:, 2 * b2:2 * b2 + 2, :], True, True, [B_kd[0], B_vblk], [bpu])
                    for bb in range(2):
                        b = 2 * b2 + bb
                        si = b % 4
                        DMA("pool", s0[si], sgla[b, h], (), [B_s0[si]])
                        CP("act", s0b[si], s0[si], [B_s0[si]], [B_s0b[si]])
                        for j in range(2):
                            MM(po[j][:, 8 * b:8 * b + 8], s0b[si][:, j * 128:(j + 1) * 128], qeT[:, c0 + 8 * b:c0 + 8 * b + 8], False, False,
                               [B_s0b[si], B_qe], [bpo[j]])
                        STT("dve", s0[si], s0[si], offs[:, 48 + b:49 + b], pu[:, bb * 256:(bb + 1) * 256], ALU.mult, ALU.add,
                            [B_s0[si], B_off, bpu, B_s0b[si]], [B_s0[si]])
                        DMA("sp", glas[b, h], s0[si], [B_s0[si]], ())
                out_norm(h, po, bpo, c0, 0)
        arena.top = m2
        B_sin = Buf()
        B_sout = Buf()
        B_SA = Buf()
        DMA("sp", sin_d.rearrange("p (h n) -> p h n", h=4), Sst[:, :, :], B_S, [B_sin])
        S.coll(lambda e: e.collective_compute("AllGather", ALU.bypass, replica_groups=[[0, 1], [2, 3], [4, 5], [6, 7]],
                                              ins=[sin_d.opt()], outs=[sout_d.opt()]), [B_sin], [B_sout])
        DMA("sp", SAf[:, :, :], sout_d[0:128, :].rearrange("p (h n) -> p h n", h=4), [B_sout], [B_SA])
        TS("dve", SAf[:, :, :], SAf[:, :, :], flag[:, 0:1], ALU.mult, [B_SA, B_const], [B_SA])
        CP("act", SAb[:, :, :], SAf[:, :, :], [B_SA], [B_SA])
        fsq = [[arena.alloc((512,), F32) for _ in range(2)] for _ in range(3)]
        B_fsq = [[Buf() for _ in range(2)] for _ in range(3)]
        frs = [arena.alloc((512,), F32) for _ in range(3)]
        B_frs = [Buf() for _ in range(3)]
        fpo = {}
        fpm = {}

        def f0(i):
            h, q4 = i // 4, i % 4
            c0 = q4 * 512
            si = i % 3
            lst = []
            for j in range(2):
                po, bpo = bank8()
                MM(po[:, 0:512], ident[:], onT[:, 2 * h + j, c0:c0 + 512], True, False, [B_const, B_on[2 * h + j]], [bpo])
                MM(po[:, 0:512], SAb[:, h, j * 128:(j + 1) * 128], qgT[:, h, c0:c0 + 512], False, True, [B_SA, B_qg[h]], [bpo])
                ACT(fsq[si][j], po[:, 0:512], AF.Square, [bpo], [B_fsq[si][j]])
                lst.append((po, bpo))
            fpo[i] = lst

        def f1(i):
            si = i % 3
            pm, bpm = bank8()
            for j in range(2):
                MM(pm[:, 0:512], ones256[:], fsq[si][j], j == 0, j == 1, [B_const, B_fsq[si][j]], [bpm])
            ACT(frs[si], pm[:, 0:512], AF.Sqrt, [bpm, B_const], [B_frs[si]], bias=epsb[:], scale=1.0)

        def f2(i):
            h, q4 = i // 4, i % 4
            c0 = q4 * 512
            si = i % 3
            lst = fpo.pop(i)
            RCP(frs[si], frs[si], [B_frs[si]], [B_frs[si]])
            for j in range(2):
                po, bpo = lst[j]
                STT("dve", onT[:, 2 * h + j, c0:c0 + 512], po[:, 0:512], gout[:, j:j + 1], frs[si], ALU.mult, ALU.mult,
                    [bpo, B_const, B_frs[si]], [B_on[2 * h + j]])

        mpipe(16, [f0, f1, f2], [0, 1, 2])
        for h in range(4):
            STT("dve", Sst[:, h, :], SAf[:, h, :], dgall[:, h:h + 1], Sst[:, h, :], ALU.mult, ALU.add, [B_SA, B_dg, B_S[h]], [B_S[h]])
            DMA("sp", glap[h], Sst[:, h, :], [B_S[h]], ())
        if dbg:
            DMA("pool", dbgO[:, :, 0:NT], onT[:, :, 0:NT], B_on, ())
        if mstop == 8:
            S.barrier(); arena.top = m0; return

        S.barrier()
        arena.top = mO
        allg = groups(ntl)
        wr = [arena.alloc((8, 128), BF16) for _ in range(3)]
        B_wr = [Buf() for _ in range(3)]
        sr = [arena.alloc((512,), BF16) for _ in range(3)]
        B_sr = [Buf() for _ in range(3)]
        cnt = 0
        for bk in range(8):
            wi_ = bk % 3
            wload(wr[wi_], w_in, C_RG + bk * 128, 128, 8, [B_wr[wi_]])
            for (t0, n) in allg:
                N = n * 128
                c0 = t0 * 128
                pr, bp = bank8()
                for kc in range(8):
                    MM(pr[:, 0:N], wr[wi_][:, kc, :], xnT[:, kc, c0:c0 + N], kc == 0, kc == 7, [B_xnT[t0 + j] for j in range(n)] + [B_wr[wi_]], [bp])
                i = cnt % 3
                cnt += 1
                ACT(sr[i][:, 0:N], pr[:, 0:N], AF.Silu, [bp], [B_sr[i]])
                TT("dve", onT[:, bk, c0:c0 + N], onT[:, bk, c0:c0 + N], sr[i][:, 0:N], ALU.mult, [B_sr[i], B_on[bk]], [B_on[bk]])
        mT = arena.alloc((8, 2176), BF16)
        B_m = [Buf() for _ in range(8)]
        NW = 3
        wga = [arena.alloc((8, 128), BF16) for _ in range(NW)]
        wgb = [arena.alloc((8, 128), BF16) for _ in range(NW)]
        wao = [arena.alloc((2, 128), BF16) for _ in range(NW)]
        wbo = [arena.alloc((8, 128), BF16) for _ in range(NW)]
        B_wga = [Buf() for _ in range(NW)]
        B_wgb = [Buf() for _ in range(NW)]
        B_wao = [Buf() for _ in range(NW)]
        B_wbo = [Buf() for _ in range(NW)]
        sa = [arena.alloc((512,), F32) for _ in range(4)]
        B_sa = [Buf() for _ in range(4)]
        t1 = [arena.alloc((512,), F32) for _ in range(3)]
        B_t1 = [Buf() for _ in range(3)]
        wo = arena.alloc((8, 1024), BF16)
        B_wo = Buf()
        xio = [arena.alloc((512,), F32) for _ in range(4)]
        B_xio = [Buf() for _ in range(4)]
        cnt = 0
        def load_fc(fc):
            w_ = fc % NW
            wload(wga[w_], w_in, C_GA + fc * 128, 128, 8, [B_wga[w_]])
            wload(wgb[w_], w_in, C_GB + fc * 128, 128, 8, [B_wgb[w_]])
            wload(wao[w_], w_a_out, fc * 128, 128, 2, [B_wao[w_]])
            wload(wbo[w_], w_b_out, fc * 128, 128, 8, [B_wbo[w_]])

        load_fc(0)
        load_fc(1)
        for fc in range(8):
            wi_ = fc % NW
            if fc + 2 < 8:
                load_fc(fc + 2)
            if fc == 2:
                wload(wo, w_out, 0, 1024, 8, [B_wo])
            for (t0, n) in allg:
                N = n * 128
                c0 = t0 * 128
                rdx = [B_xnT[t0 + j] for j in range(n)]
                pA, bA = bank8()
                for kc in range(8):
                    MM(pA[:, 0:N], wga[wi_][:, kc, :], xnT[:, kc, c0:c0 + N], kc == 0, kc == 7, rdx + [B_wga[wi_]], [bA])
                pC, bC = bank8()
                for hp in range(2):
                    MM(pC[:, 0:N], wao[wi_][:, hp, :], AT[:, hp, c0:c0 + N], hp == 0, hp == 1, [B_AT, B_wao[wi_]], [bC])
                ia = cnt % 4
                it = (cnt // 2) % 3
                cnt += 1
                ACT(sa[ia][:, 0:N], pA[:, 0:N], AF.Sigmoid, [bA], [B_sa[ia]])
                TT("dve", t1[it][:, 0:N], sa[ia][:, 0:N], pC[:, 0:N], ALU.mult, [B_sa[ia], bC], [B_t1[it]])
                pB, bB = bank8()
                for kc in range(8):
                    MM(pB[:, 0:N], wgb[wi_][:, kc, :], xnT[:, kc, c0:c0 + N], kc == 0, kc == 7, rdx + [B_wgb[wi_]], [bB])
                pD, bD = bank8()
                for bk in range(8):
                    MM(pD[:, 0:N], wbo[wi_][:, bk, :], onT[:, bk, c0:c0 + N], bk == 0, bk == 7, [B_on[bk], B_wbo[wi_]], [bD])
                ib = cnt % 4
                cnt += 1
                ACT(sa[ib][:, 0:N], pB[:, 0:N], AF.Sigmoid, [bB], [B_sa[ib]])
                TT("dve", sa[ib][:, 0:N], sa[ib][:, 0:N], pD[:, 0:N], ALU.mult, [B_sa[ib], bD], [B_sa[ib]])
                TT("pool", mT[:, fc, c0:c0 + N], t1[it][:, 0:N], sa[ib][:, 0:N], ALU.add, [B_t1[it], B_sa[ib]], [B_m[fc]])
        cnt = 0
        for t in range(ntl):
            for ch in range(2):
                po, bpo = bank8()
                for fc in range(8):
                    MM(po[:, :], mT[:, fc, t * 128:(t + 1) * 128], wo[:, fc, ch * 512:(ch + 1) * 512], fc == 0, fc == 7, [B_m[fc], B_wo], [bpo])
                xi = cnt % 4
                cnt += 1
                DMA("sp", xio[xi], src(ps, t)[:, ch * 512:(ch + 1) * 512], (), [B_xio[xi]])
                TT("dve", xio[xi], xio[xi], po[:, :], ALU.add, [bpo, B_xio[xi]], [B_xio[xi]])
                DMA("act", dst(ps, t)[:, ch * 512:(ch + 1) * 512], xio[xi], [B_xio[xi]], ())
        S.barrier()
        arena.top = m0

    def src_x(ps, t):
        return xs[0:128, :] if t == 16 else xp[128 * t:128 * (t + 1), :]

    def src_x1(ps, t):
        r = row0(ps, t)
        return x1d[r:r + 128, :]

    def src_x2(ps, t):
        r = row0(ps, t)
        return x2d[r:r + 128, :]

    def dst_y(ps, t):
        return ys[0:128, :] if t == 16 else yp[128 * t:128 * (t + 1), :]

    steps = []
    for ps in range(1):
        ntl = 17
        steps.append(lambda ps=ps, ntl=ntl: phase_norm(ps, ntl, src_x, 0))
        steps.append(lambda ps=ps, ntl=ntl: phase_ffn(ps, ntl, 0, src_x, src_x1))
        steps.append(lambda ps=ps, ntl=ntl: phase_norm(ps, ntl, src_x1, 1))
        steps.append(lambda ps=ps, ntl=ntl: phase_mixer(ps, ntl, src_x1, src_x2))
        steps.append(lambda ps=ps, ntl=ntl: phase_norm(ps, ntl, src_x2, 2))
        steps.append(lambda ps=ps, ntl=ntl: phase_ffn(ps, ntl, 1, src_x2, dst_y))
    for st_ in steps[:nsteps]:
        st_()
        S.barrier()
    S.emit(nc)
    return nc


def _consts():
    k = np.arange(128)[:, None]
    j = np.arange(256)[None, :]
    tri2 = np.where(j < 128, k <= j, k >= (j - 128)).astype(np.float32)
    q = np.arange(128)[None, :]
    causal = (k <= q).astype(np.float32)
    same = (k // 8) == (q // 8)
    bd8 = (same & (k <= q)).astype(np.float32)
    mnew = np.zeros((128, 3, 128), np.float32)
    for g, d in enumerate(DIL):
        mnew[:, g, :] = (same & (k <= q) & (((q - k) % d) == 0)).astype(np.float32)
    mc = np.zeros((128, 13, 32), np.float32)
    t = np.arange(8)[None, :]
    kk = np.arange(128)[:, None]
    m0 = (kk >= t).astype(np.float32)
    mc[:, 0, :] = np.tile(m0, (1, 4))
    for r in range(4):
        mm = (((t % 4) == r) & ((4 * kk + r) >= t)).astype(np.float32)
        mc[:, 1 + r, :] = np.tile(mm, (1, 4))
    for r in range(8):
        mm = ((t == r) & (kk >= 0)).astype(np.float32)
        mc[:, 5 + r, :] = np.tile(mm, (1, 4))
    blk = ((np.arange(128)[:, None] // 8) == np.arange(16)[None, :]).astype(np.float32)
    onespad = np.zeros((128, 2, 128), np.float32)
    onespad[:, 0, 0:64] = 1.0
    onespad[:, 1, 64:128] = 1.0
    return dict(c_ident=np.eye(128, dtype=np.float32), c_tri2=tri2, c_causal=causal, c_bd8=bd8, c_mnew=mnew,
                c_mc=mc, c_blk=blk, c_onespad=onespad)


_NC = None


def kernel(x_prompt, x_sample, cache_swa128_kv, cache_swa512_kv, cache_swa2048_kv, state_gla,
           norm_ffn1, ffn1_gate, ffn1_up, ffn1_down, norm_mix, w_in, a_q_norm, a_k_norm,
           g_alpha_up, g_alpha_bias, g_out_norm, w_a_out, w_b_out, w_out,
           norm_ffn2, ffn2_gate, ffn2_up, ffn2_down):
    global _NC
    f = lambda a: np.ascontiguousarray(np.asarray(a, dtype=np.float32))
    if _NC is None:
        _NC = build_nc()
    nc = _NC
    shared = dict(_consts())
    shared.update(f1g=f(ffn1_gate), f1u=f(ffn1_up), f1d=f(ffn1_down), f2g=f(ffn2_gate), f2u=f(ffn2_up), f2d=f(ffn2_down),
                  w_in=f(w_in), w_a_out=f(w_a_out), w_b_out=f(w_b_out), w_out=f(w_out), g_up=f(g_alpha_up))
    gn = np.stack([f(norm_ffn1), f(norm_mix), f(norm_ffn2)], 0)
    shared["c_gnorm"] = np.ascontiguousarray(np.broadcast_to(gn[None], (128, 3, D)))
    aq, ak = f(a_q_norm), f(a_k_norm)
    gqk = np.concatenate([np.tile(aq[:, None, :], (1, 4, 1)).reshape(3, 256), np.tile(ak[:, None, :], (1, 4, 1)).reshape(3, 256)], axis=1)
    shared["c_gqk"] = np.ascontiguousarray(np.broadcast_to(gqk[None], (128, 3, 512)))
    shared["c_gout"] = np.ascontiguousarray(f(g_out_norm).reshape(2, 128).T)
    shared["c_abias"] = np.ascontiguousarray(f(g_alpha_bias).reshape(4, 128).T)
    xp_, xs_ = f(x_prompt), f(x_sample)
    c1, c2, c3, sg = f(cache_swa128_kv), f(cache_swa512_kv), f(cache_swa2048_kv), f(state_gla)
    in_maps = []
    for c in range(8):
        m = dict(shared)
        m["xp"] = np.ascontiguousarray(xp_[c // 2, (c % 2) * 2048:(c % 2 + 1) * 2048])
        m["c_flag"] = np.full((128, 1), float(c % 2), np.float32)
        m["xs"] = xs_[16 * c:16 * c + 16].reshape(128, D)
        m["c128"] = c1[16 * c:16 * c + 16].reshape(16, 128, 2, 256)
        m["c512"] = c2[16 * c:16 * c + 16].reshape(16, 512, 2, 256)
        m["c2048"] = c3[16 * c:16 * c + 16].reshape(16, 2048, 2, 256)
        m["sgla"] = sg[16 * c:16 * c + 16]
        in_maps.append(m)
    res = run_bass_kernel_spmd(nc, in_maps, core_ids=list(range(8)))
    R = res.results
    y_prompt = np.stack([np.concatenate([R[2 * b]["yp"], R[2 * b + 1]["yp"]], 0) for b in range(4)], 0)
    y_sample = np.concatenate([R[c]["ys"].reshape(16, 8, D) for c in range(8)], 0)
    outs = [y_prompt, y_sample]
    for nm, w in (("kv128p", 128), ("kv512p", 512), ("kv2048p", 2048)):
        outs.append(np.stack([R[2 * b + 1][nm].reshape(w, 2, 4, 64) for b in range(4)], 0))
    outs.append(np.stack([R[2 * b + 1]["glap"] for b in range(4)], 0))
    for nm in ("kv128s", "kv512s", "kv2048s"):
        outs.append(np.concatenate([R[c][nm].reshape(16, 8, 2, 4, 64) for c in range(8)], 0))
    outs.append(np.concatenate([R[c]["glas"] for c in range(8)], 0))
    return tuple(np.ascontiguousarray(o, dtype=np.float32) for o in outs)
```
